# Optimizing a Trainium2 kernel written in Bass

```python
import math
import jax
import jax.numpy as jnp
from jax import lax
import numpy as np

D_MODEL = 2048
BATCH = 4
SEQ = 4096
DEPTH = 2

GRID_W = 64
CTX_LEN = 256
HEAD_DIM = 128
ROPE_BASE = 10000.0
EPS = 1e-6
Q_BLOCK = 128

DIFF_HEADS = 8
DIFF_SUB = HEAD_DIM // 2
SWA_HEADS = 8
SWA_KV_HEADS = 2
SWA_GROUP = SWA_HEADS // SWA_KV_HEADS
WINDOW = 128

DIFF_Q = DIFF_HEADS * 2 * DIFF_SUB
DIFF_V = DIFF_HEADS * HEAD_DIM
SWA_Q = SWA_HEADS * HEAD_DIM
SWA_KV = SWA_KV_HEADS * HEAD_DIM
ATTN_IN = 2 * DIFF_Q + DIFF_V + SWA_Q + 2 * SWA_KV
ATTN_MIX = DIFF_V + SWA_Q

HYENA_ORDER = 2
HYENA_WIDTH = D_MODEL
SHORT_CONV = 3
FILTER_EMB = 33
FILTER_BANDS = (FILTER_EMB - 1) // 2
FILTER_HIDDEN = 64
DECAY_FAST = 0.3
DECAY_SLOW = 1.5
DECAY_TARGET = 1e-2

MOE_GROUPS = 4
MOE_PER_GROUP = 8
MOE_EXPERTS = MOE_GROUPS * MOE_PER_GROUP
MOE_TOPK = 2
MOE_FF = 1024
MOE_BLOCK = 256

kernel_name = 'hybrid_diffattn_swa_hyena_hmoe'


def _rms_norm(x, g):
    xf = x.astype(jnp.float32)
    y = xf * lax.rsqrt(jnp.mean(xf * xf, axis=-1, keepdims=True) + EPS)
    return (y * g.astype(jnp.float32)).astype(x.dtype)


def _heads(t, n, d):
    b, l, _ = t.shape
    return t.reshape(b, l, n, d).transpose(0, 2, 1, 3)


def _merge(t):
    b, n, l, d = t.shape
    return t.transpose(0, 2, 1, 3).reshape(b, l, n * d)


def _axial_rope(n_tokens, dim):
    n_rows = n_tokens // GRID_W
    row = jnp.repeat(jnp.arange(n_rows, dtype=jnp.float32), GRID_W)
    col = jnp.tile(jnp.arange(GRID_W, dtype=jnp.float32), n_rows)
    n_freq = dim // 4
    inv = ROPE_BASE ** (-jnp.arange(n_freq, dtype=jnp.float32) / n_freq)
    ang = jnp.concatenate([row[:, None] * inv, col[:, None] * inv], axis=-1)
    return jnp.cos(ang), jnp.sin(ang)


def _apply_rope(x, cos, sin):
    b, h, l, d = x.shape
    xp = x.reshape(b, h, l, d // 2, 2).astype(jnp.float32)
    x0, x1 = xp[..., 0], xp[..., 1]
    out = jnp.stack([x0 * cos - x1 * sin, x0 * sin + x1 * cos], axis=-1)
    return out.reshape(b, h, l, d).astype(x.dtype)


def _diff_attend(q, k, v, lam):
    s = jnp.einsum('bhqd,bhkd->bhqk', q, k).astype(jnp.float32)
    p = jax.nn.softmax(s, axis=-1)
    b, h2, nq, nk = p.shape
    p = p.reshape(b, h2 // 2, 2, nq, nk)
    a = p[:, :, 0] - lam * p[:, :, 1]
    return jnp.einsum('bhqk,bhkd->bhqd', a.astype(v.dtype), v)


def _sink_softmax(s, sink):
    m = jnp.maximum(jnp.max(s, axis=-1, keepdims=True), sink)
    e = jnp.exp(s - m)
    return e / (jnp.sum(e, axis=-1, keepdims=True) + jnp.exp(sink - m))


def _window_attend(q, k, v, kc, vc, sink):
    b, _, n, d = q.shape
    nb = n // Q_BLOCK
    span = Q_BLOCK + 2 * WINDOW
    qb = q.reshape(b, SWA_KV_HEADS, SWA_GROUP, nb, Q_BLOCK, d)
    idx = jnp.arange(nb)[:, None] * Q_BLOCK + jnp.arange(span)[None, :]
    pad = ((0, 0), (0, 0), (WINDOW, WINDOW), (0, 0))
    kb = jnp.pad(k, pad)[:, :, idx]
    vb = jnp.pad(v, pad)[:, :, idx]
    key_pos = idx - WINDOW
    rel = jnp.arange(span)[None, :] - jnp.arange(Q_BLOCK)[:, None]
    band = (rel >= 0) & (rel <= 2 * WINDOW)
    inside = (key_pos >= 0) & (key_pos < n)
    mask = band[None] & inside[:, None, :]
    s_loc = jnp.einsum('bhgnqd,bhnkd->bhgnqk', qb, kb).astype(jnp.float32)
    s_loc = jnp.where(mask, s_loc, -jnp.inf)
    s_ctx = jnp.einsum('bhgnqd,bhkd->bhgnqk', qb, kc).astype(jnp.float32)
    n_ctx = kc.shape[2]
    sk = sink.astype(jnp.float32).reshape(1, SWA_KV_HEADS, SWA_GROUP, 1, 1, 1)
    p = _sink_softmax(jnp.concatenate([s_ctx, s_loc], axis=-1), sk).astype(v.dtype)
    o = (jnp.einsum('bhgnqk,bhkd->bhgnqd', p[..., :n_ctx], vc)
         + jnp.einsum('bhgnqk,bhnkd->bhgnqd', p[..., n_ctx:], vb))
    return o.reshape(b, SWA_HEADS, n, d)


def _attention_mixer(h, hc, layer, ctx_live, w_in, w_out, dq_g, dk_g, lq1, lk1, lq2, lk2, dsub_g, sq_g, sk_g, sink):
    f32 = jnp.float32
    b, n, _ = h.shape
    cuts = [DIFF_Q, 2 * DIFF_Q, 2 * DIFF_Q + DIFF_V, 2 * DIFF_Q + DIFF_V + SWA_Q, 2 * DIFF_Q + DIFF_V + SWA_Q + SWA_KV]
    dq, dk, dv, sq, sk, sv = jnp.split(h @ w_in, cuts, axis=-1)
    cq, ck, cv, csq, csk, csv = jnp.split(hc @ w_in, cuts, axis=-1)
    lam_init = 0.8 - 0.6 * math.exp(-0.3 * layer)
    lam = (jnp.exp(jnp.sum((lq1 * lk1).astype(f32))) - jnp.exp(jnp.sum((lq2 * lk2).astype(f32))) + lam_init)
    sd = DIFF_SUB ** -0.5
    ss = HEAD_DIM ** -0.5
    cos_d, sin_d = _axial_rope(n, DIFF_SUB)
    cos_s, sin_s = _axial_rope(n, HEAD_DIM)

    q = _apply_rope(_rms_norm(_heads(dq, 2 * DIFF_HEADS, DIFF_SUB), dq_g), cos_d, sin_d) * sd
    k = _apply_rope(_rms_norm(_heads(dk, 2 * DIFF_HEADS, DIFF_SUB), dk_g), cos_d, sin_d)
    kc = _rms_norm(_heads(ck, 2 * DIFF_HEADS, DIFF_SUB), dk_g)
    vc = _heads(cv, DIFF_HEADS, HEAD_DIM)
    k_all = jnp.concatenate([kc, k], axis=2)
    v_all = jnp.concatenate([vc, _heads(dv, DIFF_HEADS, HEAD_DIM)], axis=2)
    nb = n // Q_BLOCK
    q_blocks = q.reshape(b, 2 * DIFF_HEADS, nb, Q_BLOCK, DIFF_SUB).transpose(2, 0, 1, 3, 4)
    o = lax.map(lambda qb: _diff_attend(qb, k_all, v_all, lam), q_blocks)
    o = o.transpose(1, 2, 0, 3, 4).reshape(b, DIFF_HEADS, n, HEAD_DIM)
    o_diff = _merge(_rms_norm(o, dsub_g) * (1.0 - lam_init))

    q2 = _apply_rope(_rms_norm(_heads(sq, SWA_HEADS, HEAD_DIM), sq_g), cos_s, sin_s) * ss
    k2 = _apply_rope(_rms_norm(_heads(sk, SWA_KV_HEADS, HEAD_DIM), sk_g), cos_s, sin_s)
    kc2 = _rms_norm(_heads(csk, SWA_KV_HEADS, HEAD_DIM), sk_g)
    vc2 = _heads(csv, SWA_KV_HEADS, HEAD_DIM)
    o_swa = _merge(_window_attend(q2, k2, _heads(sv, SWA_KV_HEADS, HEAD_DIM), kc2, vc2, sink))
    y = jnp.concatenate([o_diff, o_swa], axis=-1) @ w_out
    if not ctx_live:
        return y, None

    qc = _rms_norm(_heads(cq, 2 * DIFF_HEADS, DIFF_SUB), dq_g) * sd
    oc_diff = _merge(_rms_norm(_diff_attend(qc, kc, vc, lam), dsub_g) * (1.0 - lam_init))
    nc = hc.shape[1]
    qc2 = (_rms_norm(_heads(csq, SWA_HEADS, HEAD_DIM), sq_g) * ss).reshape(b, SWA_KV_HEADS, SWA_GROUP, nc, HEAD_DIM)
    sc = jnp.einsum('bhgqd,bhkd->bhgqk', qc2, kc2).astype(f32)
    pc = _sink_softmax(sc, sink.astype(f32).reshape(1, SWA_KV_HEADS, SWA_GROUP, 1, 1))
    oc_swa = jnp.einsum('bhgqk,bhkd->bhgqd', pc.astype(vc2.dtype), vc2).reshape(b, SWA_HEADS, nc, HEAD_DIM)
    yc = jnp.concatenate([oc_diff, _merge(oc_swa)], axis=-1) @ w_out
    return y, yc


def _short_conv(u, w, b):
    half = SHORT_CONV // 2
    n = u.shape[1]
    up = jnp.pad(u, ((0, 0), (half, half), (0, 0)))
    out = b
    for j in range(SHORT_CONV):
        out = out + up[:, j:j + n] * w[j]
    return out


def _hyena_filters(n, w1, b1, f1, w2, b2, f2, w3):
    t = jnp.linspace(0.0, 1.0, n, dtype=jnp.float32)[:, None]
    w = 2.0 * math.pi * jnp.arange(n, dtype=jnp.float32) / n
    f = jnp.linspace(1e-4, FILTER_BANDS - 1, FILTER_BANDS, dtype=jnp.float32)
    z = jnp.concatenate([t, jnp.cos(w[:, None] * f), -jnp.sin(w[:, None] * f)], axis=-1).astype(w1.dtype)
    a = jnp.sin(f1 * (z @ w1 + b1))
    a = jnp.sin(f2 * (a @ w2 + b2))
    hf = (a @ w3).astype(jnp.float32)
    max_decay = math.log(DECAY_TARGET) / DECAY_FAST
    min_decay = math.log(DECAY_TARGET) / DECAY_SLOW
    deltas = jnp.tile(jnp.linspace(min_decay, max_decay, HYENA_WIDTH, dtype=jnp.float32), 2 * HYENA_ORDER)
    hf = hf * jnp.exp(-t * jnp.abs(deltas))
    return hf.reshape(n, HYENA_ORDER, 2, HYENA_WIDTH)


def _long_conv(u, h_fwd, h_bwd, bias):
    n = u.shape[1]
    kern = jnp.concatenate([h_fwd, jnp.zeros_like(h_fwd[:1]), h_bwd[1:][::-1]], axis=0)
    uf = jnp.fft.rfft(u.astype(jnp.float32), n=2 * n, axis=1)
    kf = jnp.fft.rfft(kern, n=2 * n, axis=0)
    y = jnp.fft.irfft(uf * kf[None], n=2 * n, axis=1)[:, :n]
    return (y + u.astype(jnp.float32) * bias.astype(jnp.float32)).astype(u.dtype)


def _hyena_mixer(u, w_in, b_in, conv_w, conv_b, fw1, fb1, ff1, fw2, fb2, ff2, fw3, fbias, w_out, b_out):
    n = u.shape[1]
    z = _short_conv(u @ w_in + b_in, conv_w, conv_b)
    v, x1, x2 = jnp.split(z, 3, axis=-1)
    filt = _hyena_filters(n, fw1, fb1, ff1, fw2, fb2, ff2, fw3)
    y = x1 * _long_conv(v, filt[:, 0, 0], filt[:, 0, 1], fbias[0])
    y = x2 * _long_conv(y, filt[:, 1, 0], filt[:, 1, 1], fbias[1])
    return y @ w_out + b_out


def _expert_dispatch(t, expert, gates, w_gate, w_up, w_down):
    n, d = t.shape
    n_assign = n * MOE_TOPK
    flat = expert.reshape(-1)
    order = jnp.argsort(flat)
    e_sorted = flat[order]
    counts = jnp.bincount(flat, length=MOE_EXPERTS)
    padded = (counts + MOE_BLOCK - 1) // MOE_BLOCK * MOE_BLOCK
    pad_end = jnp.cumsum(padded)
    pad_start = pad_end - padded
    start = jnp.cumsum(counts) - counts
    slot_sorted = pad_start[e_sorted] + jnp.arange(n_assign) - start[e_sorted]
    slot = jnp.zeros_like(slot_sorted).at[order].set(slot_sorted)
    n_blocks = -(-n_assign // MOE_BLOCK) + MOE_EXPERTS
    src = jnp.full((n_blocks * MOE_BLOCK,), n, jnp.int32).at[slot].set(jnp.arange(n_assign, dtype=jnp.int32) // MOE_TOPK)
    xb = jnp.concatenate([t, jnp.zeros((1, d), t.dtype)], axis=0)[src].reshape(n_blocks, MOE_BLOCK, d)
    block_e = jnp.minimum(jnp.searchsorted(pad_end, jnp.arange(n_blocks) * MOE_BLOCK, side='right'), MOE_EXPERTS - 1)

    def expert_block(args):
        xs, e = args
        return (jax.nn.silu(xs @ w_gate[e]) * (xs @ w_up[e])) @ w_down[e]

    yb = lax.map(expert_block, (xb, block_e)).reshape(-1, d)
    y = yb[slot].reshape(n, MOE_TOPK, d)
    return jnp.einsum('nkd,nk->nd', y, gates.astype(y.dtype))


def _hier_moe(h, wg1, bg1, wg2, bg2, w_gate, w_up, w_down):
    shape = h.shape
    t = h.reshape(-1, shape[-1])
    n = t.shape[0]
    p_grp = jax.nn.softmax((t @ wg1).astype(jnp.float32) + bg1.astype(jnp.float32), axis=-1)
    p_top, grp = lax.top_k(p_grp, 1)
    lg = ((t @ wg2).astype(jnp.float32) + bg2.astype(jnp.float32)).reshape(n, MOE_GROUPS, MOE_PER_GROUP)
    lg = lg[jnp.arange(n), grp[:, 0]]
    top_lg, local = lax.top_k(lg, MOE_TOPK)
    gates = p_top * jax.nn.softmax(top_lg, axis=-1)
    expert = grp * MOE_PER_GROUP + local
    return _expert_dispatch(t, expert, gates, w_gate, w_up, w_down).reshape(shape)


def setup_inputs(seed: int = 0) -> dict:
    key = jax.random.key(seed)
    keys = jax.random.split(key, 48)
    counter = [0]

    def nrm(shape, scale):
        k = keys[counter[0]]
        counter[0] += 1
        return jax.random.normal(k, shape, jnp.float32) * scale

    def gain(shape):
        return 1.0 + nrm(shape, 0.05)

    D = D_MODEL
    W = HYENA_WIDTH
    ne = (DEPTH + 1) // 2
    no = DEPTH // 2
    return {
        'x': nrm((BATCH, SEQ, D), 1.0),
        'c': nrm((BATCH, D), 1.0),
        'ctx': nrm((BATCH, CTX_LEN, D), 1.0),
        'c_ctx': nrm((D,), 1.0),
        'ada_w': nrm((DEPTH, D, 6 * D), 0.5 * D ** -0.5),
        'ada_b': nrm((DEPTH, 6 * D), 0.01),
        'norm1_g': gain((DEPTH, D)),
        'norm2_g': gain((DEPTH, D)),
        'attn_w_in': nrm((ne, D, ATTN_IN), D ** -0.5),
        'attn_w_out': nrm((ne, ATTN_MIX, D), ATTN_MIX ** -0.5),
        'diff_q_g': gain((ne, DIFF_SUB)),
        'diff_k_g': gain((ne, DIFF_SUB)),
        'diff_lq1': nrm((ne, DIFF_SUB), 0.1),
        'diff_lk1': nrm((ne, DIFF_SUB), 0.1),
        'diff_lq2': nrm((ne, DIFF_SUB), 0.1),
        'diff_lk2': nrm((ne, DIFF_SUB), 0.1),
        'diff_sub_g': gain((ne, HEAD_DIM)),
        'swa_q_g': gain((ne, HEAD_DIM)),
        'swa_k_g': gain((ne, HEAD_DIM)),
        'swa_sink': nrm((ne, SWA_HEADS), 0.5),
        'hy_w_in': nrm((no, D, 3 * W), D ** -0.5),
        'hy_b_in': nrm((no, 3 * W), 0.02),
        'hy_conv_w': nrm((no, SHORT_CONV, 3 * W), SHORT_CONV ** -0.5),
        'hy_conv_b': nrm((no, 3 * W), 0.02),
        'flt_w1': nrm((no, FILTER_EMB, FILTER_HIDDEN), FILTER_EMB ** -0.5),
        'flt_b1': nrm((no, FILTER_HIDDEN), 0.1),
        'flt_f1': gain((no, FILTER_HIDDEN)),
        'flt_w2': nrm((no, FILTER_HIDDEN, FILTER_HIDDEN), FILTER_HIDDEN ** -0.5),
        'flt_b2': nrm((no, FILTER_HIDDEN), 0.1),
        'flt_f2': gain((no, FILTER_HIDDEN)),
        'flt_w3': nrm((no, FILTER_HIDDEN, 2 * HYENA_ORDER * W), 0.05 * FILTER_HIDDEN ** -0.5),
        'hy_bias': nrm((no, HYENA_ORDER, W), 0.5),
        'hy_w_out': nrm((no, W, D), W ** -0.5),
        'hy_b_out': nrm((no, D), 0.02),
        'moe_wg1': nrm((DEPTH, D, MOE_GROUPS), D ** -0.5),
        'moe_bg1': nrm((DEPTH, MOE_GROUPS), 0.01),
        'moe_wg2': nrm((DEPTH, D, MOE_EXPERTS), D ** -0.5),
        'moe_bg2': nrm((DEPTH, MOE_EXPERTS), 0.01),
        'moe_w_gate': nrm((DEPTH, MOE_EXPERTS, D, MOE_FF), D ** -0.5),
        'moe_w_up': nrm((DEPTH, MOE_EXPERTS, D, MOE_FF), D ** -0.5),
        'moe_w_down': nrm((DEPTH, MOE_EXPERTS, MOE_FF, D), MOE_FF ** -0.5),
    }


def reference(x, c, ctx, c_ctx, ada_w, ada_b, norm1_g, norm2_g, attn_w_in, attn_w_out, diff_q_g, diff_k_g, diff_lq1, diff_lk1, diff_lq2, diff_lk2, diff_sub_g, swa_q_g, swa_k_g, swa_sink, hy_w_in, hy_b_in, hy_conv_w, hy_conv_b, flt_w1, flt_b1, flt_f1, flt_w2, flt_b2, flt_f2, flt_w3, hy_bias, hy_w_out, hy_b_out, moe_wg1, moe_bg1, moe_wg2, moe_bg2, moe_w_gate, moe_w_up, moe_w_down):
    s_lat = jax.nn.silu(c)
    s_ctx = jax.nn.silu(c_ctx)
    for layer in range(DEPTH):
        even = layer % 2 == 0
        i = layer // 2
        ctx_live = any(j % 2 == 0 for j in range(layer + 1, DEPTH))
        mod = (s_lat @ ada_w[layer] + ada_b[layer])[:, None, :]
        sh1, sc1, g1, sh2, sc2, g2 = jnp.split(mod, 6, axis=-1)
        h = _rms_norm(x, norm1_g[layer]) * (1 + sc1) + sh1
        if even or ctx_live:
            cmod = s_ctx @ ada_w[layer] + ada_b[layer]
            csh1, csc1, cg1, csh2, csc2, cg2 = jnp.split(cmod, 6)
            hc = _rms_norm(ctx, norm1_g[layer]) * (1 + csc1) + csh1
        if even:
            y, yc = _attention_mixer(h, hc, layer, ctx_live, attn_w_in[i], attn_w_out[i], diff_q_g[i], diff_k_g[i],
                                     diff_lq1[i], diff_lk1[i], diff_lq2[i], diff_lk2[i], diff_sub_g[i],
                                     swa_q_g[i], swa_k_g[i], swa_sink[i])
        else:
            hy = (hy_w_in[i], hy_b_in[i], hy_conv_w[i], hy_conv_b[i], flt_w1[i], flt_b1[i], flt_f1[i],
                  flt_w2[i], flt_b2[i], flt_f2[i], flt_w3[i], hy_bias[i], hy_w_out[i], hy_b_out[i])
            y = _hyena_mixer(h, *hy)
            yc = _hyena_mixer(hc, *hy) if ctx_live else None
        moe = (moe_wg1[layer], moe_bg1[layer], moe_wg2[layer], moe_bg2[layer],
               moe_w_gate[layer], moe_w_up[layer], moe_w_down[layer])
        x = x + g1 * y
        x = x + g2 * _hier_moe(_rms_norm(x, norm2_g[layer]) * (1 + sc2) + sh2, *moe)
        if ctx_live:
            ctx = ctx + cg1 * yc
            ctx = ctx + cg2 * _hier_moe(_rms_norm(ctx, norm2_g[layer]) * (1 + csc2) + csh2, *moe)
    return x
```

```python
import contextlib
import numpy as np
import ml_dtypes
import concourse.bass as bass
import concourse.mybir as mybir
from concourse.bass_utils import run_bass_kernel_spmd

F32 = mybir.dt.float32
BF16 = mybir.dt.bfloat16
I32 = mybir.dt.int32
ALU = mybir.AluOpType
AF = mybir.ActivationFunctionType
AX = mybir.AxisListType
NPBF16 = ml_dtypes.bfloat16


class Res:
    __slots__ = ("w", "r")

    def __init__(self):
        self.w = None
        self.r = {}


class DmaSem:
    def __init__(self, sem):
        self.sem = sem
        self.n = 0


class Prog:
    ENGS = ("pe", "act", "dve", "pool", "sp")

    def __init__(self, name="k"):
        self.nc = bass.Bass("TRN2", target_bir_lowering=False)
        self.es = contextlib.ExitStack()
        self.streams = {e: [] for e in self.ENGS}
        self.cnt = {e: 0 for e in self.ENGS}
        self.seen = {e: {} for e in self.ENGS}
        self.esem = {e: self.es.enter_context(self.nc.semaphore("sem_" + e)) for e in self.ENGS}
        self.dsems = []
        self.uid = 0

    def sbuf(self, shape, dtype, name=None):
        self.uid += 1
        return self.es.enter_context(self.nc.sbuf_tensor(name or f"sb{self.uid}", list(shape), dtype))

    def psum(self, shape, dtype, name=None):
        self.uid += 1
        return self.es.enter_context(self.nc.psum_tensor(name or f"ps{self.uid}", list(shape), dtype))

    def dsem(self):
        self.uid += 1
        d = DmaSem(self.es.enter_context(self.nc.semaphore(f"dsem{self.uid}")))
        self.dsems.append(d)
        return d

    def dram_in(self, name, shape, dtype):
        return self.nc.dram_tensor(name, list(shape), dtype, kind="ExternalInput").ap()

    def dram_out(self, name, shape, dtype):
        return self.nc.dram_tensor(name, list(shape), dtype, kind="ExternalOutput").ap()

    def dram_tmp(self, name, shape, dtype):
        return self.nc.dram_tensor(name, list(shape), dtype).ap()

    def op(self, eng, fn, reads=(), writes=(), dsem=None):
        deps = []
        for r in reads:
            if r.w is not None:
                deps.append(r.w)
        for w in writes:
            if w.w is not None:
                deps.append(w.w)
            deps.extend(w.r.items())
        waits = []
        seen = self.seen[eng]
        for src, n in deps:
            if src == eng and eng == "pe":
                continue
            if seen.get(src, 0) >= n:
                continue
            seen[src] = n
            waits.append((src, n))
        if dsem is None:
            self.cnt[eng] += 1
            tok = (eng, self.cnt[eng])
        else:
            dsem.n += 1
            tok = (dsem, dsem.n)
        self.streams[eng].append((waits, fn, tok))
        for r in reads:
            if r.r.get(tok[0], 0) < tok[1]:
                r.r[tok[0]] = tok[1]
        for w in writes:
            w.w = tok
            w.r = {}
        return tok

    def dma(self, eng, out, in_, dsem, reads=(), writes=(), **kw):
        return self.op(eng, lambda e: e.dma_start(out=out, in_=in_, **kw), reads, writes, dsem=dsem)

    def finish(self):
        fin = []
        for d in self.dsems:
            if d.n:
                fin.append((d, d.n))
        self.streams["sp"].append((fin, None, None))
        nc = self.nc
        with nc.Block() as block:
            def emit(e, name):
                for waits, fn, tok in self.streams[name]:
                    for src, n in waits:
                        if isinstance(src, DmaSem):
                            e.wait_ge(src.sem, 16 * n)
                        else:
                            e.wait_ge(self.esem[src], n)
                    if fn is None:
                        continue
                    ins = fn(e)
                    if isinstance(tok[0], DmaSem):
                        ins.then_inc(tok[0].sem, 16)
                    else:
                        ins.then_inc(self.esem[name], 1)

            @block.tensor
            def _(e):
                emit(e, "pe")

            @block.scalar
            def _(e):
                emit(e, "act")

            @block.vector
            def _(e):
                emit(e, "dve")

            @block.gpsimd
            def _(e):
                emit(e, "pool")

            @block.sync
            def _(e):
                emit(e, "sp")
        self.es.close()
        return nc


def run(prog_nc, in_maps):
    res = run_bass_kernel_spmd(prog_nc, in_maps, core_ids=list(range(8)))
    return res.results


def MM(out, lhsT, rhs, start, stop):
    return lambda e: e.matmul(out, lhsT=lhsT, rhs=rhs, start=start, stop=stop)


def TR(out, in_, ident):
    return lambda e: e.transpose(out, in_, ident)


def ACTV(out, in_, func, **kw):
    return lambda e: e.activation(out=out, in_=in_, func=func, **kw)


def TT(out, in0, in1, op):
    return lambda e: e.tensor_tensor(out=out, in0=in0, in1=in1, op=op)


def TS(out, in0, s1, s2, op0, op1=None):
    if op1 is None:
        return lambda e: e.tensor_scalar(out=out, in0=in0, scalar1=s1, scalar2=None, op0=op0)
    return lambda e: e.tensor_scalar(out=out, in0=in0, scalar1=s1, scalar2=s2, op0=op0, op1=op1)


def STT(out, in0, scalar, in1, op0, op1):
    return lambda e: e.scalar_tensor_tensor(out=out, in0=in0, scalar=scalar, in1=in1, op0=op0, op1=op1)


def TRED(out, in_, op, axis=AX.X):
    return lambda e: e.tensor_reduce(out=out, in_=in_, axis=axis, op=op)


def COPY(out, in_):
    return lambda e: e.tensor_copy(out=out, in_=in_)


def RECIP(out, in_):
    return lambda e: e.reciprocal(out=out, in_=in_)


def MEMSET(ap, v):
    return lambda e: e.memset(ap, v)


def barrier(P):
    toks = [(e, P.cnt[e]) for e in P.ENGS if P.cnt[e] > 0] + [(d, d.n) for d in P.dsems if d.n > 0]
    for e in P.ENGS:
        waits = []
        for src, n in toks:
            if src == e:
                continue
            if P.seen[e].get(src, 0) >= n:
                continue
            P.seen[e][src] = n
            waits.append((src, n))
        P.streams[e].append((waits, None, None))


D = 2048
EPS = 1e-6


def rstd_ops(P, ss, rstd, n, r_ss, r_rstd):
    P.op("dve", TS(rstd, ss, 1.0 / n, EPS, ALU.mult, ALU.add), reads=[r_ss], writes=[r_rstd])
    P.op("act", ACTV(rstd, rstd, AF.Sqrt), reads=[r_rstd], writes=[r_rstd])
    P.op("dve", RECIP(rstd, rstd), reads=[r_rstd], writes=[r_rstd])


def build_mod():
    P = Prog("mod")
    sT = P.dram_in("sT", [128, 16, 5], F32)
    W = P.dram_in("W", [2, 2048, 1536], F32)
    B = P.dram_in("B", [5, 2, 1536], F32)
    M = P.dram_out("M", [5, 2, 1536], F32)
    s_raw = P.sbuf([128, 16, 5], F32); s_act = P.sbuf([128, 16, 5], F32)
    wt = [P.sbuf([128, 16, 512], F32) for _ in range(2)]
    bt = P.sbuf([5, 2, 1536], F32); ot = P.sbuf([5, 2, 1536], F32)
    ps = [P.psum([128, 512], F32) for _ in range(2)]
    r_s = Res(); r_sa = Res(); r_w = [Res(), Res()]; r_b = Res(); r_o = Res(); r_ps = [Res(), Res()]
    d_s = P.dsem(); d_w = [P.dsem(), P.dsem()]; d_b = P.dsem(); d_o = P.dsem()
    P.dma("sp", s_raw[:], sT, d_s, writes=[r_s])
    P.dma("sp", bt[:], B, d_b, writes=[r_b])
    P.op("act", ACTV(s_act[:], s_raw[:], AF.Silu), reads=[r_s], writes=[r_sa])
    i = 0
    for l in range(2):
        for cc in range(3):
            sl = i % 2
            P.dma("sp", wt[sl][:], W[l, :, cc * 512:(cc + 1) * 512].rearrange("(kc p) n -> p kc n", p=128), d_w[sl], writes=[r_w[sl]])
            for kc in range(16):
                P.op("pe", MM(ps[sl][0:5, :], s_act[:, kc, :], wt[sl][:, kc, :], kc == 0, kc == 15), reads=[r_sa, r_w[sl]], writes=[r_ps[sl]])
            P.op("dve", TT(ot[:, l, cc * 512:(cc + 1) * 512], ps[sl][0:5, :], bt[:, l, cc * 512:(cc + 1) * 512], ALU.add), reads=[r_ps[sl], r_b], writes=[r_o])
            i += 1
    P.dma("sp", M, ot[:], d_o, reads=[r_o])
    return P.finish()


def build_comb(nt, with_moe, with_norm, ctx_tiles=0):
    P = Prog("comb")
    n = nt * 128
    xin = P.dram_in("xin", [n, D], F32)
    if with_moe:
        Y = P.dram_in("Y", [n, 2, D], BF16)
        gates = P.dram_in("gates", [n, 2], F32)
        g2r = P.dram_in("g2r", [128, D], F32)
        xout = P.dram_out("xout", [n, D], F32)
    if with_norm:
        ngr = P.dram_in("ngr", [128, D], F32)
        scr = P.dram_in("scr", [128, D], F32)
        shr = P.dram_in("shr", [128, D], F32)
        if ctx_tiles:
            cscr = P.dram_in("cscr", [128, D], F32)
            cshr = P.dram_in("cshr", [128, D], F32)
        hout = P.dram_out("hout", [n, D], BF16)
    xt = [P.sbuf([128, D], F32) for _ in range(2)]; r_x = [Res(), Res()]; d_x = [P.dsem(), P.dsem()]
    d_st = P.dsem()
    if with_moe:
        yt = [P.sbuf([128, 2, D], BF16) for _ in range(2)]; r_y = [Res(), Res()]; d_y = [P.dsem(), P.dsem()]
        gt = [P.sbuf([128, 2], F32) for _ in range(2)]; r_g = [Res(), Res()]
        g2t = P.sbuf([128, D], F32); r_g2 = Res(); d_c = P.dsem()
        mt = P.sbuf([128, D], F32); r_m = Res()
        P.dma("sp", g2t[:], g2r, d_c, writes=[r_g2])
    if with_norm:
        d_c2 = P.dsem()
        ngt = P.sbuf([128, D], F32); tmp = P.sbuf([128, D], F32)
        Gt = P.sbuf([128, D], F32); SHt = P.sbuf([128, D], F32); r_G = Res(); r_SH = Res(); r_ng = Res(); r_tmp = Res()
        P.dma("sp", ngt[:], ngr, d_c2, writes=[r_ng])
        P.dma("sp", tmp[:], scr, d_c2, writes=[r_tmp])
        P.dma("sp", SHt[:], shr, d_c2, writes=[r_SH])
        P.op("dve", STT(Gt[:], tmp[:], 1.0, ngt[:], ALU.add, ALU.mult), reads=[r_tmp, r_ng], writes=[r_G])
        if ctx_tiles:
            cGt = P.sbuf([128, D], F32); cSHt = P.sbuf([128, D], F32); r_cG = Res(); r_cSH = Res()
            P.dma("sp", tmp[:], cscr, d_c2, writes=[r_tmp])
            P.dma("sp", cSHt[:], cshr, d_c2, writes=[r_cSH])
            P.op("dve", STT(cGt[:], tmp[:], 1.0, ngt[:], ALU.add, ALU.mult), reads=[r_tmp, r_ng], writes=[r_cG])
        junk = P.sbuf([128, D], BF16); r_junk = Res()
        ss = P.sbuf([128, 1], F32); rs = P.sbuf([128, 1], F32); r_ss = Res(); r_rs = Res()
        hn = P.sbuf([128, D], F32); r_hn = Res()
        hb = [P.sbuf([128, D], BF16) for _ in range(2)]; r_hb = [Res(), Res()]
    for t in range(nt):
        sl = t % 2
        rows = slice(t * 128, (t + 1) * 128)
        P.dma("sp", xt[sl][:], xin[rows, :], d_x[sl], writes=[r_x[sl]])
        if with_moe:
            P.dma("sp", yt[sl][:], Y[rows], d_y[sl], writes=[r_y[sl]])
            P.dma("sp", gt[sl][:], gates[rows, :], d_y[sl], writes=[r_g[sl]])
            P.op("dve", TS(mt[:], yt[sl][:, 0, :], gt[sl][:, 0:1], None, ALU.mult), reads=[r_y[sl], r_g[sl]], writes=[r_m])
            P.op("dve", STT(mt[:], yt[sl][:, 1, :], gt[sl][:, 1:2], mt[:], ALU.mult, ALU.add), reads=[r_y[sl], r_g[sl], r_m], writes=[r_m])
            P.op("dve", TT(mt[:], mt[:], g2t[:], ALU.mult), reads=[r_m, r_g2], writes=[r_m])
            P.op("dve", TT(xt[sl][:], xt[sl][:], mt[:], ALU.add), reads=[r_m, r_x[sl]], writes=[r_x[sl]])
            P.dma("sp", xout[rows, :], xt[sl][:], d_st, reads=[r_x[sl]])
        if with_norm:
            isctx = t >= nt - ctx_tiles
            P.op("act", ACTV(junk[:], xt[sl][:], AF.Square, accum_out=ss[:]), reads=[r_x[sl]], writes=[r_junk, r_ss])
            rstd_ops(P, ss[:], rs[:], D, r_ss, r_rs)
            P.op("dve", STT(hn[:], xt[sl][:], rs[:, 0:1], (cGt if isctx else Gt)[:], ALU.mult, ALU.mult), reads=[r_x[sl], r_rs, (r_cG if isctx else r_G)], writes=[r_hn])
            P.op("dve", TT(hb[sl][:], hn[:], (cSHt if isctx else SHt)[:], ALU.add), reads=[r_hn, (r_cSH if isctx else r_SH)], writes=[r_hb[sl]])
            P.dma("sp", hout[rows, :], hb[sl][:], d_st, reads=[r_hb[sl]])
    return P.finish()


def load_w_bf16(P, dst, src, nkc, ncols, stage, r_stage, d_stage, r_dst):
    for k0 in range(0, nkc, 4):
        P.dma("sp", stage[:, :, :ncols], src[k0 * 128:(k0 + 4) * 128, :].rearrange("(kc p) n -> p kc n", p=128), d_stage, writes=[r_stage])
        P.op("act", ACTV(dst[:, k0:k0 + 4, :ncols], stage[:, :, :ncols], AF.Copy), reads=[r_stage], writes=[r_dst])


def normrope(P, ps_ap, ncols, hd, gain, cos, sin, outb, r_ps, r_gain, r_tab, r_out, S):
    nh = ncols // hd
    hp = hd // 2
    sq, ss, rs, vn = S["sq"], S["ss"], S["rs"], S["vn"]
    ta, tb = S["ta"], S["tb"]
    P.op("act", ACTV(sq[:, :ncols], ps_ap, AF.Square), reads=[r_ps], writes=[S["r_sq"]])
    P.op("dve", TRED(ss[:, :nh], sq[:, :ncols].rearrange("p (h d) -> p h d", d=hd), ALU.add), reads=[S["r_sq"]], writes=[S["r_ss"]])
    rstd_ops(P, ss[:, :nh], rs[:, :nh], hd, S["r_ss"], S["r_rs"])
    P.op("dve", TT(vn[:, :ncols].rearrange("p (h d) -> p h d", d=hd), ps_ap.rearrange("p (h d) -> p h d", d=hd),
                   rs[:, :nh].unsqueeze(2).to_broadcast([128, nh, hd]), ALU.mult), reads=[r_ps, S["r_rs"]], writes=[S["r_vn"]])
    P.op("dve", TT(vn[:, :ncols], vn[:, :ncols], gain, ALU.mult), reads=[S["r_vn"], r_gain], writes=[S["r_vn"]])
    v4 = vn[:, :ncols].rearrange("p (h i two) -> p h i two", two=2, i=hp)
    o4 = outb.rearrange("p (h i two) -> p h i two", two=2, i=hp)
    ve, vo = v4[:, :, :, 0], v4[:, :, :, 1]
    cb = cos.unsqueeze(1).to_broadcast([128, nh, hp])
    sb = sin.unsqueeze(1).to_broadcast([128, nh, hp])
    n2 = nh * hp
    tav = ta[:, :n2].rearrange("p (h i) -> p h i", i=hp)
    tbv = tb[:, :n2].rearrange("p (h i) -> p h i", i=hp)
    P.op("dve", TT(tav, ve, cb, ALU.mult), reads=[S["r_vn"], r_tab], writes=[S["r_ta"]])
    P.op("dve", TT(tbv, vo, sb, ALU.mult), reads=[S["r_vn"], r_tab], writes=[S["r_tb"]])
    P.op("dve", TT(o4[:, :, :, 0], tav, tbv, ALU.subtract), reads=[S["r_ta"], S["r_tb"]], writes=[r_out])
    P.op("dve", TT(tav, ve, sb, ALU.mult), reads=[S["r_vn"], r_tab], writes=[S["r_ta"]])
    P.op("dve", TT(tbv, vo, cb, ALU.mult), reads=[S["r_vn"], r_tab], writes=[S["r_tb"]])
    P.op("dve", TT(o4[:, :, :, 1], tav, tbv, ALU.add), reads=[S["r_ta"], S["r_tb"]], writes=[r_out])


def build_qkv(nt):
    P = Prog("qkv")
    n = nt * 128
    hT = P.dram_in("hT", [D, n], BF16)
    w_in = P.dram_in("w_in", [D, 4608], F32)
    gq = P.dram_in("gq", [128, 512], F32)
    gk = P.dram_in("gk", [128, 512], F32)
    gsq = P.dram_in("gsq", [128, 512], F32)
    gsk = P.dram_in("gsk", [128, 512], F32)
    cosd = P.dram_in("cosd", [n, 32], F32); sind = P.dram_in("sind", [n, 32], F32)
    coss = P.dram_in("coss", [n, 64], F32); sins = P.dram_in("sins", [n, 64], F32)
    out = P.dram_out("qkv", [n, 4608], BF16)
    d_c = P.dsem(); d_st = P.dsem()
    gt = {}
    r_gain = Res()
    for nm, ap, scale in (("gq", gq, 64 ** -0.5), ("gk", gk, None), ("gsq", gsq, 128 ** -0.5), ("gsk", gsk, None)):
        t = P.sbuf([128, 512], F32)
        P.dma("sp", t[:], ap, d_c, writes=[r_gain])
        if scale is not None:
            P.op("dve", TS(t[:], t[:], scale, None, ALU.mult), reads=[r_gain], writes=[r_gain])
        gt[nm] = t
    cd = P.sbuf([128, nt, 32], F32); sd_ = P.sbuf([128, nt, 32], F32)
    cs = P.sbuf([128, nt, 64], F32); sn = P.sbuf([128, nt, 64], F32)
    r_tab = Res()
    for t_, ap in ((cd, cosd), (sd_, sind), (cs, coss), (sn, sins)):
        P.dma("sp", t_[:], ap.rearrange("(t p) i -> p t i", p=128), d_c, writes=[r_tab])
    S = dict(sq=P.sbuf([128, 512], F32), ss=P.sbuf([128, 8], F32), rs=P.sbuf([128, 8], F32), vn=P.sbuf([128, 512], F32),
             ta=P.sbuf([128, 256], F32), tb=P.sbuf([128, 256], F32),
             r_sq=Res(), r_ss=Res(), r_rs=Res(), r_vn=Res(), r_ta=Res(), r_tb=Res())
    wt = [P.sbuf([128, 16, 512], BF16) for _ in range(2)]; r_w = [Res(), Res()]; d_w = [P.dsem(), P.dsem()]
    ht = [P.sbuf([128, 16, 128], BF16) for _ in range(2)]; r_h = [Res(), Res()]; d_h = [P.dsem(), P.dsem()]
    ps = [P.psum([128, 512], F32) for _ in range(2)]; r_ps = [Res(), Res()]
    ob = [P.sbuf([128, 512], BF16) for _ in range(2)]; r_ob = [Res(), Res()]
    wst = P.sbuf([128, 4, 512], F32); r_wst = Res(); d_wst = P.dsem()
    it = 0
    for cg in range(9):
        ws = cg % 2
        load_w_bf16(P, wt[ws], w_in[:, cg * 512:(cg + 1) * 512], 16, 512, wst, r_wst, d_wst, r_w[ws])
        for t in range(nt):
            sl = it % 2
            it += 1
            P.dma("sp", ht[sl][:], hT[:, t * 128:(t + 1) * 128].rearrange("(kc p) n -> p kc n", p=128), d_h[sl], writes=[r_h[sl]])
            for kc in range(16):
                P.op("pe", MM(ps[sl][:], ht[sl][:, kc, :], wt[ws][:, kc, :], kc == 0, kc == 15), reads=[r_h[sl], r_w[ws]], writes=[r_ps[sl]])
            o = ob[sl]
            if cg in (0, 1):
                normrope(P, ps[sl][:], 512, 64, gt["gq"][:], cd[:, t, :], sd_[:, t, :], o[:], r_ps[sl], r_gain, r_tab, r_ob[sl], S)
            elif cg in (2, 3):
                normrope(P, ps[sl][:], 512, 64, gt["gk"][:], cd[:, t, :], sd_[:, t, :], o[:], r_ps[sl], r_gain, r_tab, r_ob[sl], S)
            elif cg in (4, 5):
                P.op("act", ACTV(o[:], ps[sl][:], AF.Copy), reads=[r_ps[sl]], writes=[r_ob[sl]])
            elif cg in (6, 7):
                normrope(P, ps[sl][:], 512, 128, gt["gsq"][:], cs[:, t, :], sn[:, t, :], o[:], r_ps[sl], r_gain, r_tab, r_ob[sl], S)
            else:
                normrope(P, ps[sl][:, 0:256], 256, 128, gt["gsk"][:, 0:256], cs[:, t, :], sn[:, t, :], o[:, 0:256], r_ps[sl], r_gain, r_tab, r_ob[sl], S)
                P.op("act", ACTV(o[:, 256:512], ps[sl][:, 256:512], AF.Copy), reads=[r_ps[sl]], writes=[r_ob[sl]])
            P.dma("sp", out[t * 128:(t + 1) * 128, cg * 512:(cg + 1) * 512], o[:], d_st, reads=[r_ob[sl]])
    return P.finish()


def rope_tables(pos, dim):
    pos = np.asarray(pos)
    row = (pos // 64).astype(np.float32); col = (pos % 64).astype(np.float32)
    nf = dim // 4
    inv = (np.float32(10000.0) ** (-np.arange(nf, dtype=np.float32) / np.float32(nf))).astype(np.float32)
    ang = np.concatenate([row[:, None] * inv, col[:, None] * inv], axis=-1).astype(np.float32)
    return np.cos(ang).astype(np.float32), np.sin(ang).astype(np.float32)


def build_att():
    P = Prog("att")
    NK = 4352; NC = 34
    QT = P.dram_in("QT", [8, 128, 2048], BF16)
    KT = P.dram_in("KT", [8, 128, NK], BF16)
    VA = P.dram_in("VA", [8, 128, NC, 129], BF16)
    SQT = P.dram_in("SQT", [2, 128, 4, 2048], BF16)
    SKT = P.dram_in("SKT", [2, 128, NK], BF16)
    SVA = P.dram_in("SVA", [2, 128, NC, 129], BF16)
    lam4 = P.dram_in("lam4", [128, 4, 64], F32)
    gsub = P.dram_in("gsub", [128, 128], F32)
    sink = P.dram_in("sink", [128, 8], F32)
    masks = P.dram_in("masks", [4, 128, 128], BF16)
    A = P.dram_out("A", [2048, 2048], BF16)
    d_c = P.dsem(); d_st = P.dsem()
    l4 = P.sbuf([128, 4, 64], F32); r_l4 = Res()
    P.dma("sp", l4[:], lam4, d_c, writes=[r_l4])
    gs = P.sbuf([128, 128], F32); r_gs = Res()
    P.dma("sp", gs[:], gsub, d_c, writes=[r_gs])
    sk = P.sbuf([128, 8], F32); r_sk = Res()
    P.dma("sp", sk[:], sink, d_c, writes=[r_sk])
    mk = P.sbuf([128, 4, 128], BF16); r_mk = Res()
    P.dma("sp", mk[:], masks.rearrange("m k q -> k m q"), d_c, writes=[r_mk])
    lam_init = 0.8 - 0.6 * 1.0
    prod = P.sbuf([128, 2, 64], F32); lsum = P.sbuf([128, 2], F32); nlam = P.sbuf([128, 1], F32); r_lam = Res()
    P.op("dve", TT(prod[:, 0, :], l4[:, 0, :], l4[:, 1, :], ALU.mult), reads=[r_l4], writes=[r_lam])
    P.op("dve", TT(prod[:, 1, :], l4[:, 2, :], l4[:, 3, :], ALU.mult), reads=[r_l4, r_lam], writes=[r_lam])
    P.op("dve", TRED(lsum[:], prod[:], ALU.add), reads=[r_lam], writes=[r_lam])
    P.op("act", ACTV(lsum[:], lsum[:], AF.Exp), reads=[r_lam], writes=[r_lam])
    P.op("dve", TT(nlam[:], lsum[:, 1:2], lsum[:, 0:1], ALU.subtract), reads=[r_lam], writes=[r_lam])
    P.op("dve", TS(nlam[:], nlam[:], -lam_init, None, ALU.add), reads=[r_lam], writes=[r_lam])
    P.op("dve", TS(gs[:], gs[:], 1.0 - lam_init, None, ALU.mult), reads=[r_gs], writes=[r_gs])
    P.op("act", ACTV(sk[:], sk[:], AF.Exp), reads=[r_sk], writes=[r_sk])
    kt = [P.sbuf([128, NK], BF16) for _ in range(2)]; qt = [P.sbuf([128, 4, 2048], BF16) for _ in range(2)]
    va = [P.sbuf([128, NC, 129], BF16) for _ in range(2)]
    r_in = [Res(), Res()]; d_in = [P.dsem(), P.dsem()]
    pT = [P.sbuf([128, 512], BF16) for _ in range(3)]; r_pT = [Res() for _ in range(3)]
    psS = [P.psum([128, 512], F32) for _ in range(2)]; r_S = [Res(), Res()]
    psO = [P.psum([128, 129], F32) for _ in range(4)]; r_O = [Res() for _ in range(4)]
    o1s = P.sbuf([128, 4, 129], F32); r_o1s = [Res() for _ in range(4)]
    rz = P.sbuf([128, 2], F32); o1 = P.sbuf([128, 128], F32); o2 = P.sbuf([128, 128], F32); junk = P.sbuf([128, 128], F32)
    ss = P.sbuf([128, 1], F32); rs = P.sbuf([128, 1], F32)
    r_f = Res(); r_ss = Res(); r_rs = Res()
    ab = [P.sbuf([128, 128], BF16) for _ in range(4)]; r_ab = [Res() for _ in range(4)]
    ipt = 0; iab = 0; iS = 0
    for h in range(8):
        sl = h % 2
        P.dma("sp", kt[sl][:], KT[h], d_in[sl], writes=[r_in[sl]])
        P.dma("sp", qt[sl][:, 0, :], QT[h], d_in[sl], writes=[r_in[sl]])
        P.dma("sp", va[sl][:], VA[h], d_in[sl], writes=[r_in[sl]])
        for qb in range(4):
            for sub in range(2):
                for kc in range(NC):
                    s_ = iS % 2; iS += 1
                    p_ = ipt % 3; ipt += 1
                    P.op("pe", MM(psS[s_][:], kt[sl][sub * 64:(sub + 1) * 64, kc * 128:(kc + 1) * 128],
                                  qt[sl][sub * 64:(sub + 1) * 64, 0, qb * 512:(qb + 1) * 512], True, True), reads=[r_in[sl]], writes=[r_S[s_]])
                    P.op("act", ACTV(pT[p_][:], psS[s_][:], AF.Exp), reads=[r_S[s_]], writes=[r_pT[p_]])
                    for j in range(4):
                        P.op("pe", MM(psO[j][:], pT[p_][:, j * 128:(j + 1) * 128], va[sl][:, kc, :], kc == 0, kc == NC - 1),
                             reads=[r_pT[p_], r_in[sl]], writes=[r_O[j]])
                if sub == 0:
                    for j in range(4):
                        P.op("act", ACTV(o1s[:, j, :], psO[j][:], AF.Copy), reads=[r_O[j]], writes=[r_o1s[j]])
            for j in range(4):
                O1 = o1s[:, j, :]; O2 = psO[j][:]
                rO = [r_o1s[j], r_O[j]]
                P.op("dve", RECIP(rz[:, 0:1], O1[:, 128:129]), reads=[rO[0]], writes=[r_f])
                P.op("dve", RECIP(rz[:, 1:2], O2[:, 128:129]), reads=[rO[1], r_f], writes=[r_f])
                P.op("dve", TT(rz[:, 1:2], rz[:, 1:2], nlam[:], ALU.mult), reads=[r_f, r_lam], writes=[r_f])
                P.op("dve", TS(o2[:], O2[:, 0:128], rz[:, 1:2], None, ALU.mult), reads=[rO[1], r_f], writes=[r_f])
                P.op("dve", STT(o1[:], O1[:, 0:128], rz[:, 0:1], o2[:], ALU.mult, ALU.add), reads=[rO[0], r_f], writes=[r_f])
                P.op("act", ACTV(junk[:], o1[:], AF.Square, accum_out=ss[:]), reads=[r_f], writes=[r_ss])
                rstd_ops(P, ss[:], rs[:], 128, r_ss, r_rs)
                a_ = iab % 4; iab += 1
                P.op("dve", STT(ab[a_][:], o1[:], rs[:, 0:1], gs[:], ALU.mult, ALU.mult), reads=[r_f, r_rs, r_gs], writes=[r_ab[a_]])
                tok0 = qb * 512 + j * 128
                P.dma("sp", A[tok0:tok0 + 128, h * 128:(h + 1) * 128], ab[a_][:], d_st, reads=[r_ab[a_]])
    for g in range(2):
        sl = g % 2
        P.dma("sp", kt[sl][:], SKT[g], d_in[sl], writes=[r_in[sl]])
        P.dma("sp", qt[sl][:], SQT[g], d_in[sl], writes=[r_in[sl]])
        P.dma("sp", va[sl][:], SVA[g], d_in[sl], writes=[r_in[sl]])
        for qi in range(16):
            chunks = [(0, None), (1, None)]
            chunks.append((2 + qi - 1, 0) if qi > 0 else (18, 2))
            chunks.append((2 + qi, None))
            chunks.append((2 + qi + 1, 1) if qi < 15 else (18, 3))
            for ci, (kc, m) in enumerate(chunks):
                s_ = iS % 2; iS += 1
                p_ = ipt % 3; ipt += 1
                P.op("pe", MM(psS[s_][:].rearrange("p (h q) -> p h q", q=128), kt[sl][:, kc * 128:(kc + 1) * 128],
                              qt[sl][:, :, qi * 128:(qi + 1) * 128], True, True), reads=[r_in[sl]], writes=[r_S[s_]])
                P.op("act", ACTV(pT[p_][:], psS[s_][:], AF.Exp), reads=[r_S[s_]], writes=[r_pT[p_]])
                if m is not None:
                    P.op("dve", TT(pT[p_][:].rearrange("p (h q) -> p h q", q=128), pT[p_][:].rearrange("p (h q) -> p h q", q=128),
                                   mk[:, m, :].unsqueeze(1).to_broadcast([128, 4, 128]), ALU.mult), reads=[r_pT[p_], r_mk], writes=[r_pT[p_]])
                for hh in range(4):
                    P.op("pe", MM(psO[hh][:], pT[p_][:, hh * 128:(hh + 1) * 128], va[sl][:, kc, :], ci == 0, ci == 4),
                         reads=[r_pT[p_], r_in[sl]], writes=[r_O[hh]])
            for hh in range(4):
                O = psO[hh][:]
                hd_ = g * 4 + hh
                P.op("dve", TT(rz[:, 0:1], O[:, 128:129], sk[:, hd_:hd_ + 1], ALU.add), reads=[r_O[hh], r_sk], writes=[r_f])
                P.op("dve", RECIP(rz[:, 0:1], rz[:, 0:1]), reads=[r_f], writes=[r_f])
                a_ = iab % 4; iab += 1
                P.op("dve", TS(ab[a_][:], O[:, 0:128], rz[:, 0:1], None, ALU.mult), reads=[r_O[hh], r_f], writes=[r_ab[a_]])
                P.dma("sp", A[qi * 128:(qi + 1) * 128, 1024 + hd_ * 128:1024 + (hd_ + 1) * 128], ab[a_][:], d_st, reads=[r_ab[a_]])
    return P.finish()


def rep128(v):
    v = np.asarray(v)
    return np.ascontiguousarray(np.broadcast_to(v[None], (128,) + v.shape))


def att_inputs(qkv, d):
    tri_prev = (np.arange(128)[:, None] >= np.arange(128)[None, :])
    tri_next = (np.arange(128)[:, None] <= np.arange(128)[None, :])
    zeros = np.zeros((128, 128), bool)
    in_maps = []
    for c in range(8):
        b, hf = c // 2, c % 2
        own = qkv[c, :2048]; oth_c = 2 * b + (1 - hf)
        other = qkv[oth_c, :2048]
        if hf == 1:
            other = np.concatenate([other[1920:], other[:1920]], 0)
        ctxr = np.concatenate([qkv[2 * b, 2048:], qkv[2 * b + 1, 2048:]], 0)
        keys = np.concatenate([ctxr, own, other], 0)
        QT = np.ascontiguousarray(own[:, 0:1024].T.reshape(8, 128, 2048))
        KT = np.ascontiguousarray(keys[:, 1024:2048].T.reshape(8, 128, 4352))
        V = keys[:, 2048:3072].reshape(34, 128, 8, 128)
        VA = np.ones((8, 128, 34, 129), NPBF16); VA[:, :, :, :128] = V.transpose(2, 1, 0, 3)
        SQT = np.ascontiguousarray(own[:, 3072:4096].reshape(2048, 2, 4, 128).transpose(1, 3, 2, 0))
        SKT = np.ascontiguousarray(keys[:, 4096:4352].T.reshape(2, 128, 4352))
        SV = keys[:, 4352:4608].reshape(34, 128, 2, 128)
        SVA = np.ones((2, 128, 34, 129), NPBF16); SVA[:, :, :, :128] = SV.transpose(2, 1, 0, 3)
        masks = np.stack([tri_prev, tri_next, tri_prev if hf == 1 else zeros, tri_next if hf == 0 else zeros]).astype(NPBF16)
        lam4 = np.stack([d["diff_lq1"][0], d["diff_lk1"][0], d["diff_lq2"][0], d["diff_lk2"][0]])
        in_maps.append({"QT": QT, "KT": KT, "VA": VA, "SQT": SQT, "SKT": SKT, "SVA": SVA, "lam4": rep128(lam4),
                        "gsub": rep128(d["diff_sub_g"][0]), "sink": rep128(d["swa_sink"][0]), "masks": masks})
    return in_maps


def build_post(nt):
    P = Prog("post")
    n = nt * 128
    AT = P.dram_in("AT", [D, n], BF16)
    xin = P.dram_in("xin", [n, D], F32)
    w_out = P.dram_in("w_out", [D, D], F32)
    tabs = {k: P.dram_in(k, [128, D], F32) for k in ("g1r", "boutr", "n2gr", "sc2r", "sh2r")}
    x1o = P.dram_out("x1", [n, D], F32)
    h2o = P.dram_out("h2", [n, D], F32)
    h2bo = P.dram_out("h2b", [n, D], BF16)
    d_c = P.dsem(); d_st = P.dsem()
    T = {}; r_T = Res()
    for k, ap in tabs.items():
        T[k] = P.sbuf([128, D], F32)
        P.dma("sp", T[k][:], ap, d_c, writes=[r_T])
    P.op("dve", STT(T["sc2r"][:], T["sc2r"][:], 1.0, T["n2gr"][:], ALU.add, ALU.mult), reads=[r_T], writes=[r_T])
    wt = P.sbuf([128, 16, D], BF16); r_w = Res()
    wst = P.sbuf([128, 4, 512], F32); r_wst = Res(); d_wst = P.dsem()
    for cgp in range(4):
        for k0 in range(0, 16, 4):
            P.dma("sp", wst[:], w_out[k0 * 128:(k0 + 4) * 128, cgp * 512:(cgp + 1) * 512].rearrange("(kc p) n -> p kc n", p=128), d_wst, writes=[r_wst])
            P.op("act", ACTV(wt[:, k0:k0 + 4, cgp * 512:(cgp + 1) * 512], wst[:], AF.Copy), reads=[r_wst], writes=[r_w])
    at = [P.sbuf([128, 16, 128], BF16) for _ in range(2)]; r_a = [Res(), Res()]; d_a = [P.dsem(), P.dsem()]
    xt = [P.sbuf([128, D], F32) for _ in range(2)]; r_x = [Res(), Res()]
    ps = [P.psum([128, 512], F32) for _ in range(4)]; r_ps = [Res() for _ in range(4)]
    yt = P.sbuf([128, D], F32); r_y = Res()
    junk = P.sbuf([128, D], BF16); r_junk = Res()
    ss = P.sbuf([128, 1], F32); rs = P.sbuf([128, 1], F32); r_ss = Res(); r_rs = Res()
    h2 = [P.sbuf([128, D], F32) for _ in range(2)]; r_h2 = [Res(), Res()]
    h2b = [P.sbuf([128, D], BF16) for _ in range(2)]; r_h2b = [Res(), Res()]
    for t in range(nt):
        sl = t % 2
        rows = slice(t * 128, (t + 1) * 128)
        P.dma("sp", at[sl][:], AT[:, rows].rearrange("(kc p) n -> p kc n", p=128), d_a[sl], writes=[r_a[sl]])
        P.dma("sp", xt[sl][:], xin[rows, :], d_a[sl], writes=[r_x[sl]])
        for cgp in range(4):
            for kc in range(16):
                P.op("pe", MM(ps[cgp][:], at[sl][:, kc, :], wt[:, kc, cgp * 512:(cgp + 1) * 512], kc == 0, kc == 15), reads=[r_a[sl], r_w], writes=[r_ps[cgp]])
            cs_ = slice(cgp * 512, (cgp + 1) * 512)
            P.op("dve", TT(yt[:, cs_], ps[cgp][:], T["boutr"][:, cs_], ALU.add), reads=[r_ps[cgp], r_T], writes=[r_y])
        P.op("dve", TT(yt[:], yt[:], T["g1r"][:], ALU.mult), reads=[r_y, r_T], writes=[r_y])
        P.op("dve", TT(xt[sl][:], xt[sl][:], yt[:], ALU.add), reads=[r_y, r_x[sl]], writes=[r_x[sl]])
        P.dma("sp", x1o[rows, :], xt[sl][:], d_st, reads=[r_x[sl]])
        P.op("act", ACTV(junk[:], xt[sl][:], AF.Square, accum_out=ss[:]), reads=[r_x[sl]], writes=[r_junk, r_ss])
        rstd_ops(P, ss[:], rs[:], D, r_ss, r_rs)
        P.op("dve", STT(h2[sl][:], xt[sl][:], rs[:, 0:1], T["sc2r"][:], ALU.mult, ALU.mult), reads=[r_x[sl], r_rs, r_T], writes=[r_h2[sl]])
        P.op("dve", TT(h2[sl][:], h2[sl][:], T["sh2r"][:], ALU.add), reads=[r_h2[sl], r_T], writes=[r_h2[sl]])
        P.op("act", ACTV(h2b[sl][:], h2[sl][:], AF.Copy), reads=[r_h2[sl]], writes=[r_h2b[sl]])
        P.dma("sp", h2o[rows, :], h2[sl][:], d_st, reads=[r_h2[sl]])
        P.dma("sp", h2bo[rows, :], h2b[sl][:], d_st, reads=[r_h2b[sl]])
    return P.finish()


def build_route(nt):
    P = Prog("route")
    n = nt * 128
    h2i = P.dram_in("h2", [n, D], F32)
    wgT = P.dram_in("wgT", [36, D], F32)
    bgr = P.dram_in("bgr", [128, 36], F32)
    iot = P.dram_in("iota8", [128, 8], F32)
    ro = P.dram_out("route", [n, 4], F32)
    d_c = P.dsem(); d_st = P.dsem(); d_h = P.dsem()
    hall = P.sbuf([128, nt, D], F32); r_h = Res()
    for t in range(nt):
        P.dma("sp", hall[:, t, :], h2i[t * 128:(t + 1) * 128, :], d_h, writes=[r_h])
    bg = P.sbuf([128, 36], F32); io = P.sbuf([128, 8], F32); r_c = Res()
    P.dma("sp", bg[:], bgr, d_c, writes=[r_c]); P.dma("sp", io[:], iot, d_c, writes=[r_c])
    wc = [P.sbuf([128, D], F32) for _ in range(2)]; r_wc = [Res(), Res()]; d_wc = [P.dsem(), P.dsem()]
    lg = P.sbuf([128, nt, 36], F32); r_lg = Res()
    junk = P.sbuf([128, D], F32); r_junk = Res()
    for j in range(36):
        sl = j % 2
        P.dma("sp", wc[sl][:], wgT[j:j + 1, :].partition_broadcast(128), d_wc[sl], writes=[r_wc[sl]])
        for t in range(nt):
            P.op("dve", TT(junk[:], hall[:, t, :], wc[sl][:], ALU.mult), reads=[r_h, r_wc[sl]], writes=[r_junk])
            P.op("dve", TRED(lg[:, t, j:j + 1], junk[:], ALU.add), reads=[r_junk], writes=[r_lg])
    S = {k: P.sbuf([128, w], F32) for k, w in (("l", 36), ("gmax", 1), ("e4", 4), ("s4", 1), ("oh4", 4), ("sel", 8), ("m1", 1), ("mk1", 8),
                                                ("sel2", 8), ("m2", 1), ("mk2", 8), ("t8", 8), ("dd", 1), ("out", 4), ("gi", 1), ("i4", 4))}
    r_s = Res()
    def dv(fn):
        P.op("dve", fn, reads=[r_s, r_lg, r_c], writes=[r_s])
    outt = [P.sbuf([128, 4], F32) for _ in range(2)]; r_out = [Res(), Res()]
    for t in range(nt):
        sl = t % 2
        dv(TT(S["l"][:], lg[:, t, :], bg[:], ALU.add))
        dv(TRED(S["gmax"][:], S["l"][:, 0:4], ALU.max))
        dv(TS(S["e4"][:], S["l"][:, 0:4], S["gmax"][:, 0:1], None, ALU.subtract))
        P.op("act", ACTV(S["e4"][:], S["e4"][:], AF.Exp), reads=[r_s], writes=[r_s])
        dv(TRED(S["s4"][:], S["e4"][:], ALU.add))
        dv(RECIP(S["s4"][:], S["s4"][:]))
        dv(TS(S["oh4"][:], S["l"][:, 0:4], S["gmax"][:, 0:1], None, ALU.is_equal))
        dv(TS(S["sel"][:], S["l"][:, 4:12], S["oh4"][:, 0:1], None, ALU.mult))
        for g in range(1, 4):
            dv(STT(S["sel"][:], S["l"][:, 4 + 8 * g:12 + 8 * g], S["oh4"][:, g:g + 1], S["sel"][:], ALU.mult, ALU.add))
        dv(TT(S["i4"][:], S["oh4"][:], io[:, 0:4], ALU.mult))
        dv(TRED(S["gi"][:], S["i4"][:], ALU.add))
        dv(TRED(S["m1"][:], S["sel"][:], ALU.max))
        dv(TS(S["mk1"][:], S["sel"][:], S["m1"][:, 0:1], None, ALU.is_equal))
        dv(STT(S["sel2"][:], S["mk1"][:], -1e30, S["sel"][:], ALU.mult, ALU.add))
        dv(TRED(S["m2"][:], S["sel2"][:], ALU.max))
        dv(TS(S["mk2"][:], S["sel2"][:], S["m2"][:, 0:1], None, ALU.is_equal))
        o = outt[sl]
        P.op("dve", TT(S["t8"][:], S["mk1"][:], io[:], ALU.mult), reads=[r_s, r_c], writes=[r_s])
        P.op("dve", TRED(o[:, 0:1], S["t8"][:], ALU.add), reads=[r_s], writes=[r_out[sl]])
        P.op("dve", TT(S["t8"][:], S["mk2"][:], io[:], ALU.mult), reads=[r_s, r_c, r_out[sl]], writes=[r_s])
        P.op("dve", TRED(o[:, 1:2], S["t8"][:], ALU.add), reads=[r_s], writes=[r_out[sl]])
        for k in range(2):
            P.op("dve", STT(o[:, k:k + 1], S["gi"][:], 8.0, o[:, k:k + 1], ALU.mult, ALU.add), reads=[r_s, r_out[sl]], writes=[r_out[sl]])
        dv(TT(S["dd"][:], S["m2"][:], S["m1"][:], ALU.subtract))
        P.op("act", ACTV(S["dd"][:], S["dd"][:], AF.Exp), reads=[r_s], writes=[r_s])
        dv(TS(S["dd"][:], S["dd"][:], 1.0, None, ALU.add))
        dv(RECIP(S["dd"][:], S["dd"][:]))
        P.op("dve", TT(o[:, 2:3], S["dd"][:], S["s4"][:], ALU.mult), reads=[r_s, r_out[sl]], writes=[r_out[sl]])
        P.op("dve", TT(o[:, 3:4], S["s4"][:], o[:, 2:3], ALU.subtract), reads=[r_s, r_out[sl]], writes=[r_out[sl]])
        P.dma("sp", ro[t * 128:(t + 1) * 128, :], o[:], d_st, reads=[r_out[sl]])
    return P.finish()


def build_moe(cap):
    P = Prog("moe")
    xsT = P.dram_in("xsT", [4, D, cap], BF16)
    wg = P.dram_in("wg", [4, D, 1024], F32)
    wu = P.dram_in("wu", [4, D, 1024], F32)
    wd = P.dram_in("wd", [4, 1024, D], F32)
    yT = P.dram_out("yT", [4, D, cap], BF16)
    d_st = P.dsem()
    wgt = P.sbuf([128, 16, 1024], BF16); wut = P.sbuf([128, 16, 1024], BF16); wdt = P.sbuf([128, 8, D], BF16)
    r_wg = Res(); r_wu = Res(); r_wd = Res()
    wst = P.sbuf([128, 4, 512], F32); r_wst = Res(); d_wst = P.dsem()
    xt = [P.sbuf([128, 16, 512], BF16) for _ in range(2)]; r_x = [Res(), Res()]; d_x = [P.dsem(), P.dsem()]
    psg = [P.psum([128, 512], F32) for _ in range(2)]; psu = [P.psum([128, 512], F32) for _ in range(2)]; psy = [P.psum([128, 512], F32) for _ in range(2)]
    r_pg = [Res(), Res()]; r_pu = [Res(), Res()]; r_py = [Res(), Res()]
    sg = P.sbuf([128, 512], F32); r_sg = Res()
    ht = P.sbuf([128, 8, 512], BF16); r_ht = Res()
    yo = [P.sbuf([128, 16, 512], BF16) for _ in range(2)]; r_yo = [Res(), Res()]
    ib = 0
    for e_ in range(4):
        for dst, src, nk, r_ in ((wgt, wg[e_], 16, r_wg), (wut, wu[e_], 16, r_wu)):
            for c0 in range(0, 1024, 512):
                for k0 in range(0, nk, 4):
                    P.dma("sp", wst[:], src[k0 * 128:(k0 + 4) * 128, c0:c0 + 512].rearrange("(kc p) n -> p kc n", p=128), d_wst, writes=[r_wst])
                    P.op("act", ACTV(dst[:, k0:k0 + 4, c0:c0 + 512], wst[:], AF.Copy), reads=[r_wst], writes=[r_])
        for c0 in range(0, D, 512):
            for k0 in range(0, 8, 4):
                P.dma("sp", wst[:], wd[e_][k0 * 128:(k0 + 4) * 128, c0:c0 + 512].rearrange("(kc p) n -> p kc n", p=128), d_wst, writes=[r_wst])
                P.op("act", ACTV(wdt[:, k0:k0 + 4, c0:c0 + 512], wst[:], AF.Copy), reads=[r_wst], writes=[r_wd])
        for sb in range(cap // 512):
            sl = ib % 2; ib += 1
            cols = slice(sb * 512, (sb + 1) * 512)
            P.dma("sp", xt[sl][:], xsT[e_][:, cols].rearrange("(kc p) n -> p kc n", p=128), d_x[sl], writes=[r_x[sl]])
            for fc in range(8):
                b_ = fc % 2
                for kc in range(16):
                    P.op("pe", MM(psg[b_][:], wgt[:, kc, fc * 128:(fc + 1) * 128], xt[sl][:, kc, :], kc == 0, kc == 15), reads=[r_wg, r_x[sl]], writes=[r_pg[b_]])
                for kc in range(16):
                    P.op("pe", MM(psu[b_][:], wut[:, kc, fc * 128:(fc + 1) * 128], xt[sl][:, kc, :], kc == 0, kc == 15), reads=[r_wu, r_x[sl]], writes=[r_pu[b_]])
                P.op("act", ACTV(sg[:], psg[b_][:], AF.Silu), reads=[r_pg[b_]], writes=[r_sg])
                P.op("dve", TT(ht[:, fc, :], sg[:], psu[b_][:], ALU.mult), reads=[r_sg, r_pu[b_]], writes=[r_ht])
            for dc in range(16):
                b_ = dc % 2
                for fc in range(8):
                    P.op("pe", MM(psy[b_][:], wdt[:, fc, dc * 128:(dc + 1) * 128], ht[:, fc, :], fc == 0, fc == 7), reads=[r_wd, r_ht], writes=[r_py[b_]])
                P.op("act", ACTV(yo[sl][:, dc, :], psy[b_][:], AF.Copy), reads=[r_py[b_]], writes=[r_yo[sl]])
            P.dma("sp", yT[e_][:, cols].rearrange("(kc p) n -> p kc n", p=128), yo[sl][:], d_st, reads=[r_yo[sl]])
    return P.finish()


_PROGS = {}


def _prog(key, builder):
    if key not in _PROGS:
        _PROGS[key] = builder()
    return _PROGS[key]


def _tok_shard(a):
    return [a[c // 2, (c % 2) * 2048:(c % 2 + 1) * 2048] for c in range(8)]


def _moe_layer(l, h2, h2b, inp, x1_sh, M, next_norm):
    wgT = np.ascontiguousarray(np.concatenate([inp["moe_wg1"][l], inp["moe_wg2"][l]], 1).T)
    bgr = rep128(np.concatenate([inp["moe_bg1"][l], inp["moe_bg2"][l]]))
    iota8 = rep128(np.arange(8, dtype=np.float32))
    res = run(_prog("route", lambda: build_route(16)), [{"h2": h2[c], "wgT": wgT, "bgr": bgr, "iota8": iota8} for c in range(8)])
    route = np.concatenate([r["route"] for r in res], 0)
    eid = np.rint(route[:, 0:2]).astype(np.int64)
    h2b_all = np.concatenate(h2b, 0)
    flat_e = eid.reshape(-1)
    order = np.argsort(flat_e, kind="stable")
    counts = np.bincount(flat_e, minlength=32)
    cap = max(512, int(-(-counts.max() // 512) * 512))
    starts = np.cumsum(counts) - counts
    slot = np.empty(flat_e.shape[0], np.int64)
    slot[order] = np.arange(flat_e.shape[0]) - starts[flat_e[order]]
    in_maps = []
    for c in range(8):
        xsT = np.zeros((4, D, cap), NPBF16)
        for j in range(4):
            e_ = 4 * c + j
            a = order[starts[e_]:starts[e_] + counts[e_]]
            xsT[j][:, :counts[e_]] = h2b_all[a // 2].T
        in_maps.append({"xsT": xsT, "wg": inp["moe_w_gate"][l, 4 * c:4 * c + 4], "wu": inp["moe_w_up"][l, 4 * c:4 * c + 4],
                        "wd": inp["moe_w_down"][l, 4 * c:4 * c + 4]})
    res = run(_prog(("moe", cap), lambda: build_moe(cap)), in_maps)
    yT = np.stack([r["yT"] for r in res]).reshape(32, D, cap)
    Y = np.ascontiguousarray(yT.transpose(0, 2, 1))[flat_e, slot].reshape(16384, 2, D)
    in_maps = []
    for c in range(8):
        b = c // 2
        rows = slice(c * 2048, (c + 1) * 2048)
        m = {"xin": x1_sh[c], "Y": Y[rows], "gates": np.ascontiguousarray(route[rows, 2:4]), "g2r": rep128(M[b, l, 5 * D:6 * D])}
        if next_norm:
            m.update({"ngr": rep128(inp["norm1_g"][l + 1]), "scr": rep128(M[b, l + 1, D:2 * D]), "shr": rep128(M[b, l + 1, 0:D])})
        in_maps.append(m)
    return run(_prog(("comb", next_norm), lambda: build_comb(16, True, next_norm)), in_maps)


def _post(AT_sh, x_sh, w_out, b_out, inp, M, l):
    in_maps = []
    for c in range(8):
        b = c // 2
        in_maps.append({"AT": AT_sh[c], "xin": x_sh[c], "w_out": w_out, "g1r": rep128(M[b, l, 2 * D:3 * D]), "boutr": rep128(b_out),
                        "n2gr": rep128(inp["norm2_g"][l]), "sc2r": rep128(M[b, l, 4 * D:5 * D]), "sh2r": rep128(M[b, l, 3 * D:4 * D])})
    res = run(_prog("post", lambda: build_post(16)), in_maps)
    return [r["x1"] for r in res], [r["h2"] for r in res], [r["h2b"] for r in res]


def kernel(**inp):
    inp = {k: np.asarray(v) for k, v in inp.items()}
    x = inp["x"]; ctx = inp["ctx"]
    cc = np.concatenate([inp["c"], inp["c_ctx"][None]], 0)
    sT = np.ascontiguousarray(cc.T.reshape(16, 128, 5).transpose(1, 0, 2))
    in_maps = [{"sT": sT, "W": np.ascontiguousarray(inp["ada_w"][:, :, i * 1536:(i + 1) * 1536]),
                "B": np.ascontiguousarray(np.broadcast_to(inp["ada_b"][:, i * 1536:(i + 1) * 1536][None], (5, 2, 1536)))} for i in range(8)]
    res = run(_prog("mod", build_mod), in_maps)
    M = np.concatenate([r["M"] for r in res], axis=2)
    x_sh = _tok_shard(x)
    in_maps = []
    for c in range(8):
        b, hf = c // 2, c % 2
        in_maps.append({"xin": np.concatenate([x_sh[c], ctx[b, hf * 128:(hf + 1) * 128]], 0), "ngr": rep128(inp["norm1_g"][0]),
                        "scr": rep128(M[b, 0, D:2 * D]), "shr": rep128(M[b, 0, 0:D]), "cscr": rep128(M[4, 0, D:2 * D]), "cshr": rep128(M[4, 0, 0:D])})
    res = run(_prog("norm0", lambda: build_comb(17, False, True, ctx_tiles=1)), in_maps)
    h0 = [r["hout"] for r in res]
    in_maps = []
    for c in range(8):
        hf = c % 2
        pos = np.arange(hf * 2048, (hf + 1) * 2048)
        cd, sd = rope_tables(pos, 64); cs, sn = rope_tables(pos, 128)
        one = lambda a: np.concatenate([a, np.ones((128, a.shape[1]), np.float32)], 0)
        zero = lambda a: np.concatenate([a, np.zeros((128, a.shape[1]), np.float32)], 0)
        in_maps.append({"hT": np.ascontiguousarray(h0[c].T), "w_in": inp["attn_w_in"][0],
                        "gq": rep128(np.tile(inp["diff_q_g"][0], 8)), "gk": rep128(np.tile(inp["diff_k_g"][0], 8)),
                        "gsq": rep128(np.tile(inp["swa_q_g"][0], 4)), "gsk": rep128(np.tile(inp["swa_k_g"][0], 4)),
                        "cosd": one(cd), "sind": zero(sd), "coss": one(cs), "sins": zero(sn)})
    res = run(_prog("qkv", lambda: build_qkv(17)), in_maps)
    qkv = np.stack([r["qkv"] for r in res])
    res = run(_prog("att", build_att), att_inputs(qkv, inp))
    AT_sh = [np.ascontiguousarray(r["A"].T) for r in res]
    x1_sh, h2, h2b = _post(AT_sh, x_sh, inp["attn_w_out"][0], np.zeros(D, np.float32), inp, M, 0)
    res = _moe_layer(0, h2, h2b, inp, x1_sh, M, True)
    x2_sh = [r["xout"] for r in res]; h1 = [r["hout"] for r in res]
    if _DEBUG.get("stop") == "l0":
        return np.stack(x2_sh).reshape(4, 4096, D)
    ycvT_sh = hyena_layer(h1, inp)[0]
    x3_sh, h2, h2b = _post(ycvT_sh, x2_sh, inp["hy_w_out"][0], inp["hy_b_out"][0], inp, M, 1)
    res = _moe_layer(1, h2, h2b, inp, x3_sh, M, False)
    return np.stack([r["xout"] for r in res]).reshape(4, 4096, D).astype(np.float32)


_DEBUG = {}


def sin_reduced(P, v, tmp, ki, negpi_unused, r_v, r_tmp):
    import math
    P.op("dve", TS(v, v, 1.0 / (2 * math.pi), 16.5, ALU.mult, ALU.add), reads=[r_v], writes=[r_v])
    P.op("dve", COPY(ki, v), reads=[r_v], writes=[r_tmp])
    P.op("dve", COPY(tmp, ki), reads=[r_tmp], writes=[r_tmp])
    P.op("dve", TT(v, v, tmp, ALU.subtract), reads=[r_v, r_tmp], writes=[r_v])
    P.op("dve", TS(v, v, 2 * math.pi, -math.pi, ALU.mult, ALU.add), reads=[r_v], writes=[r_v])
    P.op("dve", TS(tmp, v, -math.pi, None, ALU.is_lt), reads=[r_v, r_tmp], writes=[r_tmp])
    P.op("dve", STT(v, tmp, 2 * math.pi, v, ALU.mult, ALU.add), reads=[r_v, r_tmp], writes=[r_v])
    P.op("dve", TS(v, v, 3.1415925, -3.1415925, ALU.min, ALU.max), reads=[r_v], writes=[r_v])
    P.op("act", ACTV(v, v, AF.Sin), reads=[r_v], writes=[r_v])


def build_hyin(nt=16):
    P = Prog("hyin")
    n = nt * 128
    NCOL = 6144
    hTp = P.dram_in("hTp", [D, n + 2], BF16)
    onesr = P.dram_in("onesr", [1, n + 2], BF16)
    W = P.dram_in("W", [D, NCOL], F32)
    b_in = P.dram_in("b_in", [1, NCOL], F32)
    cw = P.dram_in("cw", [128, 3, NCOL], F32)
    cb = P.dram_in("cb", [128, NCOL], F32)
    zo = P.dram_out("z", [n, NCOL], BF16)
    d_c = P.dsem(); d_st = P.dsem(); d_g = P.dsem()
    ow = P.sbuf([1, n + 2], BF16); r_ow = Res()
    P.dma("sp", ow[:], onesr, d_c, writes=[r_ow])
    wt = [P.sbuf([128, 16, 512], BF16) for _ in range(3)]; r_wt = Res()
    wst = P.sbuf([128, 4, 512], F32); r_wst = Res(); d_wst = P.dsem()
    cwt = P.sbuf([128, 3, 512], F32); cbt = P.sbuf([128, 512], F32); bint = P.sbuf([1, 512], F32); r_g = Res()
    brf = P.sbuf([1, 3, 512], F32); brow = P.sbuf([1, 3, 512], BF16); r_br = Res()
    ht = [P.sbuf([128, 16, 130], BF16) for _ in range(2)]; r_h = [Res(), Res()]; d_h = [P.dsem(), P.dsem()]
    ps = [P.psum([128, 512], F32) for _ in range(2)]; r_ps = [Res(), Res()]
    ob = [P.sbuf([128, 512], BF16) for _ in range(2)]; r_ob = [Res(), Res()]
    it = 0
    for cg in range(NCOL // 512):
        cs_ = slice(cg * 512, (cg + 1) * 512)
        P.dma("sp", cwt[:], cw[:, :, cs_], d_g, writes=[r_g])
        P.dma("sp", cbt[:], cb[:, cs_], d_g, writes=[r_g])
        P.dma("sp", bint[:], b_in[:, cs_], d_g, writes=[r_g])
        for k0 in range(0, 16, 4):
            P.dma("sp", wst[:], W[k0 * 128:(k0 + 4) * 128, cs_].rearrange("(kc p) n -> p kc n", p=128), d_wst, writes=[r_wst])
            for j in range(3):
                P.op("dve", TT(wt[j][:, k0:k0 + 4, :], wst[:], cwt[:, j, :].unsqueeze(1).to_broadcast([128, 4, 512]), ALU.mult), reads=[r_wst, r_g], writes=[r_wt])
        P.op("dve", TT(brf[:], cwt[0:1, :, :], bint[:].unsqueeze(1).to_broadcast([1, 3, 512]), ALU.mult), reads=[r_g], writes=[r_br])
        P.op("dve", COPY(brow[:], brf[:]), reads=[r_br], writes=[r_br])
        for t in range(nt):
            sl = it % 2; it += 1
            P.dma("sp", ht[sl][:], hTp[:, t * 128:t * 128 + 130].rearrange("(kc p) n -> p kc n", p=128), d_h[sl], writes=[r_h[sl]])
            first = True
            for j in range(3):
                for kc in range(16):
                    P.op("pe", MM(ps[sl][:], ht[sl][:, kc, j:j + 128], wt[j][:, kc, :], first, False), reads=[r_h[sl], r_wt], writes=[r_ps[sl]])
                    first = False
                P.op("pe", MM(ps[sl][:], ow[0:1, t * 128 + j:t * 128 + j + 128], brow[0:1, j, :], False, j == 2), reads=[r_ow, r_br], writes=[r_ps[sl]])
            P.op("dve", TT(ob[sl][:], ps[sl][:], cbt[:], ALU.add), reads=[r_ps[sl], r_g], writes=[r_ob[sl]])
            P.dma("sp", zo[t * 128:(t + 1) * 128, cs_], ob[sl][:], d_st, reads=[r_ob[sl]])
    return P.finish()


def build_filt():
    P = Prog("filt")
    zT = P.dram_in("zT", [33, 8192], F32)
    w1 = P.dram_in("w1", [33, 64], F32); w2 = P.dram_in("w2", [64, 64], F32)
    cols = P.dram_in("cols", [64, 4], F32)
    w3s = P.dram_in("w3s", [64, 2, 2, 256], F32)
    text = P.dram_in("text", [128, 8192], F32)
    nad = P.dram_in("nad", [128, 2], F32)
    biasc = P.dram_in("biasc", [128, 2, 2], F32)
    kl = P.dram_out("kl", [2, 256, 8192], BF16)
    d_c = P.dsem(); d_st = P.dsem()
    zt = P.sbuf([33, 8192], F32); w1t = P.sbuf([33, 64], F32); w2t = P.sbuf([64, 64], F32); ct = P.sbuf([64, 4], F32)
    w3t = P.sbuf([64, 2, 2, 256], F32); tx = P.sbuf([128, 8192], F32); nd = P.sbuf([128, 2], F32); bc = P.sbuf([128, 2, 2], F32)
    r_c = Res()
    for t_, ap in ((zt, zT), (w1t, w1), (w2t, w2), (ct, cols), (w3t, w3s), (tx, text), (nd, nad), (bc, biasc)):
        P.dma("sp", t_[:], ap, d_c, writes=[r_c])
    ps1 = P.psum([128, 512], F32); ps2 = P.psum([128, 512], F32); r_p1 = Res(); r_p2 = Res()
    ps3 = [P.psum([128, 512], F32) for _ in range(2)]; r_p3 = [Res(), Res()]
    a1 = P.sbuf([64, 512], F32); a2 = P.sbuf([64, 512], F32); r_a1 = Res(); r_a2 = Res()
    tmp = P.sbuf([64, 512], F32); ki = P.sbuf([64, 512], I32); r_tmp = Res()
    dec = P.sbuf([128, 512], F32); r_dec = Res()
    kf = P.sbuf([128, 512], F32); r_kf = Res()
    kb = [P.sbuf([128, 512], BF16) for _ in range(2)]; r_kb = [Res(), Res()]
    i3 = 0
    for blk in range(16):
        dr = 1 if blk < 8 else 0
        cs_ = slice(blk * 512, (blk + 1) * 512)
        P.op("pe", MM(ps1[0:64, :], w1t[:], zt[:, cs_], True, True), reads=[r_c], writes=[r_p1])
        P.op("dve", TS(a1[:], ps1[0:64, :], ct[:, 0:1], ct[:, 1:2], ALU.add, ALU.mult), reads=[r_p1, r_c], writes=[r_a1])
        sin_reduced(P, a1[:], tmp[:], ki[:], None, r_a1, r_tmp)
        P.op("pe", MM(ps2[0:64, :], w2t[:], a1[:], True, True), reads=[r_c, r_a1], writes=[r_p2])
        P.op("dve", TS(a2[:], ps2[0:64, :], ct[:, 2:3], ct[:, 3:4], ALU.add, ALU.mult), reads=[r_p2, r_c], writes=[r_a2])
        sin_reduced(P, a2[:], tmp[:], ki[:], None, r_a2, r_tmp)
        for c2 in range(2):
            P.op("act", ACTV(dec[:], tx[:, cs_], AF.Exp, scale=nd[:, c2:c2 + 1]), reads=[r_c], writes=[r_dec])
            for o in range(2):
                b_ = i3 % 2; i3 += 1
                P.op("pe", MM(ps3[b_][:], w3t[:, o, dr, c2 * 128:(c2 + 1) * 128], a2[:], True, True), reads=[r_c, r_a2], writes=[r_p3[b_]])
                P.op("dve", TT(kf[:], ps3[b_][:], dec[:], ALU.mult), reads=[r_p3[b_], r_dec], writes=[r_kf])
                if blk == 8:
                    P.op("dve", TT(kf[:, 0:1], kf[:, 0:1], bc[:, o, c2:c2 + 1], ALU.add), reads=[r_kf, r_c], writes=[r_kf])
                P.op("act", ACTV(kb[b_][:], kf[:], AF.Copy), reads=[r_kf], writes=[r_kb[b_]])
                P.dma("sp", kl[o, c2 * 128:(c2 + 1) * 128, cs_], kb[b_][:], d_st, reads=[r_kb[b_]])
    return P.finish()


def build_conv():
    P = Prog("conv")
    Zv = P.dram_in("Zv", [128, 256, 128], BF16)
    Zx1 = P.dram_in("Zx1", [128, 256, 128], BF16)
    Zx2 = P.dram_in("Zx2", [128, 256, 128], BF16)
    kl1 = P.dram_in("kl1", [256, 8192], BF16)
    rl2 = P.dram_in("rl2", [256, 8192], BF16)
    yo = P.dram_out("ycv", [128, 256, 128], BF16)
    d_st = P.dsem()
    G = 64
    vt = P.sbuf([128, G, 128], BF16); x1t = P.sbuf([128, G, 128], BF16); x2t = P.sbuf([128, G, 128], BF16)
    r_in = Res(); d_in = P.dsem()
    ot = P.sbuf([128, G, 128], BF16); r_ot = Res()
    s1 = [P.sbuf([128, 8064], BF16) for _ in range(2)]; s2 = [P.sbuf([128, 8064], BF16) for _ in range(2)]
    r_s1 = [Res(), Res()]; r_s2 = [Res(), Res()]; d_s1 = [P.dsem(), P.dsem()]; d_s2 = [P.dsem(), P.dsem()]
    pa = [P.psum([128, 512], F32) for _ in range(2)]; pb = [P.psum([128, 512], F32) for _ in range(2)]
    r_pa = [Res(), Res()]; r_pb = [Res(), Res()]
    y1 = [P.sbuf([128, 128], BF16) for _ in range(2)]; r_y1 = [Res(), Res()]
    dseq = [0]
    for a in range(1, 32):
        dseq += [a, -a]
    for g in range(256 // G):
        P.dma("sp", vt[:], Zv[:, g * G:(g + 1) * G, :], d_in, writes=[r_in])
        P.dma("sp", x1t[:], Zx1[:, g * G:(g + 1) * G, :], d_in, writes=[r_in])
        P.dma("sp", x2t[:], Zx2[:, g * G:(g + 1) * G, :], d_in, writes=[r_in])
        for c in range(G):
            ch = g * G + c
            sl = ch % 2
            P.dma("sp", s1[sl][:], bass.AP(tensor=kl1.tensor, offset=ch * 8192 + 1, ap=[[1, 128], [1, 8064]]), d_s1[sl], writes=[r_s1[sl]])
            P.dma("sp", s2[sl][:], bass.AP(tensor=rl2.tensor, offset=ch * 8192, ap=[[1, 128], [1, 8064]]), d_s2[sl], writes=[r_s2[sl]])
            for n_, d_ in enumerate(dseq):
                T0 = max(0, d_); T1 = min(32, 32 + d_)
                P.op("pe", MM(pa[sl][:, 4 * T0:4 * T1], s1[sl][:, (d_ + 31) * 128:(d_ + 32) * 128], vt[:, c, 4 * (T0 - d_):4 * (T1 - d_)], n_ == 0, n_ == 62),
                     reads=[r_s1[sl], r_in], writes=[r_pa[sl]])
            P.op("dve", TT(y1[sl][:], pa[sl][:, 0:128], x1t[:, c, :], ALU.mult), reads=[r_pa[sl], r_in], writes=[r_y1[sl]])
            for n_, d_ in enumerate(dseq):
                T0 = max(0, d_); T1 = min(32, 32 + d_)
                P.op("pe", MM(pb[sl][:, 4 * T0:4 * T1], s2[sl][:, (31 - d_) * 128:(32 - d_) * 128], y1[sl][:, 4 * (T0 - d_):4 * (T1 - d_)], n_ == 0, n_ == 62),
                     reads=[r_s2[sl], r_y1[sl]], writes=[r_pb[sl]])
            P.op("dve", TT(ot[:, c, :], pb[sl][:, 0:128], x2t[:, c, :], ALU.mult), reads=[r_pb[sl], r_in], writes=[r_ot])
        P.dma("sp", yo[:, g * G:(g + 1) * G, :], ot[:], d_st, reads=[r_ot])
    return P.finish()


def _filter_consts():
    n = 4096
    idx = np.arange(8192)
    a = np.minimum(np.abs(idx - 4096), n - 1)
    t = np.linspace(0.0, 1.0, n, dtype=np.float32)
    w = (np.float32(2.0 * np.pi) * np.arange(n, dtype=np.float32) / np.float32(n)).astype(np.float32)
    f = np.linspace(1e-4, 15.0, 16, dtype=np.float32)
    wf = (w[:, None] * f[None, :]).astype(np.float32)
    z = np.concatenate([t[:, None], np.cos(wf), -np.sin(wf)], axis=-1).astype(np.float32)
    zT = np.ascontiguousarray(z[a].T)
    text = rep128(t[a])
    max_decay = np.log(1e-2) / 0.3; min_decay = np.log(1e-2) / 1.5
    deltas = np.abs(np.linspace(min_decay, max_decay, 2048, dtype=np.float32)).astype(np.float32)
    return zT, text, deltas


def hyena_layer(h1, inp):
    cw = rep128(inp["hy_conv_w"][0]); cb = rep128(inp["hy_conv_b"][0])
    in_maps = []
    for c in range(8):
        hf = c % 2
        hTp = np.zeros((D, 2050), NPBF16); ones = np.zeros((1, 2050), NPBF16)
        hTp[:, 1:2049] = h1[c].T; ones[0, 1:2049] = 1
        if hf == 1:
            hTp[:, 0] = h1[c - 1][-1]; ones[0, 0] = 1
        else:
            hTp[:, 2049] = h1[c + 1][0]; ones[0, 2049] = 1
        in_maps.append({"hTp": hTp, "onesr": ones, "W": inp["hy_w_in"][0], "b_in": inp["hy_b_in"][0][None], "cw": cw, "cb": cb})
    res = run(_prog("hyin", build_hyin), in_maps)
    z = np.stack([r["z"] for r in res]).reshape(4, 4096, 6144)
    zT, text, deltas = _filter_consts()
    w3 = inp["flt_w3"][0].reshape(64, 2, 2, 2048)
    colsv = np.stack([inp["flt_b1"][0], inp["flt_f1"][0], inp["flt_b2"][0], inp["flt_f2"][0]], 1).astype(np.float32)
    in_maps = []
    for c in range(8):
        chs = slice(256 * c, 256 * c + 256)
        nad = np.ascontiguousarray((-deltas[chs]).reshape(2, 128).T)
        biasc = np.ascontiguousarray(inp["hy_bias"][0][:, chs].reshape(2, 2, 128).transpose(2, 0, 1))
        in_maps.append({"zT": zT, "w1": inp["flt_w1"][0], "w2": inp["flt_w2"][0], "cols": colsv, "w3s": np.ascontiguousarray(w3[:, :, :, chs]),
                        "text": text, "nad": nad, "biasc": biasc})
    res = run(_prog("filt", build_filt), in_maps)
    kls = [r["kl"] for r in res]
    def lay(a, rev):
        v = a.reshape(4, 32, 128, 256).transpose(2, 3, 1, 0).reshape(128, 256, 128)
        return np.ascontiguousarray(v[::-1] if rev else v)
    in_maps = []
    for c in range(8):
        chs = slice(256 * c, 256 * c + 256)
        in_maps.append({"Zv": lay(z[:, :, chs], True), "Zx1": lay(z[:, :, 2048 + 256 * c:2048 + 256 * c + 256], False),
                        "Zx2": lay(z[:, :, 4096 + 256 * c:4096 + 256 * c + 256], True),
                        "kl1": kls[c][0], "rl2": np.ascontiguousarray(kls[c][1][:, ::-1])})
    res = run(_prog("conv", build_conv), in_maps)
    ys = []
    for c in range(8):
        y = res[c]["ycv"][::-1].reshape(128, 256, 32, 4).transpose(3, 2, 0, 1).reshape(4, 4096, 256)
        ys.append(y)
    ycat = np.concatenate(ys, axis=2)
    return [np.ascontiguousarray(ycat[c // 2, (c % 2) * 2048:(c % 2 + 1) * 2048].T) for c in range(8)], z, kls
```

```python
import contextlib
import numpy as np
import ml_dtypes
import concourse.bass as bass
import concourse.mybir as mybir
from concourse.bass_utils import run_bass_kernel_spmd

F32 = mybir.dt.float32
BF16 = mybir.dt.bfloat16
I32 = mybir.dt.int32
ALU = mybir.AluOpType
AF = mybir.ActivationFunctionType
AX = mybir.AxisListType
NPBF16 = ml_dtypes.bfloat16


SAME_ENGINE_SYNC = True


class Res:
    __slots__ = ("w", "r")

    def __init__(self):
        self.w = None
        self.r = {}


class DmaSem:
    def __init__(self, sem):
        self.sem = sem
        self.n = 0


class Prog:
    ENGS = ("pe", "act", "dve", "pool", "sp")

    def __init__(self, name="k"):
        self.nc = bass.Bass("TRN2", target_bir_lowering=False)
        self.es = contextlib.ExitStack()
        self.streams = {e: [] for e in self.ENGS}
        self.cnt = {e: 0 for e in self.ENGS}
        self.seen = {e: {} for e in self.ENGS}
        self.esem = {e: self.es.enter_context(self.nc.semaphore("sem_" + e)) for e in self.ENGS}
        self.dsems = []
        self.uid = 0

    def sbuf(self, shape, dtype, name=None):
        self.uid += 1
        return self.es.enter_context(self.nc.sbuf_tensor(name or f"sb{self.uid}", list(shape), dtype))

    def psum(self, shape, dtype, name=None):
        self.uid += 1
        return self.es.enter_context(self.nc.psum_tensor(name or f"ps{self.uid}", list(shape), dtype))

    def dsem(self):
        self.uid += 1
        d = DmaSem(self.es.enter_context(self.nc.semaphore(f"dsem{self.uid}")))
        self.dsems.append(d)
        return d

    def dram_in(self, name, shape, dtype):
        return self.nc.dram_tensor(name, list(shape), dtype, kind="ExternalInput").ap()

    def dram_out(self, name, shape, dtype):
        return self.nc.dram_tensor(name, list(shape), dtype, kind="ExternalOutput").ap()

    def dram_tmp(self, name, shape, dtype):
        return self.nc.dram_tensor(name, list(shape), dtype).ap()

    def op(self, eng, fn, reads=(), writes=(), dsem=None):
        deps = []
        for r in reads:
            if r.w is not None:
                deps.append(r.w)
        for w in writes:
            if w.w is not None:
                deps.append(w.w)
            deps.extend(w.r.items())
        waits = []
        seen = self.seen[eng]
        for src, n in deps:
            if src == eng and (eng == "pe" or not SAME_ENGINE_SYNC):
                continue
            if seen.get(src, 0) >= n:
                continue
            seen[src] = n
            waits.append((src, n))
        if dsem is None:
            self.cnt[eng] += 1
            tok = (eng, self.cnt[eng])
        else:
            dsem.n += 1
            tok = (dsem, dsem.n)
        self.streams[eng].append((waits, fn, tok))
        for r in reads:
            if r.r.get(tok[0], 0) < tok[1]:
                r.r[tok[0]] = tok[1]
        for w in writes:
            w.w = tok
            w.r = {}
        return tok

    def dma(self, eng, out, in_, dsem, reads=(), writes=(), **kw):
        return self.op(eng, lambda e: e.dma_start(out=out, in_=in_, **kw), reads, writes, dsem=dsem)

    def finish(self):
        fin = []
        for d in self.dsems:
            if d.n:
                fin.append((d, d.n))
        self.streams["sp"].append((fin, None, None))
        nc = self.nc
        with nc.Block() as block:
            def emit(e, name):
                for waits, fn, tok in self.streams[name]:
                    for src, n in waits:
                        if isinstance(src, DmaSem):
                            e.wait_ge(src.sem, 16 * n)
                        else:
                            e.wait_ge(self.esem[src], n)
                    if fn is None:
                        continue
                    ins = fn(e)
                    if isinstance(tok[0], DmaSem):
                        ins.then_inc(tok[0].sem, 16)
                    else:
                        ins.then_inc(self.esem[name], 1)

            @block.tensor
            def _(e):
                emit(e, "pe")

            @block.scalar
            def _(e):
                emit(e, "act")

            @block.vector
            def _(e):
                emit(e, "dve")

            @block.gpsimd
            def _(e):
                emit(e, "pool")

            @block.sync
            def _(e):
                emit(e, "sp")
        self.es.close()
        return nc


_TRACE = {"on": False, "log": []}


def run(prog_nc, in_maps):
    if _TRACE["on"]:
        res = run_bass_kernel_spmd(prog_nc, in_maps, core_ids=list(range(8)), trace=True)
        _TRACE["log"].append(res.exec_time_ns)
        print("EXEC_TIME_NS", res.exec_time_ns, flush=True)
        return res.results
    res = run_bass_kernel_spmd(prog_nc, in_maps, core_ids=list(range(8)))
    return res.results


def MM(out, lhsT, rhs, start, stop):
    return lambda e: e.matmul(out, lhsT=lhsT, rhs=rhs, start=start, stop=stop)


def TR(out, in_, ident):
    return lambda e: e.transpose(out, in_, ident)


def ACTV(out, in_, func, **kw):
    return lambda e: e.activation(out=out, in_=in_, func=func, **kw)


def TT(out, in0, in1, op):
    return lambda e: e.tensor_tensor(out=out, in0=in0, in1=in1, op=op)


def TS(out, in0, s1, s2, op0, op1=None):
    if op1 is None:
        return lambda e: e.tensor_scalar(out=out, in0=in0, scalar1=s1, scalar2=None, op0=op0)
    return lambda e: e.tensor_scalar(out=out, in0=in0, scalar1=s1, scalar2=s2, op0=op0, op1=op1)


def STT(out, in0, scalar, in1, op0, op1):
    return lambda e: e.scalar_tensor_tensor(out=out, in0=in0, scalar=scalar, in1=in1, op0=op0, op1=op1)


def TRED(out, in_, op, axis=AX.X):
    return lambda e: e.tensor_reduce(out=out, in_=in_, axis=axis, op=op)


def COPY(out, in_):
    return lambda e: e.tensor_copy(out=out, in_=in_)


def RECIP(out, in_):
    return lambda e: e.reciprocal(out=out, in_=in_)


def MEMSET(ap, v):
    return lambda e: e.memset(ap, v)


def barrier(P):
    toks = [(e, P.cnt[e]) for e in P.ENGS if P.cnt[e] > 0] + [(d, d.n) for d in P.dsems if d.n > 0]
    for e in P.ENGS:
        waits = []
        for src, n in toks:
            if src == e:
                continue
            if P.seen[e].get(src, 0) >= n:
                continue
            P.seen[e][src] = n
            waits.append((src, n))
        P.streams[e].append((waits, None, None))


D = 2048
EPS = 1e-6


def rstd_ops(P, ss, rstd, n, r_ss, r_rstd):
    P.op("dve", TS(rstd, ss, 1.0 / n, EPS, ALU.mult, ALU.add), reads=[r_ss], writes=[r_rstd])
    P.op("act", ACTV(rstd, rstd, AF.Sqrt), reads=[r_rstd], writes=[r_rstd])
    P.op("dve", RECIP(rstd, rstd), reads=[r_rstd], writes=[r_rstd])


def build_mod():
    P = Prog("mod")
    sT = P.dram_in("sT", [128, 16, 5], F32)
    W = P.dram_in("W", [2, 2048, 1536], F32)
    B = P.dram_in("B", [5, 2, 1536], F32)
    M = P.dram_out("M", [5, 2, 1536], F32)
    s_raw = P.sbuf([128, 16, 5], F32); s_act = P.sbuf([128, 16, 5], F32)
    wt = [P.sbuf([128, 16, 512], F32) for _ in range(2)]
    bt = P.sbuf([5, 2, 1536], F32); ot = P.sbuf([5, 2, 1536], F32)
    ps = [P.psum([128, 512], F32) for _ in range(2)]
    r_s = Res(); r_sa = Res(); r_w = [Res(), Res()]; r_b = Res(); r_o = Res(); r_ps = [Res(), Res()]
    d_s = P.dsem(); d_w = [P.dsem(), P.dsem()]; d_b = P.dsem(); d_o = P.dsem()
    P.dma("sp", s_raw[:], sT, d_s, writes=[r_s])
    P.dma("sp", bt[:], B, d_b, writes=[r_b])
    P.op("act", ACTV(s_act[:], s_raw[:], AF.Silu), reads=[r_s], writes=[r_sa])
    i = 0
    for l in range(2):
        for cc in range(3):
            sl = i % 2
            P.dma("sp", wt[sl][:], W[l, :, cc * 512:(cc + 1) * 512].rearrange("(kc p) n -> p kc n", p=128), d_w[sl], writes=[r_w[sl]])
            for kc in range(16):
                P.op("pe", MM(ps[sl][0:5, :], s_act[:, kc, :], wt[sl][:, kc, :], kc == 0, kc == 15), reads=[r_sa, r_w[sl]], writes=[r_ps[sl]])
            P.op("dve", TT(ot[:, l, cc * 512:(cc + 1) * 512], ps[sl][0:5, :], bt[:, l, cc * 512:(cc + 1) * 512], ALU.add), reads=[r_ps[sl], r_b], writes=[r_o])
            i += 1
    P.dma("sp", M, ot[:], d_o, reads=[r_o])
    return P.finish()


def build_comb(nt, with_moe, with_norm, ctx_tiles=0):
    P = Prog("comb")
    n = nt * 128
    xin = P.dram_in("xin", [n, D], F32)
    if with_moe:
        Y = P.dram_in("Y", [n, 2, D], BF16)
        gates = P.dram_in("gates", [n, 2], F32)
        g2r = P.dram_in("g2r", [128, D], F32)
        xout = P.dram_out("xout", [n, D], F32)
    if with_norm:
        ngr = P.dram_in("ngr", [128, D], F32)
        scr = P.dram_in("scr", [128, D], F32)
        shr = P.dram_in("shr", [128, D], F32)
        if ctx_tiles:
            cscr = P.dram_in("cscr", [128, D], F32)
            cshr = P.dram_in("cshr", [128, D], F32)
        hout = P.dram_out("hout", [n, D], BF16)
    xt = [P.sbuf([128, D], F32) for _ in range(2)]; r_x = [Res(), Res()]; d_x = [P.dsem(), P.dsem()]
    d_stx = [P.dsem(), P.dsem()]; d_sth = [P.dsem(), P.dsem()]
    if with_moe:
        yt = [P.sbuf([128, 2, D], BF16) for _ in range(2)]; r_y = [Res(), Res()]; d_y = [P.dsem(), P.dsem()]
        gt = [P.sbuf([128, 2], F32) for _ in range(2)]; r_g = [Res(), Res()]
        g2t = P.sbuf([128, D], F32); r_g2 = Res(); d_c = P.dsem()
        mt = P.sbuf([128, D], F32); r_m = Res()
        P.dma("sp", g2t[:], g2r, d_c, writes=[r_g2])
    if with_norm:
        d_c2 = P.dsem()
        ngt = P.sbuf([128, D], F32); tmp = P.sbuf([128, D], F32)
        Gt = P.sbuf([128, D], F32); SHt = P.sbuf([128, D], F32); r_G = Res(); r_SH = Res(); r_ng = Res(); r_tmp = Res()
        P.dma("sp", ngt[:], ngr, d_c2, writes=[r_ng])
        P.dma("sp", tmp[:], scr, d_c2, writes=[r_tmp])
        P.dma("sp", SHt[:], shr, d_c2, writes=[r_SH])
        P.op("dve", STT(Gt[:], tmp[:], 1.0, ngt[:], ALU.add, ALU.mult), reads=[r_tmp, r_ng], writes=[r_G])
        if ctx_tiles:
            cGt = P.sbuf([128, D], F32); cSHt = P.sbuf([128, D], F32); r_cG = Res(); r_cSH = Res()
            P.dma("sp", tmp[:], cscr, d_c2, writes=[r_tmp])
            P.dma("sp", cSHt[:], cshr, d_c2, writes=[r_cSH])
            P.op("dve", STT(cGt[:], tmp[:], 1.0, ngt[:], ALU.add, ALU.mult), reads=[r_tmp, r_ng], writes=[r_cG])
        junk = P.sbuf([128, D], BF16); r_junk = Res()
        ss = P.sbuf([128, 1], F32); rs = P.sbuf([128, 1], F32); r_ss = Res(); r_rs = Res()
        hn = P.sbuf([128, D], F32); r_hn = Res()
        hb = [P.sbuf([128, D], BF16) for _ in range(2)]; r_hb = [Res(), Res()]
    barrier(P)
    for t in range(nt):
        sl = t % 2
        rows = slice(t * 128, (t + 1) * 128)
        P.dma("sp", xt[sl][:], xin[rows, :], d_x[sl], writes=[r_x[sl]])
        if with_moe:
            P.dma("sp", yt[sl][:], Y[rows], d_y[sl], writes=[r_y[sl]])
            P.dma("sp", gt[sl][:], gates[rows, :], d_y[sl], writes=[r_g[sl]])
            P.op("dve", TS(mt[:], yt[sl][:, 0, :], gt[sl][:, 0:1], None, ALU.mult), reads=[r_y[sl], r_g[sl]], writes=[r_m])
            P.op("dve", STT(mt[:], yt[sl][:, 1, :], gt[sl][:, 1:2], mt[:], ALU.mult, ALU.add), reads=[r_y[sl], r_g[sl], r_m], writes=[r_m])
            P.op("dve", TT(mt[:], mt[:], g2t[:], ALU.mult), reads=[r_m, r_g2], writes=[r_m])
            P.op("dve", TT(xt[sl][:], xt[sl][:], mt[:], ALU.add), reads=[r_m, r_x[sl]], writes=[r_x[sl]])
            P.dma("sp", xout[rows, :], xt[sl][:], d_stx[sl], reads=[r_x[sl]])
        if with_norm:
            isctx = t >= nt - ctx_tiles
            P.op("act", ACTV(junk[:], xt[sl][:], AF.Square, accum_out=ss[:]), reads=[r_x[sl]], writes=[r_junk, r_ss])
            rstd_ops(P, ss[:], rs[:], D, r_ss, r_rs)
            P.op("dve", STT(hn[:], xt[sl][:], rs[:, 0:1], (cGt if isctx else Gt)[:], ALU.mult, ALU.mult), reads=[r_x[sl], r_rs, (r_cG if isctx else r_G)], writes=[r_hn])
            P.op("dve", TT(hb[sl][:], hn[:], (cSHt if isctx else SHt)[:], ALU.add), reads=[r_hn, (r_cSH if isctx else r_SH)], writes=[r_hb[sl]])
            P.dma("sp", hout[rows, :], hb[sl][:], d_sth[sl], reads=[r_hb[sl]])
    return P.finish()


def load_w_bf16(P, dst, src, nkc, ncols, stage, r_stage, d_stage, r_dst):
    for k0 in range(0, nkc, 4):
        P.dma("sp", stage[:, :, :ncols], src[k0 * 128:(k0 + 4) * 128, :].rearrange("(kc p) n -> p kc n", p=128), d_stage, writes=[r_stage])
        P.op("act", ACTV(dst[:, k0:k0 + 4, :ncols], stage[:, :, :ncols], AF.Copy), reads=[r_stage], writes=[r_dst])


def normrope(P, ps_ap, ncols, hd, gain, cos, sin, outb, r_ps, r_gain, r_tab, r_out, S):
    nh = ncols // hd
    hp = hd // 2
    sq, ss, rs, vn = S["sq"], S["ss"], S["rs"], S["vn"]
    ta, tb = S["ta"], S["tb"]
    P.op("act", ACTV(sq[:, :ncols], ps_ap, AF.Square), reads=[r_ps], writes=[S["r_sq"]])
    P.op("dve", TRED(ss[:, :nh], sq[:, :ncols].rearrange("p (h d) -> p h d", d=hd), ALU.add), reads=[S["r_sq"]], writes=[S["r_ss"]])
    rstd_ops(P, ss[:, :nh], rs[:, :nh], hd, S["r_ss"], S["r_rs"])
    P.op("dve", TT(vn[:, :ncols].rearrange("p (h d) -> p h d", d=hd), ps_ap.rearrange("p (h d) -> p h d", d=hd),
                   rs[:, :nh].unsqueeze(2).to_broadcast([128, nh, hd]), ALU.mult), reads=[r_ps, S["r_rs"]], writes=[S["r_vn"]])
    P.op("dve", TT(vn[:, :ncols], vn[:, :ncols], gain, ALU.mult), reads=[S["r_vn"], r_gain], writes=[S["r_vn"]])
    v4 = vn[:, :ncols].rearrange("p (h i two) -> p h i two", two=2, i=hp)
    o4 = outb.rearrange("p (h i two) -> p h i two", two=2, i=hp)
    ve, vo = v4[:, :, :, 0], v4[:, :, :, 1]
    cb = cos.unsqueeze(1).to_broadcast([128, nh, hp])
    sb = sin.unsqueeze(1).to_broadcast([128, nh, hp])
    n2 = nh * hp
    tav = ta[:, :n2].rearrange("p (h i) -> p h i", i=hp)
    tbv = tb[:, :n2].rearrange("p (h i) -> p h i", i=hp)
    P.op("dve", TT(tav, ve, cb, ALU.mult), reads=[S["r_vn"], r_tab], writes=[S["r_ta"]])
    P.op("dve", TT(tbv, vo, sb, ALU.mult), reads=[S["r_vn"], r_tab], writes=[S["r_tb"]])
    P.op("dve", TT(o4[:, :, :, 0], tav, tbv, ALU.subtract), reads=[S["r_ta"], S["r_tb"]], writes=[r_out])
    P.op("dve", TT(tav, ve, sb, ALU.mult), reads=[S["r_vn"], r_tab], writes=[S["r_ta"]])
    P.op("dve", TT(tbv, vo, cb, ALU.mult), reads=[S["r_vn"], r_tab], writes=[S["r_tb"]])
    P.op("dve", TT(o4[:, :, :, 1], tav, tbv, ALU.add), reads=[S["r_ta"], S["r_tb"]], writes=[r_out])


def build_qkv(nt):
    P = Prog("qkv")
    n = nt * 128
    hT = P.dram_in("hT", [D, n], BF16)
    w_in = P.dram_in("w_in", [D, 4608], F32)
    gq = P.dram_in("gq", [128, 512], F32)
    gk = P.dram_in("gk", [128, 512], F32)
    gsq = P.dram_in("gsq", [128, 512], F32)
    gsk = P.dram_in("gsk", [128, 512], F32)
    cosd = P.dram_in("cosd", [n, 32], F32); sind = P.dram_in("sind", [n, 32], F32)
    coss = P.dram_in("coss", [n, 64], F32); sins = P.dram_in("sins", [n, 64], F32)
    out = P.dram_out("qkv", [n, 4608], BF16)
    d_c = P.dsem(); d_sts = [P.dsem(), P.dsem()]
    gt = {}
    r_gain = Res()
    for nm, ap, scale in (("gq", gq, 64 ** -0.5), ("gk", gk, None), ("gsq", gsq, 128 ** -0.5), ("gsk", gsk, None)):
        t = P.sbuf([128, 512], F32)
        P.dma("sp", t[:], ap, d_c, writes=[r_gain])
        if scale is not None:
            P.op("dve", TS(t[:], t[:], scale, None, ALU.mult), reads=[r_gain], writes=[r_gain])
        gt[nm] = t
    cd = P.sbuf([128, nt, 32], F32); sd_ = P.sbuf([128, nt, 32], F32)
    cs = P.sbuf([128, nt, 64], F32); sn = P.sbuf([128, nt, 64], F32)
    r_tab = Res()
    for t_, ap in ((cd, cosd), (sd_, sind), (cs, coss), (sn, sins)):
        P.dma("sp", t_[:], ap.rearrange("(t p) i -> p t i", p=128), d_c, writes=[r_tab])
    S = dict(sq=P.sbuf([128, 512], F32), ss=P.sbuf([128, 8], F32), rs=P.sbuf([128, 8], F32), vn=P.sbuf([128, 512], F32),
             ta=P.sbuf([128, 256], F32), tb=P.sbuf([128, 256], F32),
             r_sq=Res(), r_ss=Res(), r_rs=Res(), r_vn=Res(), r_ta=Res(), r_tb=Res())
    wt = [P.sbuf([128, 16, 512], BF16) for _ in range(2)]; r_w = [Res(), Res()]; d_w = [P.dsem(), P.dsem()]
    ht = [P.sbuf([128, 16, 128], BF16) for _ in range(2)]; r_h = [Res(), Res()]; d_h = [P.dsem(), P.dsem()]
    ps = [P.psum([128, 512], F32) for _ in range(2)]; r_ps = [Res(), Res()]
    ob = [P.sbuf([128, 512], BF16) for _ in range(2)]; r_ob = [Res(), Res()]
    wst = P.sbuf([128, 4, 512], F32); r_wst = Res(); d_wst = P.dsem()
    barrier(P)
    it = 0
    for cg in range(9):
        ws = cg % 2
        load_w_bf16(P, wt[ws], w_in[:, cg * 512:(cg + 1) * 512], 16, 512, wst, r_wst, d_wst, r_w[ws])
        for t in range(nt):
            sl = it % 2
            it += 1
            P.dma("sp", ht[sl][:], hT[:, t * 128:(t + 1) * 128].rearrange("(kc p) n -> p kc n", p=128), d_h[sl], writes=[r_h[sl]])
            for kc in range(16):
                P.op("pe", MM(ps[sl][:], ht[sl][:, kc, :], wt[ws][:, kc, :], kc == 0, kc == 15), reads=[r_h[sl], r_w[ws]], writes=[r_ps[sl]])
            o = ob[sl]
            if cg in (0, 1):
                normrope(P, ps[sl][:], 512, 64, gt["gq"][:], cd[:, t, :], sd_[:, t, :], o[:], r_ps[sl], r_gain, r_tab, r_ob[sl], S)
            elif cg in (2, 3):
                normrope(P, ps[sl][:], 512, 64, gt["gk"][:], cd[:, t, :], sd_[:, t, :], o[:], r_ps[sl], r_gain, r_tab, r_ob[sl], S)
            elif cg in (4, 5):
                P.op("act", ACTV(o[:], ps[sl][:], AF.Copy), reads=[r_ps[sl]], writes=[r_ob[sl]])
            elif cg in (6, 7):
                normrope(P, ps[sl][:], 512, 128, gt["gsq"][:], cs[:, t, :], sn[:, t, :], o[:], r_ps[sl], r_gain, r_tab, r_ob[sl], S)
            else:
                normrope(P, ps[sl][:, 0:256], 256, 128, gt["gsk"][:, 0:256], cs[:, t, :], sn[:, t, :], o[:, 0:256], r_ps[sl], r_gain, r_tab, r_ob[sl], S)
                P.op("act", ACTV(o[:, 256:512], ps[sl][:, 256:512], AF.Copy), reads=[r_ps[sl]], writes=[r_ob[sl]])
            P.dma("sp", out[t * 128:(t + 1) * 128, cg * 512:(cg + 1) * 512], o[:], d_sts[sl], reads=[r_ob[sl]])
    return P.finish()


def rope_tables(pos, dim):
    pos = np.asarray(pos)
    row = (pos // 64).astype(np.float32); col = (pos % 64).astype(np.float32)
    nf = dim // 4
    inv = (np.float32(10000.0) ** (-np.arange(nf, dtype=np.float32) / np.float32(nf))).astype(np.float32)
    ang = np.concatenate([row[:, None] * inv, col[:, None] * inv], axis=-1).astype(np.float32)
    return np.cos(ang).astype(np.float32), np.sin(ang).astype(np.float32)


def build_att():
    P = Prog("att")
    NK = 4352; NC = 34
    QT = P.dram_in("QT", [8, 128, 2048], BF16)
    KT = P.dram_in("KT", [8, 128, NK], BF16)
    VA = P.dram_in("VA", [8, 128, NC, 129], BF16)
    SQT = P.dram_in("SQT", [2, 128, 4, 2048], BF16)
    SKT = P.dram_in("SKT", [2, 128, NK], BF16)
    SVA = P.dram_in("SVA", [2, 128, NC, 129], BF16)
    lam4 = P.dram_in("lam4", [128, 4, 64], F32)
    gsub = P.dram_in("gsub", [128, 128], F32)
    sink = P.dram_in("sink", [128, 8], F32)
    masks = P.dram_in("masks", [4, 128, 128], BF16)
    A = P.dram_out("A", [2048, 2048], BF16)
    d_c = P.dsem(); d_sta = [P.dsem() for _ in range(4)]
    l4 = P.sbuf([128, 4, 64], F32); r_l4 = Res()
    P.dma("sp", l4[:], lam4, d_c, writes=[r_l4])
    gs = P.sbuf([128, 128], F32); r_gs = Res()
    P.dma("sp", gs[:], gsub, d_c, writes=[r_gs])
    sk = P.sbuf([128, 8], F32); r_sk = Res()
    P.dma("sp", sk[:], sink, d_c, writes=[r_sk])
    mk = P.sbuf([128, 4, 128], BF16); r_mk = Res()
    P.dma("sp", mk[:], masks.rearrange("m k q -> k m q"), d_c, writes=[r_mk])
    barrier(P)
    lam_init = 0.8 - 0.6 * 1.0
    prod = P.sbuf([128, 2, 64], F32); lsum = P.sbuf([128, 2], F32); nlam = P.sbuf([128, 1], F32); r_lam = Res()
    P.op("dve", TT(prod[:, 0, :], l4[:, 0, :], l4[:, 1, :], ALU.mult), reads=[r_l4], writes=[r_lam])
    P.op("dve", TT(prod[:, 1, :], l4[:, 2, :], l4[:, 3, :], ALU.mult), reads=[r_l4, r_lam], writes=[r_lam])
    P.op("dve", TRED(lsum[:], prod[:], ALU.add), reads=[r_lam], writes=[r_lam])
    P.op("act", ACTV(lsum[:], lsum[:], AF.Exp), reads=[r_lam], writes=[r_lam])
    P.op("dve", TT(nlam[:], lsum[:, 1:2], lsum[:, 0:1], ALU.subtract), reads=[r_lam], writes=[r_lam])
    P.op("dve", TS(nlam[:], nlam[:], -lam_init, None, ALU.add), reads=[r_lam], writes=[r_lam])
    P.op("dve", TS(gs[:], gs[:], 1.0 - lam_init, None, ALU.mult), reads=[r_gs], writes=[r_gs])
    P.op("act", ACTV(sk[:], sk[:], AF.Exp), reads=[r_sk], writes=[r_sk])
    kt = [P.sbuf([128, NK], BF16) for _ in range(2)]; qt = [P.sbuf([128, 4, 2048], BF16) for _ in range(2)]
    va = [P.sbuf([128, NC, 129], BF16) for _ in range(2)]
    r_in = [Res(), Res()]; d_in = [P.dsem(), P.dsem()]
    pT = [P.sbuf([128, 512], BF16) for _ in range(3)]; r_pT = [Res() for _ in range(3)]
    psS = [P.psum([128, 512], F32) for _ in range(2)]; r_S = [Res(), Res()]
    psO = [P.psum([128, 129], F32) for _ in range(4)]; r_O = [Res() for _ in range(4)]
    o1s = P.sbuf([128, 4, 129], F32); r_o1s = [Res() for _ in range(4)]
    rz = P.sbuf([128, 2], F32); o1 = P.sbuf([128, 128], F32); o2 = P.sbuf([128, 128], F32); junk = P.sbuf([128, 128], F32)
    ss = P.sbuf([128, 1], F32); rs = P.sbuf([128, 1], F32)
    r_f = Res(); r_ss = Res(); r_rs = Res()
    ab = [P.sbuf([128, 128], BF16) for _ in range(4)]; r_ab = [Res() for _ in range(4)]
    ipt = 0; iab = 0; iS = 0
    nonlocal_state = {"iS": 0, "ipt": 0}
    for h in range(8):
        sl = h % 2
        P.dma("sp", kt[sl][:], KT[h], d_in[sl], writes=[r_in[sl]])
        P.dma("sp", qt[sl][:, 0, :], QT[h], d_in[sl], writes=[r_in[sl]])
        P.dma("sp", va[sl][:], VA[h], d_in[sl], writes=[r_in[sl]])
        for qb in range(4):
            for sub in range(2):
                def qk_exp(kc):
                    nonlocal_state["iS"] += 1; nonlocal_state["ipt"] += 1
                    s_ = nonlocal_state["iS"] % 2; p_ = nonlocal_state["ipt"] % 3
                    P.op("pe", MM(psS[s_][:], kt[sl][sub * 64:(sub + 1) * 64, kc * 128:(kc + 1) * 128],
                                  qt[sl][sub * 64:(sub + 1) * 64, 0, qb * 512:(qb + 1) * 512], True, True), reads=[r_in[sl]], writes=[r_S[s_]])
                    P.op("act", ACTV(pT[p_][:], psS[s_][:], AF.Exp), reads=[r_S[s_]], writes=[r_pT[p_]])
                    return p_
                nxt = qk_exp(0)
                for kc in range(NC):
                    p_ = nxt
                    if kc + 1 < NC:
                        nxt = qk_exp(kc + 1)
                    for j in range(4):
                        P.op("pe", MM(psO[j][:], pT[p_][:, j * 128:(j + 1) * 128], va[sl][:, kc, :], kc == 0, kc == NC - 1),
                             reads=[r_pT[p_], r_in[sl]], writes=[r_O[j]])
                if sub == 0:
                    for j in range(4):
                        P.op("act", ACTV(o1s[:, j, :], psO[j][:], AF.Copy), reads=[r_O[j]], writes=[r_o1s[j]])
            for j in range(4):
                O1 = o1s[:, j, :]; O2 = psO[j][:]
                rO = [r_o1s[j], r_O[j]]
                P.op("dve", RECIP(rz[:, 0:1], O1[:, 128:129]), reads=[rO[0]], writes=[r_f])
                P.op("dve", RECIP(rz[:, 1:2], O2[:, 128:129]), reads=[rO[1], r_f], writes=[r_f])
                P.op("dve", TT(rz[:, 1:2], rz[:, 1:2], nlam[:], ALU.mult), reads=[r_f, r_lam], writes=[r_f])
                P.op("dve", TS(o2[:], O2[:, 0:128], rz[:, 1:2], None, ALU.mult), reads=[rO[1], r_f], writes=[r_f])
                P.op("dve", STT(o1[:], O1[:, 0:128], rz[:, 0:1], o2[:], ALU.mult, ALU.add), reads=[rO[0], r_f], writes=[r_f])
                P.op("act", ACTV(junk[:], o1[:], AF.Square, accum_out=ss[:]), reads=[r_f], writes=[r_ss])
                rstd_ops(P, ss[:], rs[:], 128, r_ss, r_rs)
                a_ = iab % 4; iab += 1
                P.op("dve", STT(ab[a_][:], o1[:], rs[:, 0:1], gs[:], ALU.mult, ALU.mult), reads=[r_f, r_rs, r_gs], writes=[r_ab[a_]])
                tok0 = qb * 512 + j * 128
                P.dma("sp", A[tok0:tok0 + 128, h * 128:(h + 1) * 128], ab[a_][:], d_sta[a_], reads=[r_ab[a_]])
    for g in range(2):
        sl = g % 2
        P.dma("sp", kt[sl][:], SKT[g], d_in[sl], writes=[r_in[sl]])
        P.dma("sp", qt[sl][:], SQT[g], d_in[sl], writes=[r_in[sl]])
        P.dma("sp", va[sl][:], SVA[g], d_in[sl], writes=[r_in[sl]])
        for qi in range(16):
            chunks = [(0, None), (1, None)]
            chunks.append((2 + qi - 1, 0) if qi > 0 else (18, 2))
            chunks.append((2 + qi, None))
            chunks.append((2 + qi + 1, 1) if qi < 15 else (18, 3))
            def sqk_exp(ci):
                kc, m = chunks[ci]
                nonlocal_state["iS"] += 1; nonlocal_state["ipt"] += 1
                s_ = nonlocal_state["iS"] % 2; p_ = nonlocal_state["ipt"] % 3
                P.op("pe", MM(psS[s_][:].rearrange("p (h q) -> p h q", q=128), kt[sl][:, kc * 128:(kc + 1) * 128],
                              qt[sl][:, :, qi * 128:(qi + 1) * 128], True, True), reads=[r_in[sl]], writes=[r_S[s_]])
                P.op("act", ACTV(pT[p_][:], psS[s_][:], AF.Exp), reads=[r_S[s_]], writes=[r_pT[p_]])
                if m is not None:
                    P.op("dve", TT(pT[p_][:].rearrange("p (h q) -> p h q", q=128), pT[p_][:].rearrange("p (h q) -> p h q", q=128),
                                   mk[:, m, :].unsqueeze(1).to_broadcast([128, 4, 128]), ALU.mult), reads=[r_pT[p_], r_mk], writes=[r_pT[p_]])
                return p_
            nxt = sqk_exp(0)
            for ci, (kc, m) in enumerate(chunks):
                p_ = nxt
                if ci + 1 < 5:
                    nxt = sqk_exp(ci + 1)
                for hh in range(4):
                    P.op("pe", MM(psO[hh][:], pT[p_][:, hh * 128:(hh + 1) * 128], va[sl][:, kc, :], ci == 0, ci == 4),
                         reads=[r_pT[p_], r_in[sl]], writes=[r_O[hh]])
            for hh in range(4):
                O = psO[hh][:]
                hd_ = g * 4 + hh
                P.op("dve", TT(rz[:, 0:1], O[:, 128:129], sk[:, hd_:hd_ + 1], ALU.add), reads=[r_O[hh], r_sk], writes=[r_f])
                P.op("dve", RECIP(rz[:, 0:1], rz[:, 0:1]), reads=[r_f], writes=[r_f])
                a_ = iab % 4; iab += 1
                P.op("dve", TS(ab[a_][:], O[:, 0:128], rz[:, 0:1], None, ALU.mult), reads=[r_O[hh], r_f], writes=[r_ab[a_]])
                P.dma("sp", A[qi * 128:(qi + 1) * 128, 1024 + hd_ * 128:1024 + (hd_ + 1) * 128], ab[a_][:], d_sta[a_], reads=[r_ab[a_]])
    return P.finish()


def rep128(v):
    v = np.asarray(v)
    return np.ascontiguousarray(np.broadcast_to(v[None], (128,) + v.shape))


def att_inputs(qkv, d):
    tri_prev = (np.arange(128)[:, None] >= np.arange(128)[None, :])
    tri_next = (np.arange(128)[:, None] <= np.arange(128)[None, :])
    zeros = np.zeros((128, 128), bool)
    in_maps = []
    for c in range(8):
        b, hf = c // 2, c % 2
        own = qkv[c, :2048]; oth_c = 2 * b + (1 - hf)
        other = qkv[oth_c, :2048]
        if hf == 1:
            other = np.concatenate([other[1920:], other[:1920]], 0)
        ctxr = np.concatenate([qkv[2 * b, 2048:], qkv[2 * b + 1, 2048:]], 0)
        keys = np.concatenate([ctxr, own, other], 0)
        QT = np.ascontiguousarray(own[:, 0:1024].T.reshape(8, 128, 2048))
        KT = np.ascontiguousarray(keys[:, 1024:2048].T.reshape(8, 128, 4352))
        V = keys[:, 2048:3072].reshape(34, 128, 8, 128)
        VA = np.ones((8, 128, 34, 129), NPBF16); VA[:, :, :, :128] = V.transpose(2, 1, 0, 3)
        SQT = np.ascontiguousarray(own[:, 3072:4096].reshape(2048, 2, 4, 128).transpose(1, 3, 2, 0))
        SKT = np.ascontiguousarray(keys[:, 4096:4352].T.reshape(2, 128, 4352))
        SV = keys[:, 4352:4608].reshape(34, 128, 2, 128)
        SVA = np.ones((2, 128, 34, 129), NPBF16); SVA[:, :, :, :128] = SV.transpose(2, 1, 0, 3)
        masks = np.stack([tri_prev, tri_next, tri_prev if hf == 1 else zeros, tri_next if hf == 0 else zeros]).astype(NPBF16)
        lam4 = np.stack([d["diff_lq1"][0], d["diff_lk1"][0], d["diff_lq2"][0], d["diff_lk2"][0]])
        in_maps.append({"QT": QT, "KT": KT, "VA": VA, "SQT": SQT, "SKT": SKT, "SVA": SVA, "lam4": rep128(lam4),
                        "gsub": rep128(d["diff_sub_g"][0]), "sink": rep128(d["swa_sink"][0]), "masks": masks})
    return in_maps


def build_post(nt):
    P = Prog("post")
    n = nt * 128
    AT = P.dram_in("AT", [D, n], BF16)
    xin = P.dram_in("xin", [n, D], F32)
    w_out = P.dram_in("w_out", [D, D], F32)
    tabs = {k: P.dram_in(k, [128, D], F32) for k in ("g1r", "boutr", "n2gr", "sc2r", "sh2r")}
    x1o = P.dram_out("x1", [n, D], F32)
    h2bo = P.dram_out("h2b", [n, D], BF16)
    wg = P.dram_in("wg", [D, 36], F32)
    bgr = P.dram_in("bgr", [128, 36], F32)
    iot = P.dram_in("iota8", [128, 8], F32)
    identf = P.dram_in("identf", [128, 128], F32)
    ro = P.dram_out("route", [n, 4], F32)
    d_c = P.dsem(); d_stx = [P.dsem(), P.dsem()]; d_sth = [P.dsem(), P.dsem()]; d_str = [P.dsem(), P.dsem()]
    wgt = P.sbuf([128, 16, 36], F32); bg = P.sbuf([128, 36], F32); io = P.sbuf([128, 8], F32); idf = P.sbuf([128, 128], F32); r_c = Res()
    P.dma("sp", wgt[:], wg.rearrange("(kc p) n -> p kc n", p=128), d_c, writes=[r_c])
    P.dma("sp", bg[:], bgr, d_c, writes=[r_c]); P.dma("sp", io[:], iot, d_c, writes=[r_c]); P.dma("sp", idf[:], identf, d_c, writes=[r_c])
    pst = [P.psum([128, 512], F32) for _ in range(2)]; r_pst = [Res(), Res()]
    psl = P.psum([128, 512], F32); r_psl = Res()
    h2T = P.sbuf([128, 16, 128], F32); r_h2T = Res()
    S = {k: P.sbuf([128, w], F32) for k, w in (("l", 36), ("gmax", 1), ("e4", 4), ("s4", 1), ("oh4", 4), ("sel", 8), ("m1", 1), ("mk1", 8),
                                                ("sel2", 8), ("m2", 1), ("mk2", 8), ("t8", 8), ("dd", 1), ("gi", 1), ("i4", 4))}
    r_s = Res()
    outt = [P.sbuf([128, 4], F32) for _ in range(2)]; r_out = [Res(), Res()]
    T = {}; r_T = Res()
    for k, ap in tabs.items():
        T[k] = P.sbuf([128, D], F32)
        P.dma("sp", T[k][:], ap, d_c, writes=[r_T])
    barrier(P)
    P.op("dve", STT(T["sc2r"][:], T["sc2r"][:], 1.0, T["n2gr"][:], ALU.add, ALU.mult), reads=[r_T], writes=[r_T])
    wt = P.sbuf([128, 16, D], BF16); r_w = Res()
    wst = P.sbuf([128, 4, 512], F32); r_wst = Res(); d_wst = P.dsem()
    for cgp in range(4):
        for k0 in range(0, 16, 4):
            P.dma("sp", wst[:], w_out[k0 * 128:(k0 + 4) * 128, cgp * 512:(cgp + 1) * 512].rearrange("(kc p) n -> p kc n", p=128), d_wst, writes=[r_wst])
            P.op("act", ACTV(wt[:, k0:k0 + 4, cgp * 512:(cgp + 1) * 512], wst[:], AF.Copy), reads=[r_wst], writes=[r_w])
    at = [P.sbuf([128, 16, 128], BF16) for _ in range(2)]; r_a = [Res(), Res()]; d_a = [P.dsem(), P.dsem()]
    xt = [P.sbuf([128, D], F32) for _ in range(2)]; r_x = [Res(), Res()]; d_xl = [P.dsem(), P.dsem()]
    ps = [P.psum([128, 512], F32) for _ in range(4)]; r_ps = [Res() for _ in range(4)]
    yt = P.sbuf([128, D], F32); r_y = Res()
    junk = P.sbuf([128, D], BF16); r_junk = Res()
    ss = P.sbuf([128, 1], F32); rs = P.sbuf([128, 1], F32); r_ss = Res(); r_rs = Res()
    h2 = [P.sbuf([128, D], F32) for _ in range(2)]; r_h2 = [Res(), Res()]
    h2b = [P.sbuf([128, D], BF16) for _ in range(2)]; r_h2b = [Res(), Res()]
    for t in range(nt):
        sl = t % 2
        rows = slice(t * 128, (t + 1) * 128)
        P.dma("sp", at[sl][:], AT[:, rows].rearrange("(kc p) n -> p kc n", p=128), d_a[sl], writes=[r_a[sl]])
        P.dma("sp", xt[sl][:], xin[rows, :], d_xl[sl], writes=[r_x[sl]])
        for cgp in range(4):
            for kc in range(16):
                P.op("pe", MM(ps[cgp][:], at[sl][:, kc, :], wt[:, kc, cgp * 512:(cgp + 1) * 512], kc == 0, kc == 15), reads=[r_a[sl], r_w], writes=[r_ps[cgp]])
            cs_ = slice(cgp * 512, (cgp + 1) * 512)
            P.op("dve", TT(yt[:, cs_], ps[cgp][:], T["boutr"][:, cs_], ALU.add), reads=[r_ps[cgp], r_T], writes=[r_y])
        P.op("dve", TT(yt[:], yt[:], T["g1r"][:], ALU.mult), reads=[r_y, r_T], writes=[r_y])
        P.op("dve", TT(xt[sl][:], xt[sl][:], yt[:], ALU.add), reads=[r_y, r_x[sl]], writes=[r_x[sl]])
        P.dma("sp", x1o[rows, :], xt[sl][:], d_stx[sl], reads=[r_x[sl]])
        P.op("act", ACTV(junk[:], xt[sl][:], AF.Square, accum_out=ss[:]), reads=[r_x[sl]], writes=[r_junk, r_ss])
        rstd_ops(P, ss[:], rs[:], D, r_ss, r_rs)
        P.op("dve", STT(h2[sl][:], xt[sl][:], rs[:, 0:1], T["sc2r"][:], ALU.mult, ALU.mult), reads=[r_x[sl], r_rs, r_T], writes=[r_h2[sl]])
        P.op("dve", TT(h2[sl][:], h2[sl][:], T["sh2r"][:], ALU.add), reads=[r_h2[sl], r_T], writes=[r_h2[sl]])
        P.op("act", ACTV(h2b[sl][:], h2[sl][:], AF.Copy), reads=[r_h2[sl]], writes=[r_h2b[sl]])
        P.dma("sp", h2bo[rows, :], h2b[sl][:], d_sth[sl], reads=[r_h2b[sl]])
        for q4 in range(4):
            b_ = q4 % 2
            for i4 in range(4):
                kc = q4 * 4 + i4
                P.op("pe", TR(pst[b_][:, i4 * 128:(i4 + 1) * 128], h2[sl][:, kc * 128:(kc + 1) * 128], idf[:]), reads=[r_h2[sl], r_c], writes=[r_pst[b_]])
            P.op("act", ACTV(h2T[:, q4 * 4:(q4 + 1) * 4, :], pst[b_][:].rearrange("p (a b) -> p a b", b=128), AF.Copy), reads=[r_pst[b_]], writes=[r_h2T])
        for kc in range(16):
            P.op("pe", MM(psl[:, 0:36], h2T[:, kc, :], wgt[:, kc, :], kc == 0, kc == 15), reads=[r_h2T, r_c], writes=[r_psl])
        def dv(fn):
            P.op("dve", fn, reads=[r_s, r_c], writes=[r_s])
        P.op("dve", TT(S["l"][:], psl[:, 0:36], bg[:], ALU.add), reads=[r_psl, r_c, r_s], writes=[r_s])
        dv(TRED(S["gmax"][:], S["l"][:, 0:4], ALU.max))
        dv(TS(S["e4"][:], S["l"][:, 0:4], S["gmax"][:, 0:1], None, ALU.subtract))
        P.op("act", ACTV(S["e4"][:], S["e4"][:], AF.Exp), reads=[r_s], writes=[r_s])
        dv(TRED(S["s4"][:], S["e4"][:], ALU.add))
        dv(RECIP(S["s4"][:], S["s4"][:]))
        dv(TS(S["oh4"][:], S["l"][:, 0:4], S["gmax"][:, 0:1], None, ALU.is_equal))
        dv(TS(S["sel"][:], S["l"][:, 4:12], S["oh4"][:, 0:1], None, ALU.mult))
        for g in range(1, 4):
            dv(STT(S["sel"][:], S["l"][:, 4 + 8 * g:12 + 8 * g], S["oh4"][:, g:g + 1], S["sel"][:], ALU.mult, ALU.add))
        dv(TT(S["i4"][:], S["oh4"][:], io[:, 0:4], ALU.mult))
        dv(TRED(S["gi"][:], S["i4"][:], ALU.add))
        dv(TRED(S["m1"][:], S["sel"][:], ALU.max))
        dv(TS(S["mk1"][:], S["sel"][:], S["m1"][:, 0:1], None, ALU.is_equal))
        dv(STT(S["sel2"][:], S["mk1"][:], -1e30, S["sel"][:], ALU.mult, ALU.add))
        dv(TRED(S["m2"][:], S["sel2"][:], ALU.max))
        dv(TS(S["mk2"][:], S["sel2"][:], S["m2"][:, 0:1], None, ALU.is_equal))
        o = outt[sl]
        P.op("dve", TT(S["t8"][:], S["mk1"][:], io[:], ALU.mult), reads=[r_s, r_c], writes=[r_s])
        P.op("dve", TRED(o[:, 0:1], S["t8"][:], ALU.add), reads=[r_s], writes=[r_out[sl]])
        P.op("dve", TT(S["t8"][:], S["mk2"][:], io[:], ALU.mult), reads=[r_s, r_c, r_out[sl]], writes=[r_s])
        P.op("dve", TRED(o[:, 1:2], S["t8"][:], ALU.add), reads=[r_s], writes=[r_out[sl]])
        for k in range(2):
            P.op("dve", STT(o[:, k:k + 1], S["gi"][:], 8.0, o[:, k:k + 1], ALU.mult, ALU.add), reads=[r_s, r_out[sl]], writes=[r_out[sl]])
        dv(TT(S["dd"][:], S["m2"][:], S["m1"][:], ALU.subtract))
        P.op("act", ACTV(S["dd"][:], S["dd"][:], AF.Exp), reads=[r_s], writes=[r_s])
        dv(TS(S["dd"][:], S["dd"][:], 1.0, None, ALU.add))
        dv(RECIP(S["dd"][:], S["dd"][:]))
        P.op("dve", TT(o[:, 2:3], S["dd"][:], S["s4"][:], ALU.mult), reads=[r_s, r_out[sl]], writes=[r_out[sl]])
        P.op("dve", TT(o[:, 3:4], S["s4"][:], o[:, 2:3], ALU.subtract), reads=[r_s, r_out[sl]], writes=[r_out[sl]])
        P.dma("sp", ro[rows, :], o[:], d_str[sl], reads=[r_out[sl]])
    return P.finish()


def build_route(nt):
    P = Prog("route")
    n = nt * 128
    h2i = P.dram_in("h2", [n, D], F32)
    wgT = P.dram_in("wgT", [36, D], F32)
    bgr = P.dram_in("bgr", [128, 36], F32)
    iot = P.dram_in("iota8", [128, 8], F32)
    ro = P.dram_out("route", [n, 4], F32)
    d_c = P.dsem(); d_st = P.dsem(); d_h = P.dsem()
    hall = P.sbuf([128, nt, D], F32); r_h = Res()
    for t in range(nt):
        P.dma("sp", hall[:, t, :], h2i[t * 128:(t + 1) * 128, :], d_h, writes=[r_h])
    bg = P.sbuf([128, 36], F32); io = P.sbuf([128, 8], F32); r_c = Res()
    P.dma("sp", bg[:], bgr, d_c, writes=[r_c]); P.dma("sp", io[:], iot, d_c, writes=[r_c])
    wc = [P.sbuf([128, D], F32) for _ in range(2)]; r_wc = [Res(), Res()]; d_wc = [P.dsem(), P.dsem()]
    lg = P.sbuf([128, nt, 36], F32); r_lg = Res()
    junk = P.sbuf([128, D], F32); r_junk = Res()
    for j in range(36):
        sl = j % 2
        P.dma("sp", wc[sl][:], wgT[j:j + 1, :].partition_broadcast(128), d_wc[sl], writes=[r_wc[sl]])
        for t in range(nt):
            P.op("dve", TT(junk[:], hall[:, t, :], wc[sl][:], ALU.mult), reads=[r_h, r_wc[sl]], writes=[r_junk])
            P.op("dve", TRED(lg[:, t, j:j + 1], junk[:], ALU.add), reads=[r_junk], writes=[r_lg])
    S = {k: P.sbuf([128, w], F32) for k, w in (("l", 36), ("gmax", 1), ("e4", 4), ("s4", 1), ("oh4", 4), ("sel", 8), ("m1", 1), ("mk1", 8),
                                                ("sel2", 8), ("m2", 1), ("mk2", 8), ("t8", 8), ("dd", 1), ("out", 4), ("gi", 1), ("i4", 4))}
    r_s = Res()
    def dv(fn):
        P.op("dve", fn, reads=[r_s, r_lg, r_c], writes=[r_s])
    outt = [P.sbuf([128, 4], F32) for _ in range(2)]; r_out = [Res(), Res()]
    for t in range(nt):
        sl = t % 2
        dv(TT(S["l"][:], lg[:, t, :], bg[:], ALU.add))
        dv(TRED(S["gmax"][:], S["l"][:, 0:4], ALU.max))
        dv(TS(S["e4"][:], S["l"][:, 0:4], S["gmax"][:, 0:1], None, ALU.subtract))
        P.op("act", ACTV(S["e4"][:], S["e4"][:], AF.Exp), reads=[r_s], writes=[r_s])
        dv(TRED(S["s4"][:], S["e4"][:], ALU.add))
        dv(RECIP(S["s4"][:], S["s4"][:]))
        dv(TS(S["oh4"][:], S["l"][:, 0:4], S["gmax"][:, 0:1], None, ALU.is_equal))
        dv(TS(S["sel"][:], S["l"][:, 4:12], S["oh4"][:, 0:1], None, ALU.mult))
        for g in range(1, 4):
            dv(STT(S["sel"][:], S["l"][:, 4 + 8 * g:12 + 8 * g], S["oh4"][:, g:g + 1], S["sel"][:], ALU.mult, ALU.add))
        dv(TT(S["i4"][:], S["oh4"][:], io[:, 0:4], ALU.mult))
        dv(TRED(S["gi"][:], S["i4"][:], ALU.add))
        dv(TRED(S["m1"][:], S["sel"][:], ALU.max))
        dv(TS(S["mk1"][:], S["sel"][:], S["m1"][:, 0:1], None, ALU.is_equal))
        dv(STT(S["sel2"][:], S["mk1"][:], -1e30, S["sel"][:], ALU.mult, ALU.add))
        dv(TRED(S["m2"][:], S["sel2"][:], ALU.max))
        dv(TS(S["mk2"][:], S["sel2"][:], S["m2"][:, 0:1], None, ALU.is_equal))
        o = outt[sl]
        P.op("dve", TT(S["t8"][:], S["mk1"][:], io[:], ALU.mult), reads=[r_s, r_c], writes=[r_s])
        P.op("dve", TRED(o[:, 0:1], S["t8"][:], ALU.add), reads=[r_s], writes=[r_out[sl]])
        P.op("dve", TT(S["t8"][:], S["mk2"][:], io[:], ALU.mult), reads=[r_s, r_c, r_out[sl]], writes=[r_s])
        P.op("dve", TRED(o[:, 1:2], S["t8"][:], ALU.add), reads=[r_s], writes=[r_out[sl]])
        for k in range(2):
            P.op("dve", STT(o[:, k:k + 1], S["gi"][:], 8.0, o[:, k:k + 1], ALU.mult, ALU.add), reads=[r_s, r_out[sl]], writes=[r_out[sl]])
        dv(TT(S["dd"][:], S["m2"][:], S["m1"][:], ALU.subtract))
        P.op("act", ACTV(S["dd"][:], S["dd"][:], AF.Exp), reads=[r_s], writes=[r_s])
        dv(TS(S["dd"][:], S["dd"][:], 1.0, None, ALU.add))
        dv(RECIP(S["dd"][:], S["dd"][:]))
        P.op("dve", TT(o[:, 2:3], S["dd"][:], S["s4"][:], ALU.mult), reads=[r_s, r_out[sl]], writes=[r_out[sl]])
        P.op("dve", TT(o[:, 3:4], S["s4"][:], o[:, 2:3], ALU.subtract), reads=[r_s, r_out[sl]], writes=[r_out[sl]])
        P.dma("sp", ro[t * 128:(t + 1) * 128, :], o[:], d_st, reads=[r_out[sl]])
    return P.finish()


def build_moe(caps):
    P = Prog("moe")
    tot = sum(caps)
    offs = [sum(caps[:j]) for j in range(4)]
    xsT = P.dram_in("xsT", [D, tot], BF16)
    wg = P.dram_in("wg", [4, D, 1024], F32)
    wu = P.dram_in("wu", [4, D, 1024], F32)
    wd = P.dram_in("wd", [4, 1024, D], F32)
    yT = P.dram_out("yT", [D, tot], BF16)
    d_sty = [P.dsem(), P.dsem()]
    wgt = P.sbuf([128, 16, 1024], BF16); wut = P.sbuf([128, 16, 1024], BF16); wdt = P.sbuf([128, 8, D], BF16)
    r_wg = Res(); r_wu = Res(); r_wd = Res()
    wst = P.sbuf([128, 4, 512], F32); r_wst = Res(); d_wst = P.dsem()
    xt = [P.sbuf([128, 16, 512], BF16) for _ in range(2)]; r_x = [Res(), Res()]; d_x = [P.dsem(), P.dsem()]
    psg = [P.psum([128, 512], F32) for _ in range(2)]; psu = [P.psum([128, 512], F32) for _ in range(2)]; psy = [P.psum([128, 512], F32) for _ in range(2)]
    r_pg = [Res(), Res()]; r_pu = [Res(), Res()]; r_py = [Res(), Res()]
    sg = P.sbuf([128, 512], F32); r_sg = Res()
    ht = P.sbuf([128, 8, 512], BF16); r_ht = Res()
    yo = [P.sbuf([128, 16, 512], BF16) for _ in range(2)]; r_yo = [Res(), Res()]
    ib = 0
    for e_ in range(4):
        for dst, src, nk, r_ in ((wgt, wg[e_], 16, r_wg), (wut, wu[e_], 16, r_wu)):
            for c0 in range(0, 1024, 512):
                for k0 in range(0, nk, 4):
                    P.dma("sp", wst[:], src[k0 * 128:(k0 + 4) * 128, c0:c0 + 512].rearrange("(kc p) n -> p kc n", p=128), d_wst, writes=[r_wst])
                    P.op("act", ACTV(dst[:, k0:k0 + 4, c0:c0 + 512], wst[:], AF.Copy), reads=[r_wst], writes=[r_])
        for c0 in range(0, D, 512):
            for k0 in range(0, 8, 4):
                P.dma("sp", wst[:], wd[e_][k0 * 128:(k0 + 4) * 128, c0:c0 + 512].rearrange("(kc p) n -> p kc n", p=128), d_wst, writes=[r_wst])
                P.op("act", ACTV(wdt[:, k0:k0 + 4, c0:c0 + 512], wst[:], AF.Copy), reads=[r_wst], writes=[r_wd])
        for c0 in range(0, caps[e_], 512):
            nc_ = min(512, caps[e_] - c0)
            sl = ib % 2; ib += 1
            cols = slice(offs[e_] + c0, offs[e_] + c0 + nc_)
            P.dma("sp", xt[sl][:, :, :nc_], xsT[:, cols].rearrange("(kc p) n -> p kc n", p=128), d_x[sl], writes=[r_x[sl]])
            for fc in range(8):
                b_ = fc % 2
                for kc in range(16):
                    P.op("pe", MM(psg[b_][:, :nc_], wgt[:, kc, fc * 128:(fc + 1) * 128], xt[sl][:, kc, :nc_], kc == 0, kc == 15), reads=[r_wg, r_x[sl]], writes=[r_pg[b_]])
                for kc in range(16):
                    P.op("pe", MM(psu[b_][:, :nc_], wut[:, kc, fc * 128:(fc + 1) * 128], xt[sl][:, kc, :nc_], kc == 0, kc == 15), reads=[r_wu, r_x[sl]], writes=[r_pu[b_]])
                P.op("act", ACTV(sg[:, :nc_], psg[b_][:, :nc_], AF.Silu), reads=[r_pg[b_]], writes=[r_sg])
                P.op("dve", TT(ht[:, fc, :nc_], sg[:, :nc_], psu[b_][:, :nc_], ALU.mult), reads=[r_sg, r_pu[b_]], writes=[r_ht])
            for dc in range(16):
                b_ = dc % 2
                for fc in range(8):
                    P.op("pe", MM(psy[b_][:, :nc_], wdt[:, fc, dc * 128:(dc + 1) * 128], ht[:, fc, :nc_], fc == 0, fc == 7), reads=[r_wd, r_ht], writes=[r_py[b_]])
                P.op("act", ACTV(yo[sl][:, dc, :nc_], psy[b_][:, :nc_], AF.Copy), reads=[r_py[b_]], writes=[r_yo[sl]])
            P.dma("sp", yT[:, cols].rearrange("(kc p) n -> p kc n", p=128), yo[sl][:, :, :nc_], d_sty[sl], reads=[r_yo[sl]])
    return P.finish()


_PROGS = {}


def _prog(key, builder):
    if key not in _PROGS:
        _PROGS[key] = builder()
    return _PROGS[key]


def _tok_shard(a):
    return [a[c // 2, (c % 2) * 2048:(c % 2 + 1) * 2048] for c in range(8)]


def _moe_layer(l, route, h2b, inp, x1_sh, M, next_norm):
    eid = np.rint(route[:, 0:2]).astype(np.int64)
    h2b_all = np.concatenate(h2b, 0)
    flat_e = eid.reshape(-1)
    order = np.argsort(flat_e, kind="stable")
    counts = np.bincount(flat_e, minlength=32)
    starts = np.cumsum(counts) - counts
    slot = np.empty(flat_e.shape[0], np.int64)
    slot[order] = np.arange(flat_e.shape[0]) - starts[flat_e[order]]
    rank = np.argsort(-counts, kind="stable")
    place = np.empty((32, 2), np.int64)
    for r, e_ in enumerate(rank):
        place[e_] = (r % 8, r // 8)
    caps = tuple(int(max(128, -(-counts[rank[8 * j:8 * j + 8]].max() // 128) * 128)) for j in range(4))
    offs = [sum(caps[:j]) for j in range(4)]
    tot = sum(caps)
    exp_of = np.empty((8, 4), np.int64)
    for e_ in range(32):
        exp_of[place[e_, 0], place[e_, 1]] = e_
    in_maps = []
    for c in range(8):
        xsT = np.zeros((D, tot), NPBF16)
        for j in range(4):
            e_ = exp_of[c, j]
            a = order[starts[e_]:starts[e_] + counts[e_]]
            xsT[:, offs[j]:offs[j] + counts[e_]] = h2b_all[a // 2].T
        es = exp_of[c]
        in_maps.append({"xsT": xsT, "wg": np.ascontiguousarray(inp["moe_w_gate"][l][es]), "wu": np.ascontiguousarray(inp["moe_w_up"][l][es]),
                        "wd": np.ascontiguousarray(inp["moe_w_down"][l][es])})
    res = run(_prog(("moe", caps), lambda: build_moe(caps)), in_maps)
    yT = np.stack([r["yT"] for r in res])
    yrows = np.ascontiguousarray(yT.transpose(0, 2, 1))
    a_core = place[flat_e, 0]; a_pos = np.asarray(offs)[place[flat_e, 1]] + slot
    Y = yrows[a_core, a_pos].reshape(16384, 2, D)
    in_maps = []
    for c in range(8):
        b = c // 2
        rows = slice(c * 2048, (c + 1) * 2048)
        m = {"xin": x1_sh[c], "Y": Y[rows], "gates": np.ascontiguousarray(route[rows, 2:4]), "g2r": rep128(M[b, l, 5 * D:6 * D])}
        if next_norm:
            m.update({"ngr": rep128(inp["norm1_g"][l + 1]), "scr": rep128(M[b, l + 1, D:2 * D]), "shr": rep128(M[b, l + 1, 0:D])})
        in_maps.append(m)
    return run(_prog(("comb", next_norm), lambda: build_comb(16, True, next_norm)), in_maps)


def _post(AT_sh, x_sh, w_out, b_out, inp, M, l):
    in_maps = []
    for c in range(8):
        b = c // 2
        in_maps.append({"AT": AT_sh[c], "xin": x_sh[c], "w_out": w_out, "g1r": rep128(M[b, l, 2 * D:3 * D]), "boutr": rep128(b_out),
                        "n2gr": rep128(inp["norm2_g"][l]), "sc2r": rep128(M[b, l, 4 * D:5 * D]), "sh2r": rep128(M[b, l, 3 * D:4 * D]),
                        "wg": np.ascontiguousarray(np.concatenate([inp["moe_wg1"][l], inp["moe_wg2"][l]], 1)),
                        "bgr": rep128(np.concatenate([inp["moe_bg1"][l], inp["moe_bg2"][l]])),
                        "iota8": rep128(np.arange(8, dtype=np.float32)), "identf": np.eye(128, dtype=np.float32)})
    res = run(_prog("post", lambda: build_post(16)), in_maps)
    return [r["x1"] for r in res], np.concatenate([r["route"] for r in res], 0), [r["h2b"] for r in res]


def kernel(**inp):
    inp = {k: np.asarray(v) for k, v in inp.items()}
    x = inp["x"]; ctx = inp["ctx"]
    cc = np.concatenate([inp["c"], inp["c_ctx"][None]], 0)
    sT = np.ascontiguousarray(cc.T.reshape(16, 128, 5).transpose(1, 0, 2))
    in_maps = [{"sT": sT, "W": np.ascontiguousarray(inp["ada_w"][:, :, i * 1536:(i + 1) * 1536]),
                "B": np.ascontiguousarray(np.broadcast_to(inp["ada_b"][:, i * 1536:(i + 1) * 1536][None], (5, 2, 1536)))} for i in range(8)]
    res = run(_prog("mod", build_mod), in_maps)
    M = np.concatenate([r["M"] for r in res], axis=2)
    x_sh = _tok_shard(x)
    in_maps = []
    for c in range(8):
        b, hf = c // 2, c % 2
        in_maps.append({"xin": np.concatenate([x_sh[c], ctx[b, hf * 128:(hf + 1) * 128]], 0), "ngr": rep128(inp["norm1_g"][0]),
                        "scr": rep128(M[b, 0, D:2 * D]), "shr": rep128(M[b, 0, 0:D]), "cscr": rep128(M[4, 0, D:2 * D]), "cshr": rep128(M[4, 0, 0:D])})
    res = run(_prog("norm0", lambda: build_comb(17, False, True, ctx_tiles=1)), in_maps)
    h0 = [r["hout"] for r in res]
    in_maps = []
    for c in range(8):
        hf = c % 2
        pos = np.arange(hf * 2048, (hf + 1) * 2048)
        cd, sd = rope_tables(pos, 64); cs, sn = rope_tables(pos, 128)
        one = lambda a: np.concatenate([a, np.ones((128, a.shape[1]), np.float32)], 0)
        zero = lambda a: np.concatenate([a, np.zeros((128, a.shape[1]), np.float32)], 0)
        in_maps.append({"hT": np.ascontiguousarray(h0[c].T), "w_in": inp["attn_w_in"][0],
                        "gq": rep128(np.tile(inp["diff_q_g"][0], 8)), "gk": rep128(np.tile(inp["diff_k_g"][0], 8)),
                        "gsq": rep128(np.tile(inp["swa_q_g"][0], 4)), "gsk": rep128(np.tile(inp["swa_k_g"][0], 4)),
                        "cosd": one(cd), "sind": zero(sd), "coss": one(cs), "sins": zero(sn)})
    res = run(_prog("qkv", lambda: build_qkv(17)), in_maps)
    qkv = np.stack([r["qkv"] for r in res])
    res = run(_prog("att", build_att), att_inputs(qkv, inp))
    AT_sh = [np.ascontiguousarray(r["A"].T) for r in res]
    x1_sh, h2, h2b = _post(AT_sh, x_sh, inp["attn_w_out"][0], np.zeros(D, np.float32), inp, M, 0)
    res = _moe_layer(0, h2, h2b, inp, x1_sh, M, True)
    x2_sh = [r["xout"] for r in res]; h1 = [r["hout"] for r in res]
    if _DEBUG.get("stop") == "l0":
        return np.stack(x2_sh).reshape(4, 4096, D)
    ycvT_sh = hyena_layer(h1, inp)[0]
    x3_sh, h2, h2b = _post(ycvT_sh, x2_sh, inp["hy_w_out"][0], inp["hy_b_out"][0], inp, M, 1)
    res = _moe_layer(1, h2, h2b, inp, x3_sh, M, False)
    return np.stack([r["xout"] for r in res]).reshape(4, 4096, D).astype(np.float32)


_DEBUG = {}


def sin_reduced(P, v, tmp, ki, negpi_unused, r_v, r_tmp):
    import math
    P.op("dve", TS(v, v, 1.0 / (2 * math.pi), 16.5, ALU.mult, ALU.add), reads=[r_v], writes=[r_v])
    P.op("dve", COPY(ki, v), reads=[r_v], writes=[r_tmp])
    P.op("dve", COPY(tmp, ki), reads=[r_tmp], writes=[r_tmp])
    P.op("dve", TT(v, v, tmp, ALU.subtract), reads=[r_v, r_tmp], writes=[r_v])
    P.op("dve", TS(v, v, 2 * math.pi, -math.pi, ALU.mult, ALU.add), reads=[r_v], writes=[r_v])
    P.op("dve", TS(tmp, v, -math.pi, None, ALU.is_lt), reads=[r_v, r_tmp], writes=[r_tmp])
    P.op("dve", STT(v, tmp, 2 * math.pi, v, ALU.mult, ALU.add), reads=[r_v, r_tmp], writes=[r_v])
    P.op("dve", TS(v, v, 3.1415925, -3.1415925, ALU.min, ALU.max), reads=[r_v], writes=[r_v])
    P.op("act", ACTV(v, v, AF.Sin), reads=[r_v], writes=[r_v])


def build_hyin(nt=16):
    P = Prog("hyin")
    n = nt * 128
    NCOL = 6144
    hTp = P.dram_in("hTp", [D, n + 2], BF16)
    onesr = P.dram_in("onesr", [1, n + 2], BF16)
    W = P.dram_in("W", [D, NCOL], F32)
    b_in = P.dram_in("b_in", [1, NCOL], F32)
    cw = P.dram_in("cw", [128, 3, NCOL], F32)
    cb = P.dram_in("cb", [128, NCOL], F32)
    zo = P.dram_out("z", [n, NCOL], BF16)
    d_c = P.dsem(); d_stz = [P.dsem(), P.dsem()]; d_g = P.dsem()
    ow = P.sbuf([1, n + 2], BF16); r_ow = Res()
    P.dma("sp", ow[:], onesr, d_c, writes=[r_ow])
    wt = [P.sbuf([128, 16, 512], BF16) for _ in range(3)]; r_wt = Res()
    wst = P.sbuf([128, 4, 512], F32); r_wst = Res(); d_wst = P.dsem()
    cwt = P.sbuf([128, 3, 512], F32); cbt = P.sbuf([128, 512], F32); bint = P.sbuf([1, 512], F32); r_g = Res()
    brf = P.sbuf([1, 3, 512], F32); brow = P.sbuf([1, 3, 512], BF16); r_br = Res()
    ht = [P.sbuf([128, 16, 130], BF16) for _ in range(2)]; r_h = [Res(), Res()]; d_h = [P.dsem(), P.dsem()]
    ps = [P.psum([128, 512], F32) for _ in range(2)]; r_ps = [Res(), Res()]
    ob = [P.sbuf([128, 512], BF16) for _ in range(2)]; r_ob = [Res(), Res()]
    it = 0
    for cg in range(NCOL // 512):
        cs_ = slice(cg * 512, (cg + 1) * 512)
        P.dma("sp", cwt[:], cw[:, :, cs_], d_g, writes=[r_g])
        P.dma("sp", cbt[:], cb[:, cs_], d_g, writes=[r_g])
        P.dma("sp", bint[:], b_in[:, cs_], d_g, writes=[r_g])
        for k0 in range(0, 16, 4):
            P.dma("sp", wst[:], W[k0 * 128:(k0 + 4) * 128, cs_].rearrange("(kc p) n -> p kc n", p=128), d_wst, writes=[r_wst])
            for j in range(3):
                P.op("dve", TT(wt[j][:, k0:k0 + 4, :], wst[:], cwt[:, j, :].unsqueeze(1).to_broadcast([128, 4, 512]), ALU.mult), reads=[r_wst, r_g], writes=[r_wt])
        P.op("dve", TT(brf[:], cwt[0:1, :, :], bint[:].unsqueeze(1).to_broadcast([1, 3, 512]), ALU.mult), reads=[r_g], writes=[r_br])
        P.op("dve", COPY(brow[:], brf[:]), reads=[r_br], writes=[r_br])
        for t in range(nt):
            sl = it % 2; it += 1
            P.dma("sp", ht[sl][:], hTp[:, t * 128:t * 128 + 130].rearrange("(kc p) n -> p kc n", p=128), d_h[sl], writes=[r_h[sl]])
            first = True
            for j in range(3):
                for kc in range(16):
                    P.op("pe", MM(ps[sl][:], ht[sl][:, kc, j:j + 128], wt[j][:, kc, :], first, False), reads=[r_h[sl], r_wt], writes=[r_ps[sl]])
                    first = False
                P.op("pe", MM(ps[sl][:], ow[0:1, t * 128 + j:t * 128 + j + 128], brow[0:1, j, :], False, j == 2), reads=[r_ow, r_br], writes=[r_ps[sl]])
            P.op("dve", TT(ob[sl][:], ps[sl][:], cbt[:], ALU.add), reads=[r_ps[sl], r_g], writes=[r_ob[sl]])
            P.dma("sp", zo[t * 128:(t + 1) * 128, cs_], ob[sl][:], d_stz[sl], reads=[r_ob[sl]])
    return P.finish()


def build_filt():
    P = Prog("filt")
    zT = P.dram_in("zT", [33, 8192], F32)
    w1 = P.dram_in("w1", [33, 64], F32); w2 = P.dram_in("w2", [64, 64], F32)
    cols = P.dram_in("cols", [64, 4], F32)
    w3s = P.dram_in("w3s", [64, 2, 2, 256], F32)
    text = P.dram_in("text", [128, 8192], F32)
    nad = P.dram_in("nad", [128, 2], F32)
    biasc = P.dram_in("biasc", [128, 2, 2], F32)
    kl = P.dram_out("kl", [2, 256, 8192], BF16)
    d_c = P.dsem(); d_stk = [P.dsem(), P.dsem()]
    zt = P.sbuf([33, 8192], F32); w1t = P.sbuf([33, 64], F32); w2t = P.sbuf([64, 64], F32); ct = P.sbuf([64, 4], F32)
    w3t = P.sbuf([64, 2, 2, 256], F32); tx = P.sbuf([128, 8192], F32); nd = P.sbuf([128, 2], F32); bc = P.sbuf([128, 2, 2], F32)
    r_c = Res()
    for t_, ap in ((zt, zT), (w1t, w1), (w2t, w2), (ct, cols), (w3t, w3s), (tx, text), (nd, nad), (bc, biasc)):
        P.dma("sp", t_[:], ap, d_c, writes=[r_c])
    barrier(P)
    ps1 = P.psum([128, 512], F32); ps2 = P.psum([128, 512], F32); r_p1 = Res(); r_p2 = Res()
    ps3 = [P.psum([128, 512], F32) for _ in range(2)]; r_p3 = [Res(), Res()]
    a1 = P.sbuf([64, 512], F32); a2 = P.sbuf([64, 512], F32); r_a1 = Res(); r_a2 = Res()
    tmp = P.sbuf([64, 512], F32); ki = P.sbuf([64, 512], I32); r_tmp = Res()
    dec = P.sbuf([128, 512], F32); r_dec = Res()
    kf = P.sbuf([128, 512], F32); r_kf = Res()
    kb = [P.sbuf([128, 512], BF16) for _ in range(2)]; r_kb = [Res(), Res()]
    i3 = 0
    for blk in range(16):
        dr = 1 if blk < 8 else 0
        cs_ = slice(blk * 512, (blk + 1) * 512)
        P.op("pe", MM(ps1[0:64, :], w1t[:], zt[:, cs_], True, True), reads=[r_c], writes=[r_p1])
        P.op("dve", TS(a1[:], ps1[0:64, :], ct[:, 0:1], ct[:, 1:2], ALU.add, ALU.mult), reads=[r_p1, r_c], writes=[r_a1])
        sin_reduced(P, a1[:], tmp[:], ki[:], None, r_a1, r_tmp)
        P.op("pe", MM(ps2[0:64, :], w2t[:], a1[:], True, True), reads=[r_c, r_a1], writes=[r_p2])
        P.op("dve", TS(a2[:], ps2[0:64, :], ct[:, 2:3], ct[:, 3:4], ALU.add, ALU.mult), reads=[r_p2, r_c], writes=[r_a2])
        sin_reduced(P, a2[:], tmp[:], ki[:], None, r_a2, r_tmp)
        for c2 in range(2):
            P.op("act", ACTV(dec[:], tx[:, cs_], AF.Exp, scale=nd[:, c2:c2 + 1]), reads=[r_c], writes=[r_dec])
            for o in range(2):
                b_ = i3 % 2; i3 += 1
                P.op("pe", MM(ps3[b_][:], w3t[:, o, dr, c2 * 128:(c2 + 1) * 128], a2[:], True, True), reads=[r_c, r_a2], writes=[r_p3[b_]])
                P.op("dve", TT(kf[:], ps3[b_][:], dec[:], ALU.mult), reads=[r_p3[b_], r_dec], writes=[r_kf])
                if blk == 8:
                    P.op("dve", TT(kf[:, 0:1], kf[:, 0:1], bc[:, o, c2:c2 + 1], ALU.add), reads=[r_kf, r_c], writes=[r_kf])
                P.op("act", ACTV(kb[b_][:], kf[:], AF.Copy), reads=[r_kf], writes=[r_kb[b_]])
                P.dma("sp", kl[o, c2 * 128:(c2 + 1) * 128, cs_], kb[b_][:], d_stk[b_], reads=[r_kb[b_]])
    return P.finish()


def build_conv():
    P = Prog("conv")
    Zv = P.dram_in("Zv", [128, 256, 128], BF16)
    Zx1 = P.dram_in("Zx1", [128, 256, 128], BF16)
    Zx2 = P.dram_in("Zx2", [128, 256, 128], BF16)
    kl1 = P.dram_in("kl1", [256, 8192], BF16)
    rl2 = P.dram_in("rl2", [256, 8192], BF16)
    yo = P.dram_out("ycv", [128, 256, 128], BF16)
    d_st = P.dsem()
    G = 64
    vt = P.sbuf([128, G, 128], BF16); x1t = P.sbuf([128, G, 128], BF16); x2t = P.sbuf([128, G, 128], BF16)
    r_in = Res(); d_in = P.dsem()
    ot = P.sbuf([128, G, 128], BF16); r_ot = Res()
    s1 = [P.sbuf([128, 8064], BF16) for _ in range(2)]; s2 = [P.sbuf([128, 8064], BF16) for _ in range(2)]
    r_s1 = [Res(), Res()]; r_s2 = [Res(), Res()]; d_s1 = [P.dsem(), P.dsem()]; d_s2 = [P.dsem(), P.dsem()]
    pa = [P.psum([128, 512], F32) for _ in range(2)]; pb = [P.psum([128, 512], F32) for _ in range(2)]
    r_pa = [Res(), Res()]; r_pb = [Res(), Res()]
    y1 = [P.sbuf([128, 128], BF16) for _ in range(2)]; r_y1 = [Res(), Res()]
    dseq = [0]
    for a in range(1, 32):
        dseq += [a, -a]
    for g in range(256 // G):
        P.dma("sp", vt[:], Zv[:, g * G:(g + 1) * G, :], d_in, writes=[r_in])
        P.dma("sp", x1t[:], Zx1[:, g * G:(g + 1) * G, :], d_in, writes=[r_in])
        P.dma("sp", x2t[:], Zx2[:, g * G:(g + 1) * G, :], d_in, writes=[r_in])
        for c in range(G):
            ch = g * G + c
            sl = ch % 2
            P.dma("sp", s1[sl][:], bass.AP(tensor=kl1.tensor, offset=ch * 8192 + 1, ap=[[1, 128], [1, 8064]]), d_s1[sl], writes=[r_s1[sl]])
            P.dma("sp", s2[sl][:], bass.AP(tensor=rl2.tensor, offset=ch * 8192, ap=[[1, 128], [1, 8064]]), d_s2[sl], writes=[r_s2[sl]])
            for n_, d_ in enumerate(dseq):
                T0 = max(0, d_); T1 = min(32, 32 + d_)
                P.op("pe", MM(pa[sl][:, 4 * T0:4 * T1], s1[sl][:, (d_ + 31) * 128:(d_ + 32) * 128], vt[:, c, 4 * (T0 - d_):4 * (T1 - d_)], n_ == 0, n_ == 62),
                     reads=[r_s1[sl], r_in], writes=[r_pa[sl]])
            P.op("dve", TT(y1[sl][:], pa[sl][:, 0:128], x1t[:, c, :], ALU.mult), reads=[r_pa[sl], r_in], writes=[r_y1[sl]])
            for n_, d_ in enumerate(dseq):
                T0 = max(0, d_); T1 = min(32, 32 + d_)
                P.op("pe", MM(pb[sl][:, 4 * T0:4 * T1], s2[sl][:, (31 - d_) * 128:(32 - d_) * 128], y1[sl][:, 4 * (T0 - d_):4 * (T1 - d_)], n_ == 0, n_ == 62),
                     reads=[r_s2[sl], r_y1[sl]], writes=[r_pb[sl]])
            P.op("dve", TT(ot[:, c, :], pb[sl][:, 0:128], x2t[:, c, :], ALU.mult), reads=[r_pb[sl], r_in], writes=[r_ot])
        P.dma("sp", yo[:, g * G:(g + 1) * G, :], ot[:], d_st, reads=[r_ot])
    return P.finish()


def _filter_consts():
    n = 4096
    idx = np.arange(8192)
    a = np.minimum(np.abs(idx - 4096), n - 1)
    t = np.linspace(0.0, 1.0, n, dtype=np.float32)
    w = (np.float32(2.0 * np.pi) * np.arange(n, dtype=np.float32) / np.float32(n)).astype(np.float32)
    f = np.linspace(1e-4, 15.0, 16, dtype=np.float32)
    wf = (w[:, None] * f[None, :]).astype(np.float32)
    z = np.concatenate([t[:, None], np.cos(wf), -np.sin(wf)], axis=-1).astype(np.float32)
    zT = np.ascontiguousarray(z[a].T)
    text = rep128(t[a])
    max_decay = np.log(1e-2) / 0.3; min_decay = np.log(1e-2) / 1.5
    deltas = np.abs(np.linspace(min_decay, max_decay, 2048, dtype=np.float32)).astype(np.float32)
    return zT, text, deltas


def hyena_layer(h1, inp):
    cw = rep128(inp["hy_conv_w"][0]); cb = rep128(inp["hy_conv_b"][0])
    in_maps = []
    for c in range(8):
        hf = c % 2
        hTp = np.zeros((D, 2050), NPBF16); ones = np.zeros((1, 2050), NPBF16)
        hTp[:, 1:2049] = h1[c].T; ones[0, 1:2049] = 1
        if hf == 1:
            hTp[:, 0] = h1[c - 1][-1]; ones[0, 0] = 1
        else:
            hTp[:, 2049] = h1[c + 1][0]; ones[0, 2049] = 1
        in_maps.append({"hTp": hTp, "onesr": ones, "W": inp["hy_w_in"][0], "b_in": inp["hy_b_in"][0][None], "cw": cw, "cb": cb})
    res = run(_prog("hyin", build_hyin), in_maps)
    z = np.stack([r["z"] for r in res]).reshape(4, 4096, 6144)
    zT, text, deltas = _filter_consts()
    w3 = inp["flt_w3"][0].reshape(64, 2, 2, 2048)
    colsv = np.stack([inp["flt_b1"][0], inp["flt_f1"][0], inp["flt_b2"][0], inp["flt_f2"][0]], 1).astype(np.float32)
    in_maps = []
    for c in range(8):
        chs = slice(256 * c, 256 * c + 256)
        nad = np.ascontiguousarray((-deltas[chs]).reshape(2, 128).T)
        biasc = np.ascontiguousarray(inp["hy_bias"][0][:, chs].reshape(2, 2, 128).transpose(2, 0, 1))
        in_maps.append({"zT": zT, "w1": inp["flt_w1"][0], "w2": inp["flt_w2"][0], "cols": colsv, "w3s": np.ascontiguousarray(w3[:, :, :, chs]),
                        "text": text, "nad": nad, "biasc": biasc})
    res = run(_prog("filt", build_filt), in_maps)
    kls = [r["kl"] for r in res]
    def lay(a, rev):
        v = a.reshape(4, 32, 128, 256).transpose(2, 3, 1, 0).reshape(128, 256, 128)
        return np.ascontiguousarray(v[::-1] if rev else v)
    in_maps = []
    for c in range(8):
        chs = slice(256 * c, 256 * c + 256)
        in_maps.append({"Zv": lay(z[:, :, chs], True), "Zx1": lay(z[:, :, 2048 + 256 * c:2048 + 256 * c + 256], False),
                        "Zx2": lay(z[:, :, 4096 + 256 * c:4096 + 256 * c + 256], True),
                        "kl1": kls[c][0], "rl2": np.ascontiguousarray(kls[c][1][:, ::-1])})
    res = run(_prog("conv", build_conv), in_maps)
    ys = []
    for c in range(8):
        y = res[c]["ycv"][::-1].reshape(128, 256, 32, 4).transpose(3, 2, 0, 1).reshape(4, 4096, 256)
        ys.append(y)
    ycat = np.concatenate(ys, axis=2)
    return [np.ascontiguousarray(ycat[c // 2, (c % 2) * 2048:(c % 2 + 1) * 2048].T) for c in range(8)], z, kls
```

```python
import contextlib
import numpy as np
import ml_dtypes
import concourse.bass as bass
import concourse.mybir as mybir
from concourse.bass_utils import run_bass_kernel_spmd

F32 = mybir.dt.float32
BF16 = mybir.dt.bfloat16
I32 = mybir.dt.int32
ALU = mybir.AluOpType
AF = mybir.ActivationFunctionType
AX = mybir.AxisListType
NPBF16 = ml_dtypes.bfloat16


SAME_ENGINE_SYNC = True


class Res:
    __slots__ = ("w", "r")

    def __init__(self):
        self.w = None
        self.r = {}


class DmaSem:
    def __init__(self, sem):
        self.sem = sem
        self.n = 0


class Prog:
    ENGS = ("pe", "act", "dve", "pool", "sp")

    def __init__(self, name="k"):
        self.nc = bass.Bass("TRN2", target_bir_lowering=False)
        self.es = contextlib.ExitStack()
        self.streams = {e: [] for e in self.ENGS}
        self.cnt = {e: 0 for e in self.ENGS}
        self.seen = {e: {} for e in self.ENGS}
        self.esem = {e: self.es.enter_context(self.nc.semaphore("sem_" + e)) for e in self.ENGS}
        self.dsems = []
        self.uid = 0

    def sbuf(self, shape, dtype, name=None):
        self.uid += 1
        return self.es.enter_context(self.nc.sbuf_tensor(name or f"sb{self.uid}", list(shape), dtype))

    def psum(self, shape, dtype, name=None):
        self.uid += 1
        return self.es.enter_context(self.nc.psum_tensor(name or f"ps{self.uid}", list(shape), dtype))

    def dsem(self):
        self.uid += 1
        d = DmaSem(self.es.enter_context(self.nc.semaphore(f"dsem{self.uid}")))
        self.dsems.append(d)
        return d

    def dram_in(self, name, shape, dtype):
        return self.nc.dram_tensor(name, list(shape), dtype, kind="ExternalInput").ap()

    def dram_out(self, name, shape, dtype):
        return self.nc.dram_tensor(name, list(shape), dtype, kind="ExternalOutput").ap()

    def dram_tmp(self, name, shape, dtype):
        return self.nc.dram_tensor(name, list(shape), dtype).ap()

    def op(self, eng, fn, reads=(), writes=(), dsem=None):
        deps = []
        for r in reads:
            if r.w is not None:
                deps.append(r.w)
        for w in writes:
            if w.w is not None:
                deps.append(w.w)
            deps.extend(w.r.items())
        waits = []
        seen = self.seen[eng]
        for src, n in deps:
            if src == eng and (eng == "pe" or not SAME_ENGINE_SYNC):
                continue
            if seen.get(src, 0) >= n:
                continue
            seen[src] = n
            waits.append((src, n))
        if dsem is None:
            self.cnt[eng] += 1
            tok = (eng, self.cnt[eng])
        else:
            dsem.n += 1
            tok = (dsem, dsem.n)
        self.streams[eng].append((waits, fn, tok))
        for r in reads:
            if r.r.get(tok[0], 0) < tok[1]:
                r.r[tok[0]] = tok[1]
        for w in writes:
            w.w = tok
            w.r = {}
        return tok

    def dma(self, eng, out, in_, dsem, reads=(), writes=(), **kw):
        return self.op(eng, lambda e: e.dma_start(out=out, in_=in_, **kw), reads, writes, dsem=dsem)

    def finish(self):
        fin = []
        for d in self.dsems:
            if d.n:
                fin.append((d, d.n))
        self.streams["sp"].append((fin, None, None))
        nc = self.nc
        with nc.Block() as block:
            def emit(e, name):
                for waits, fn, tok in self.streams[name]:
                    for src, n in waits:
                        if isinstance(src, DmaSem):
                            e.wait_ge(src.sem, 16 * n)
                        else:
                            e.wait_ge(self.esem[src], n)
                    if fn is None:
                        continue
                    ins = fn(e)
                    if isinstance(tok[0], DmaSem):
                        ins.then_inc(tok[0].sem, 16)
                    else:
                        ins.then_inc(self.esem[name], 1)

            @block.tensor
            def _(e):
                emit(e, "pe")

            @block.scalar
            def _(e):
                emit(e, "act")

            @block.vector
            def _(e):
                emit(e, "dve")

            @block.gpsimd
            def _(e):
                emit(e, "pool")

            @block.sync
            def _(e):
                emit(e, "sp")
        self.es.close()
        return nc


_TRACE = {"on": False, "log": []}


def run(prog_nc, in_maps):
    if _TRACE["on"]:
        res = run_bass_kernel_spmd(prog_nc, in_maps, core_ids=list(range(8)), trace=True)
        _TRACE["log"].append(res.exec_time_ns)
        print("EXEC_TIME_NS", res.exec_time_ns, flush=True)
        return res.results
    res = run_bass_kernel_spmd(prog_nc, in_maps, core_ids=list(range(8)))
    return res.results


def MM(out, lhsT, rhs, start, stop):
    return lambda e: e.matmul(out, lhsT=lhsT, rhs=rhs, start=start, stop=stop)


def TR(out, in_, ident):
    return lambda e: e.transpose(out, in_, ident)


def ACTV(out, in_, func, **kw):
    return lambda e: e.activation(out=out, in_=in_, func=func, **kw)


def TT(out, in0, in1, op):
    return lambda e: e.tensor_tensor(out=out, in0=in0, in1=in1, op=op)


def TS(out, in0, s1, s2, op0, op1=None):
    if op1 is None:
        return lambda e: e.tensor_scalar(out=out, in0=in0, scalar1=s1, scalar2=None, op0=op0)
    return lambda e: e.tensor_scalar(out=out, in0=in0, scalar1=s1, scalar2=s2, op0=op0, op1=op1)


def STT(out, in0, scalar, in1, op0, op1):
    return lambda e: e.scalar_tensor_tensor(out=out, in0=in0, scalar=scalar, in1=in1, op0=op0, op1=op1)


def TRED(out, in_, op, axis=AX.X):
    return lambda e: e.tensor_reduce(out=out, in_=in_, axis=axis, op=op)


def COPY(out, in_):
    return lambda e: e.tensor_copy(out=out, in_=in_)


def RECIP(out, in_):
    return lambda e: e.reciprocal(out=out, in_=in_)


def MEMSET(ap, v):
    return lambda e: e.memset(ap, v)


def barrier(P):
    toks = [(e, P.cnt[e]) for e in P.ENGS if P.cnt[e] > 0] + [(d, d.n) for d in P.dsems if d.n > 0]
    for e in P.ENGS:
        waits = []
        for src, n in toks:
            if src == e:
                continue
            if P.seen[e].get(src, 0) >= n:
                continue
            P.seen[e][src] = n
            waits.append((src, n))
        P.streams[e].append((waits, None, None))


D = 2048
EPS = 1e-6


def rstd_ops(P, ss, rstd, n, r_ss, r_rstd):
    P.op("dve", TS(rstd, ss, 1.0 / n, EPS, ALU.mult, ALU.add), reads=[r_ss], writes=[r_rstd])
    P.op("act", ACTV(rstd, rstd, AF.Sqrt), reads=[r_rstd], writes=[r_rstd])
    P.op("dve", RECIP(rstd, rstd), reads=[r_rstd], writes=[r_rstd])


def build_mod():
    P = Prog("mod")
    sT = P.dram_in("sT", [128, 16, 5], F32)
    W = P.dram_in("W", [2, 2048, 1536], F32)
    B = P.dram_in("B", [5, 2, 1536], F32)
    M = P.dram_out("M", [5, 2, 1536], F32)
    s_raw = P.sbuf([128, 16, 5], F32); s_act = P.sbuf([128, 16, 5], F32)
    wt = [P.sbuf([128, 16, 512], F32) for _ in range(2)]
    bt = P.sbuf([5, 2, 1536], F32); ot = P.sbuf([5, 2, 1536], F32)
    ps = [P.psum([128, 512], F32) for _ in range(2)]
    r_s = Res(); r_sa = Res(); r_w = [Res(), Res()]; r_b = Res(); r_o = Res(); r_ps = [Res(), Res()]
    d_s = P.dsem(); d_w = [P.dsem(), P.dsem()]; d_b = P.dsem(); d_o = P.dsem()
    P.dma("sp", s_raw[:], sT, d_s, writes=[r_s])
    P.dma("sp", bt[:], B, d_b, writes=[r_b])
    P.op("act", ACTV(s_act[:], s_raw[:], AF.Silu), reads=[r_s], writes=[r_sa])
    i = 0
    for l in range(2):
        for cc in range(3):
            sl = i % 2
            P.dma("sp", wt[sl][:], W[l, :, cc * 512:(cc + 1) * 512].rearrange("(kc p) n -> p kc n", p=128), d_w[sl], writes=[r_w[sl]])
            for kc in range(16):
                P.op("pe", MM(ps[sl][0:5, :], s_act[:, kc, :], wt[sl][:, kc, :], kc == 0, kc == 15), reads=[r_sa, r_w[sl]], writes=[r_ps[sl]])
            P.op("dve", TT(ot[:, l, cc * 512:(cc + 1) * 512], ps[sl][0:5, :], bt[:, l, cc * 512:(cc + 1) * 512], ALU.add), reads=[r_ps[sl], r_b], writes=[r_o])
            i += 1
    P.dma("sp", M, ot[:], d_o, reads=[r_o])
    return P.finish()


def build_comb(nt, with_moe, with_norm, ctx_tiles=0):
    P = Prog("comb")
    n = nt * 128
    xin = P.dram_in("xin", [n, D], F32)
    if with_moe:
        Y = P.dram_in("Y", [n, 2, D], BF16)
        gates = P.dram_in("gates", [n, 2], F32)
        g2r = P.dram_in("g2r", [128, D], F32)
        xout = P.dram_out("xout", [n, D], F32)
    if with_norm:
        ngr = P.dram_in("ngr", [128, D], F32)
        scr = P.dram_in("scr", [128, D], F32)
        shr = P.dram_in("shr", [128, D], F32)
        if ctx_tiles:
            cscr = P.dram_in("cscr", [128, D], F32)
            cshr = P.dram_in("cshr", [128, D], F32)
        hout = P.dram_out("hout", [n, D], BF16)
    xt = [P.sbuf([128, D], F32) for _ in range(2)]; r_x = [Res(), Res()]; d_x = [P.dsem(), P.dsem()]
    d_stx = [P.dsem(), P.dsem()]; d_sth = [P.dsem(), P.dsem()]
    if with_moe:
        yt = [P.sbuf([128, 2, D], BF16) for _ in range(2)]; r_y = [Res(), Res()]; d_y = [P.dsem(), P.dsem()]
        gt = [P.sbuf([128, 2], F32) for _ in range(2)]; r_g = [Res(), Res()]
        g2t = P.sbuf([128, D], F32); r_g2 = Res(); d_c = P.dsem()
        mt = P.sbuf([128, D], F32); r_m = Res()
        P.dma("sp", g2t[:], g2r, d_c, writes=[r_g2])
    if with_norm:
        d_c2 = P.dsem()
        ngt = P.sbuf([128, D], F32); tmp = P.sbuf([128, D], F32)
        Gt = P.sbuf([128, D], F32); SHt = P.sbuf([128, D], F32); r_G = Res(); r_SH = Res(); r_ng = Res(); r_tmp = Res()
        P.dma("sp", ngt[:], ngr, d_c2, writes=[r_ng])
        P.dma("sp", tmp[:], scr, d_c2, writes=[r_tmp])
        P.dma("sp", SHt[:], shr, d_c2, writes=[r_SH])
        P.op("dve", STT(Gt[:], tmp[:], 1.0, ngt[:], ALU.add, ALU.mult), reads=[r_tmp, r_ng], writes=[r_G])
        if ctx_tiles:
            cGt = P.sbuf([128, D], F32); cSHt = P.sbuf([128, D], F32); r_cG = Res(); r_cSH = Res()
            P.dma("sp", tmp[:], cscr, d_c2, writes=[r_tmp])
            P.dma("sp", cSHt[:], cshr, d_c2, writes=[r_cSH])
            P.op("dve", STT(cGt[:], tmp[:], 1.0, ngt[:], ALU.add, ALU.mult), reads=[r_tmp, r_ng], writes=[r_cG])
        junk = P.sbuf([128, D], BF16); r_junk = Res()
        ss = P.sbuf([128, 1], F32); rs = P.sbuf([128, 1], F32); r_ss = Res(); r_rs = Res()
        hn = P.sbuf([128, D], F32); r_hn = Res()
        hb = [P.sbuf([128, D], BF16) for _ in range(2)]; r_hb = [Res(), Res()]
    barrier(P)
    for t in range(nt):
        sl = t % 2
        rows = slice(t * 128, (t + 1) * 128)
        P.dma("sp", xt[sl][:], xin[rows, :], d_x[sl], writes=[r_x[sl]])
        if with_moe:
            P.dma("sp", yt[sl][:], Y[rows], d_y[sl], writes=[r_y[sl]])
            P.dma("sp", gt[sl][:], gates[rows, :], d_y[sl], writes=[r_g[sl]])
            P.op("dve", TS(mt[:], yt[sl][:, 0, :], gt[sl][:, 0:1], None, ALU.mult), reads=[r_y[sl], r_g[sl]], writes=[r_m])
            P.op("dve", STT(mt[:], yt[sl][:, 1, :], gt[sl][:, 1:2], mt[:], ALU.mult, ALU.add), reads=[r_y[sl], r_g[sl], r_m], writes=[r_m])
            P.op("dve", TT(mt[:], mt[:], g2t[:], ALU.mult), reads=[r_m, r_g2], writes=[r_m])
            P.op("dve", TT(xt[sl][:], xt[sl][:], mt[:], ALU.add), reads=[r_m, r_x[sl]], writes=[r_x[sl]])
            P.dma("sp", xout[rows, :], xt[sl][:], d_stx[sl], reads=[r_x[sl]])
        if with_norm:
            isctx = t >= nt - ctx_tiles
            P.op("act", ACTV(junk[:], xt[sl][:], AF.Square, accum_out=ss[:]), reads=[r_x[sl]], writes=[r_junk, r_ss])
            rstd_ops(P, ss[:], rs[:], D, r_ss, r_rs)
            P.op("dve", STT(hn[:], xt[sl][:], rs[:, 0:1], (cGt if isctx else Gt)[:], ALU.mult, ALU.mult), reads=[r_x[sl], r_rs, (r_cG if isctx else r_G)], writes=[r_hn])
            P.op("dve", TT(hb[sl][:], hn[:], (cSHt if isctx else SHt)[:], ALU.add), reads=[r_hn, (r_cSH if isctx else r_SH)], writes=[r_hb[sl]])
            P.dma("sp", hout[rows, :], hb[sl][:], d_sth[sl], reads=[r_hb[sl]])
    return P.finish()


class WStage:
    def __init__(self, P):
        self.buf = [P.sbuf([128, 4, 512], F32) for _ in range(2)]
        self.res = [Res(), Res()]
        self.sem = [P.dsem(), P.dsem()]
        self.i = 0


def load_w_bf16(P, dst, src, nkc, ncols, stage, r_stage, d_stage, r_dst, dst_c0=0):
    ws = stage if isinstance(stage, WStage) else None
    for k0 in range(0, nkc, 4):
        if ws is None:
            sb, rs, ds = stage, r_stage, d_stage
            eng = "act"
        else:
            j = ws.i % 2; ws.i += 1
            sb, rs, ds = ws.buf[j], ws.res[j], ws.sem[j]
            eng = "act" if j == 0 else "dve"
        P.dma("sp", sb[:, :, :ncols], src[k0 * 128:(k0 + 4) * 128, :].rearrange("(kc p) n -> p kc n", p=128), ds, writes=[rs])
        if eng == "act":
            P.op("act", ACTV(dst[:, k0:k0 + 4, dst_c0:dst_c0 + ncols], sb[:, :, :ncols], AF.Copy), reads=[rs], writes=[r_dst])
        else:
            P.op("dve", COPY(dst[:, k0:k0 + 4, dst_c0:dst_c0 + ncols], sb[:, :, :ncols]), reads=[rs], writes=[r_dst])


def normrope(P, ps_ap, ncols, hd, gain, cos, sin, outb, r_ps, r_gain, r_tab, r_out, S):
    nh = ncols // hd
    hp = hd // 2
    sq, ss, rs, vn = S["sq"], S["ss"], S["rs"], S["vn"]
    ta, tb = S["ta"], S["tb"]
    P.op("act", ACTV(sq[:, :ncols], ps_ap, AF.Square), reads=[r_ps], writes=[S["r_sq"]])
    P.op("dve", TRED(ss[:, :nh], sq[:, :ncols].rearrange("p (h d) -> p h d", d=hd), ALU.add), reads=[S["r_sq"]], writes=[S["r_ss"]])
    rstd_ops(P, ss[:, :nh], rs[:, :nh], hd, S["r_ss"], S["r_rs"])
    P.op("dve", TT(vn[:, :ncols].rearrange("p (h d) -> p h d", d=hd), ps_ap.rearrange("p (h d) -> p h d", d=hd),
                   rs[:, :nh].unsqueeze(2).to_broadcast([128, nh, hd]), ALU.mult), reads=[r_ps, S["r_rs"]], writes=[S["r_vn"]])
    P.op("dve", TT(vn[:, :ncols], vn[:, :ncols], gain, ALU.mult), reads=[S["r_vn"], r_gain], writes=[S["r_vn"]])
    v4 = vn[:, :ncols].rearrange("p (h i two) -> p h i two", two=2, i=hp)
    o4 = outb.rearrange("p (h i two) -> p h i two", two=2, i=hp)
    ve, vo = v4[:, :, :, 0], v4[:, :, :, 1]
    cb = cos.unsqueeze(1).to_broadcast([128, nh, hp])
    sb = sin.unsqueeze(1).to_broadcast([128, nh, hp])
    n2 = nh * hp
    tav = ta[:, :n2].rearrange("p (h i) -> p h i", i=hp)
    tbv = tb[:, :n2].rearrange("p (h i) -> p h i", i=hp)
    P.op("dve", TT(tav, ve, cb, ALU.mult), reads=[S["r_vn"], r_tab], writes=[S["r_ta"]])
    P.op("dve", TT(tbv, vo, sb, ALU.mult), reads=[S["r_vn"], r_tab], writes=[S["r_tb"]])
    P.op("dve", TT(o4[:, :, :, 0], tav, tbv, ALU.subtract), reads=[S["r_ta"], S["r_tb"]], writes=[r_out])
    P.op("dve", TT(tav, ve, sb, ALU.mult), reads=[S["r_vn"], r_tab], writes=[S["r_ta"]])
    P.op("dve", TT(tbv, vo, cb, ALU.mult), reads=[S["r_vn"], r_tab], writes=[S["r_tb"]])
    P.op("dve", TT(o4[:, :, :, 1], tav, tbv, ALU.add), reads=[S["r_ta"], S["r_tb"]], writes=[r_out])


def build_qkv(nt):
    P = Prog("qkv")
    n = nt * 128
    hT = P.dram_in("hT", [D, n], BF16)
    w_in = P.dram_in("w_in", [D, 4608], F32)
    gq = P.dram_in("gq", [128, 512], F32)
    gk = P.dram_in("gk", [128, 512], F32)
    gsq = P.dram_in("gsq", [128, 512], F32)
    gsk = P.dram_in("gsk", [128, 512], F32)
    cosd = P.dram_in("cosd", [n, 32], F32); sind = P.dram_in("sind", [n, 32], F32)
    coss = P.dram_in("coss", [n, 64], F32); sins = P.dram_in("sins", [n, 64], F32)
    out = P.dram_out("qkv", [n, 4608], BF16)
    d_c = P.dsem(); d_sts = [P.dsem(), P.dsem()]
    gt = {}
    r_gain = Res()
    for nm, ap, scale in (("gq", gq, 64 ** -0.5), ("gk", gk, None), ("gsq", gsq, 128 ** -0.5), ("gsk", gsk, None)):
        t = P.sbuf([128, 512], F32)
        P.dma("sp", t[:], ap, d_c, writes=[r_gain])
        if scale is not None:
            P.op("dve", TS(t[:], t[:], scale, None, ALU.mult), reads=[r_gain], writes=[r_gain])
        gt[nm] = t
    cd = P.sbuf([128, nt, 32], F32); sd_ = P.sbuf([128, nt, 32], F32)
    cs = P.sbuf([128, nt, 64], F32); sn = P.sbuf([128, nt, 64], F32)
    r_tab = Res()
    for t_, ap in ((cd, cosd), (sd_, sind), (cs, coss), (sn, sins)):
        P.dma("sp", t_[:], ap.rearrange("(t p) i -> p t i", p=128), d_c, writes=[r_tab])
    S = dict(sq=P.sbuf([128, 512], F32), ss=P.sbuf([128, 8], F32), rs=P.sbuf([128, 8], F32), vn=P.sbuf([128, 512], F32),
             ta=P.sbuf([128, 256], F32), tb=P.sbuf([128, 256], F32),
             r_sq=Res(), r_ss=Res(), r_rs=Res(), r_vn=Res(), r_ta=Res(), r_tb=Res())
    wt = [P.sbuf([128, 16, 512], BF16) for _ in range(2)]; r_w = [Res(), Res()]; d_w = [P.dsem(), P.dsem()]
    ht = [P.sbuf([128, 16, 128], BF16) for _ in range(2)]; r_h = [Res(), Res()]; d_h = [P.dsem(), P.dsem()]
    ps = [P.psum([128, 512], F32) for _ in range(2)]; r_ps = [Res(), Res()]
    ob = [P.sbuf([128, 512], BF16) for _ in range(2)]; r_ob = [Res(), Res()]
    wst = WStage(P); r_wst = None; d_wst = None
    barrier(P)
    it = 0
    for cg in range(9):
        ws = cg % 2
        load_w_bf16(P, wt[ws], w_in[:, cg * 512:(cg + 1) * 512], 16, 512, wst, r_wst, d_wst, r_w[ws])
        for t in range(nt):
            sl = it % 2
            it += 1
            P.dma("sp", ht[sl][:], hT[:, t * 128:(t + 1) * 128].rearrange("(kc p) n -> p kc n", p=128), d_h[sl], writes=[r_h[sl]])
            for kc in range(16):
                P.op("pe", MM(ps[sl][:], ht[sl][:, kc, :], wt[ws][:, kc, :], kc == 0, kc == 15), reads=[r_h[sl], r_w[ws]], writes=[r_ps[sl]])
            o = ob[sl]
            if cg in (0, 1):
                normrope(P, ps[sl][:], 512, 64, gt["gq"][:], cd[:, t, :], sd_[:, t, :], o[:], r_ps[sl], r_gain, r_tab, r_ob[sl], S)
            elif cg in (2, 3):
                normrope(P, ps[sl][:], 512, 64, gt["gk"][:], cd[:, t, :], sd_[:, t, :], o[:], r_ps[sl], r_gain, r_tab, r_ob[sl], S)
            elif cg in (4, 5):
                P.op("act", ACTV(o[:], ps[sl][:], AF.Copy), reads=[r_ps[sl]], writes=[r_ob[sl]])
            elif cg in (6, 7):
                normrope(P, ps[sl][:], 512, 128, gt["gsq"][:], cs[:, t, :], sn[:, t, :], o[:], r_ps[sl], r_gain, r_tab, r_ob[sl], S)
            else:
                normrope(P, ps[sl][:, 0:256], 256, 128, gt["gsk"][:, 0:256], cs[:, t, :], sn[:, t, :], o[:, 0:256], r_ps[sl], r_gain, r_tab, r_ob[sl], S)
                P.op("act", ACTV(o[:, 256:512], ps[sl][:, 256:512], AF.Copy), reads=[r_ps[sl]], writes=[r_ob[sl]])
            P.dma("sp", out[t * 128:(t + 1) * 128, cg * 512:(cg + 1) * 512], o[:], d_sts[sl], reads=[r_ob[sl]])
    return P.finish()


def rope_tables(pos, dim):
    pos = np.asarray(pos)
    row = (pos // 64).astype(np.float32); col = (pos % 64).astype(np.float32)
    nf = dim // 4
    inv = (np.float32(10000.0) ** (-np.arange(nf, dtype=np.float32) / np.float32(nf))).astype(np.float32)
    ang = np.concatenate([row[:, None] * inv, col[:, None] * inv], axis=-1).astype(np.float32)
    return np.cos(ang).astype(np.float32), np.sin(ang).astype(np.float32)


def build_att():
    P = Prog("att")
    NK = 4352; NC = 34
    QT = P.dram_in("QT", [8, 128, 2048], BF16)
    KT = P.dram_in("KT", [8, 128, NK], BF16)
    VA = P.dram_in("VA", [8, 128, NC, 129], BF16)
    SQT = P.dram_in("SQT", [2, 128, 4, 2048], BF16)
    SKT = P.dram_in("SKT", [2, 128, NK], BF16)
    SVA = P.dram_in("SVA", [2, 128, NC, 129], BF16)
    lam4 = P.dram_in("lam4", [128, 4, 64], F32)
    gsub = P.dram_in("gsub", [128, 128], F32)
    sink = P.dram_in("sink", [128, 8], F32)
    masks = P.dram_in("masks", [4, 128, 128], BF16)
    A = P.dram_out("A", [2048, 2048], BF16)
    d_c = P.dsem(); d_sta = [P.dsem() for _ in range(4)]
    l4 = P.sbuf([128, 4, 64], F32); r_l4 = Res()
    P.dma("sp", l4[:], lam4, d_c, writes=[r_l4])
    gs = P.sbuf([128, 128], F32); r_gs = Res()
    P.dma("sp", gs[:], gsub, d_c, writes=[r_gs])
    sk = P.sbuf([128, 8], F32); r_sk = Res()
    P.dma("sp", sk[:], sink, d_c, writes=[r_sk])
    mk = P.sbuf([128, 4, 128], BF16); r_mk = Res()
    P.dma("sp", mk[:], masks.rearrange("m k q -> k m q"), d_c, writes=[r_mk])
    barrier(P)
    lam_init = 0.8 - 0.6 * 1.0
    prod = P.sbuf([128, 2, 64], F32); lsum = P.sbuf([128, 2], F32); nlam = P.sbuf([128, 1], F32); r_lam = Res()
    P.op("dve", TT(prod[:, 0, :], l4[:, 0, :], l4[:, 1, :], ALU.mult), reads=[r_l4], writes=[r_lam])
    P.op("dve", TT(prod[:, 1, :], l4[:, 2, :], l4[:, 3, :], ALU.mult), reads=[r_l4, r_lam], writes=[r_lam])
    P.op("dve", TRED(lsum[:], prod[:], ALU.add), reads=[r_lam], writes=[r_lam])
    P.op("act", ACTV(lsum[:], lsum[:], AF.Exp), reads=[r_lam], writes=[r_lam])
    P.op("dve", TT(nlam[:], lsum[:, 1:2], lsum[:, 0:1], ALU.subtract), reads=[r_lam], writes=[r_lam])
    P.op("dve", TS(nlam[:], nlam[:], -lam_init, None, ALU.add), reads=[r_lam], writes=[r_lam])
    P.op("dve", TS(gs[:], gs[:], 1.0 - lam_init, None, ALU.mult), reads=[r_gs], writes=[r_gs])
    P.op("act", ACTV(sk[:], sk[:], AF.Exp), reads=[r_sk], writes=[r_sk])
    kt = [P.sbuf([128, NK], BF16) for _ in range(2)]; qt = [P.sbuf([128, 4, 2048], BF16) for _ in range(2)]
    va = [P.sbuf([128, NC, 129], BF16) for _ in range(2)]
    r_in = [Res(), Res()]; d_in = [P.dsem(), P.dsem()]
    pT = [P.sbuf([128, 512], BF16) for _ in range(3)]; r_pT = [Res() for _ in range(3)]
    psS = [P.psum([128, 512], F32) for _ in range(2)]; r_S = [Res(), Res()]
    psO = [P.psum([128, 129], F32) for _ in range(4)]; r_O = [Res() for _ in range(4)]
    o1s = P.sbuf([128, 4, 129], F32); r_o1s = [Res() for _ in range(4)]
    rz = P.sbuf([128, 2], F32); o1 = P.sbuf([128, 128], F32); o2 = P.sbuf([128, 128], F32); junk = P.sbuf([128, 128], F32)
    ss = P.sbuf([128, 1], F32); rs = P.sbuf([128, 1], F32)
    r_f = Res(); r_ss = Res(); r_rs = Res()
    ab = [P.sbuf([128, 128], BF16) for _ in range(4)]; r_ab = [Res() for _ in range(4)]
    ipt = 0; iab = 0; iS = 0
    nonlocal_state = {"iS": 0, "ipt": 0}
    for h in range(8):
        sl = h % 2
        P.dma("sp", kt[sl][:], KT[h], d_in[sl], writes=[r_in[sl]])
        P.dma("sp", qt[sl][:, 0, :], QT[h], d_in[sl], writes=[r_in[sl]])
        P.dma("sp", va[sl][:], VA[h], d_in[sl], writes=[r_in[sl]])
        for qb in range(4):
            for sub in range(2):
                def qk_exp(kc):
                    nonlocal_state["iS"] += 1; nonlocal_state["ipt"] += 1
                    s_ = nonlocal_state["iS"] % 2; p_ = nonlocal_state["ipt"] % 3
                    P.op("pe", MM(psS[s_][:], kt[sl][sub * 64:(sub + 1) * 64, kc * 128:(kc + 1) * 128],
                                  qt[sl][sub * 64:(sub + 1) * 64, 0, qb * 512:(qb + 1) * 512], True, True), reads=[r_in[sl]], writes=[r_S[s_]])
                    P.op("act", ACTV(pT[p_][:], psS[s_][:], AF.Exp), reads=[r_S[s_]], writes=[r_pT[p_]])
                    return p_
                nxt = qk_exp(0)
                for kc in range(NC):
                    p_ = nxt
                    if kc + 1 < NC:
                        nxt = qk_exp(kc + 1)
                    for j in range(4):
                        P.op("pe", MM(psO[j][:], pT[p_][:, j * 128:(j + 1) * 128], va[sl][:, kc, :], kc == 0, kc == NC - 1),
                             reads=[r_pT[p_], r_in[sl]], writes=[r_O[j]])
                if sub == 0:
                    for j in range(4):
                        P.op("act", ACTV(o1s[:, j, :], psO[j][:], AF.Copy), reads=[r_O[j]], writes=[r_o1s[j]])
            for j in range(4):
                O1 = o1s[:, j, :]; O2 = psO[j][:]
                rO = [r_o1s[j], r_O[j]]
                P.op("dve", RECIP(rz[:, 0:1], O1[:, 128:129]), reads=[rO[0]], writes=[r_f])
                P.op("dve", RECIP(rz[:, 1:2], O2[:, 128:129]), reads=[rO[1], r_f], writes=[r_f])
                P.op("dve", TT(rz[:, 1:2], rz[:, 1:2], nlam[:], ALU.mult), reads=[r_f, r_lam], writes=[r_f])
                P.op("dve", TS(o2[:], O2[:, 0:128], rz[:, 1:2], None, ALU.mult), reads=[rO[1], r_f], writes=[r_f])
                P.op("dve", STT(o1[:], O1[:, 0:128], rz[:, 0:1], o2[:], ALU.mult, ALU.add), reads=[rO[0], r_f], writes=[r_f])
                P.op("act", ACTV(junk[:], o1[:], AF.Square, accum_out=ss[:]), reads=[r_f], writes=[r_ss])
                rstd_ops(P, ss[:], rs[:], 128, r_ss, r_rs)
                a_ = iab % 4; iab += 1
                P.op("dve", STT(ab[a_][:], o1[:], rs[:, 0:1], gs[:], ALU.mult, ALU.mult), reads=[r_f, r_rs, r_gs], writes=[r_ab[a_]])
                tok0 = qb * 512 + j * 128
                P.dma("sp", A[tok0:tok0 + 128, h * 128:(h + 1) * 128], ab[a_][:], d_sta[a_], reads=[r_ab[a_]])
    for g in range(2):
        sl = g % 2
        P.dma("sp", kt[sl][:], SKT[g], d_in[sl], writes=[r_in[sl]])
        P.dma("sp", qt[sl][:], SQT[g], d_in[sl], writes=[r_in[sl]])
        P.dma("sp", va[sl][:], SVA[g], d_in[sl], writes=[r_in[sl]])
        for qi in range(16):
            chunks = [(0, None), (1, None)]
            chunks.append((2 + qi - 1, 0) if qi > 0 else (18, 2))
            chunks.append((2 + qi, None))
            chunks.append((2 + qi + 1, 1) if qi < 15 else (18, 3))
            def sqk_exp(ci):
                kc, m = chunks[ci]
                nonlocal_state["iS"] += 1; nonlocal_state["ipt"] += 1
                s_ = nonlocal_state["iS"] % 2; p_ = nonlocal_state["ipt"] % 3
                P.op("pe", MM(psS[s_][:].rearrange("p (h q) -> p h q", q=128), kt[sl][:, kc * 128:(kc + 1) * 128],
                              qt[sl][:, :, qi * 128:(qi + 1) * 128], True, True), reads=[r_in[sl]], writes=[r_S[s_]])
                P.op("act", ACTV(pT[p_][:], psS[s_][:], AF.Exp), reads=[r_S[s_]], writes=[r_pT[p_]])
                if m is not None:
                    P.op("dve", TT(pT[p_][:].rearrange("p (h q) -> p h q", q=128), pT[p_][:].rearrange("p (h q) -> p h q", q=128),
                                   mk[:, m, :].unsqueeze(1).to_broadcast([128, 4, 128]), ALU.mult), reads=[r_pT[p_], r_mk], writes=[r_pT[p_]])
                return p_
            nxt = sqk_exp(0)
            for ci, (kc, m) in enumerate(chunks):
                p_ = nxt
                if ci + 1 < 5:
                    nxt = sqk_exp(ci + 1)
                for hh in range(4):
                    P.op("pe", MM(psO[hh][:], pT[p_][:, hh * 128:(hh + 1) * 128], va[sl][:, kc, :], ci == 0, ci == 4),
                         reads=[r_pT[p_], r_in[sl]], writes=[r_O[hh]])
            for hh in range(4):
                O = psO[hh][:]
                hd_ = g * 4 + hh
                P.op("dve", TT(rz[:, 0:1], O[:, 128:129], sk[:, hd_:hd_ + 1], ALU.add), reads=[r_O[hh], r_sk], writes=[r_f])
                P.op("dve", RECIP(rz[:, 0:1], rz[:, 0:1]), reads=[r_f], writes=[r_f])
                a_ = iab % 4; iab += 1
                P.op("dve", TS(ab[a_][:], O[:, 0:128], rz[:, 0:1], None, ALU.mult), reads=[r_O[hh], r_f], writes=[r_ab[a_]])
                P.dma("sp", A[qi * 128:(qi + 1) * 128, 1024 + hd_ * 128:1024 + (hd_ + 1) * 128], ab[a_][:], d_sta[a_], reads=[r_ab[a_]])
    return P.finish()


def rep128(v):
    v = np.asarray(v)
    return np.ascontiguousarray(np.broadcast_to(v[None], (128,) + v.shape))


def att_inputs(qkv, d):
    tri_prev = (np.arange(128)[:, None] >= np.arange(128)[None, :])
    tri_next = (np.arange(128)[:, None] <= np.arange(128)[None, :])
    zeros = np.zeros((128, 128), bool)
    in_maps = []
    for c in range(8):
        b, hf = c // 2, c % 2
        own = qkv[c, :2048]; oth_c = 2 * b + (1 - hf)
        other = qkv[oth_c, :2048]
        if hf == 1:
            other = np.concatenate([other[1920:], other[:1920]], 0)
        ctxr = np.concatenate([qkv[2 * b, 2048:], qkv[2 * b + 1, 2048:]], 0)
        keys = np.concatenate([ctxr, own, other], 0)
        QT = np.ascontiguousarray(own[:, 0:1024].T.reshape(8, 128, 2048))
        KT = np.ascontiguousarray(keys[:, 1024:2048].T.reshape(8, 128, 4352))
        V = keys[:, 2048:3072].reshape(34, 128, 8, 128)
        VA = np.ones((8, 128, 34, 129), NPBF16); VA[:, :, :, :128] = V.transpose(2, 1, 0, 3)
        SQT = np.ascontiguousarray(own[:, 3072:4096].reshape(2048, 2, 4, 128).transpose(1, 3, 2, 0))
        SKT = np.ascontiguousarray(keys[:, 4096:4352].T.reshape(2, 128, 4352))
        SV = keys[:, 4352:4608].reshape(34, 128, 2, 128)
        SVA = np.ones((2, 128, 34, 129), NPBF16); SVA[:, :, :, :128] = SV.transpose(2, 1, 0, 3)
        masks = np.stack([tri_prev, tri_next, tri_prev if hf == 1 else zeros, tri_next if hf == 0 else zeros]).astype(NPBF16)
        lam4 = np.stack([d["diff_lq1"][0], d["diff_lk1"][0], d["diff_lq2"][0], d["diff_lk2"][0]])
        in_maps.append({"QT": QT, "KT": KT, "VA": VA, "SQT": SQT, "SKT": SKT, "SVA": SVA, "lam4": rep128(lam4),
                        "gsub": rep128(d["diff_sub_g"][0]), "sink": rep128(d["swa_sink"][0]), "masks": masks})
    return in_maps


def build_post(nt):
    P = Prog("post")
    n = nt * 128
    AT = P.dram_in("AT", [D, n], BF16)
    xin = P.dram_in("xin", [n, D], F32)
    w_out = P.dram_in("w_out", [D, D], F32)
    tabs = {k: P.dram_in(k, [128, D], F32) for k in ("g1r", "boutr", "n2gr", "sc2r", "sh2r")}
    x1o = P.dram_out("x1", [n, D], F32)
    h2bo = P.dram_out("h2b", [n, D], BF16)
    wg = P.dram_in("wg", [D, 36], F32)
    bgr = P.dram_in("bgr", [128, 36], F32)
    iot = P.dram_in("iota8", [128, 8], F32)
    identf = P.dram_in("identf", [128, 128], F32)
    ro = P.dram_out("route", [n, 4], F32)
    d_c = P.dsem(); d_stx = [P.dsem(), P.dsem()]; d_sth = [P.dsem(), P.dsem()]; d_str = [P.dsem(), P.dsem()]
    wgt = P.sbuf([128, 16, 36], F32); bg = P.sbuf([128, 36], F32); io = P.sbuf([128, 8], F32); idf = P.sbuf([128, 128], F32); r_c = Res()
    P.dma("sp", wgt[:], wg.rearrange("(kc p) n -> p kc n", p=128), d_c, writes=[r_c])
    P.dma("sp", bg[:], bgr, d_c, writes=[r_c]); P.dma("sp", io[:], iot, d_c, writes=[r_c]); P.dma("sp", idf[:], identf, d_c, writes=[r_c])
    pst = [P.psum([128, 512], F32) for _ in range(2)]; r_pst = [Res(), Res()]
    psl = P.psum([128, 512], F32); r_psl = Res()
    h2T = P.sbuf([128, 16, 128], F32); r_h2T = Res()
    S = {k: P.sbuf([128, w], F32) for k, w in (("l", 36), ("gmax", 1), ("e4", 4), ("s4", 1), ("oh4", 4), ("sel", 8), ("m1", 1), ("mk1", 8),
                                                ("sel2", 8), ("m2", 1), ("mk2", 8), ("t8", 8), ("dd", 1), ("gi", 1), ("i4", 4))}
    r_s = Res()
    outt = [P.sbuf([128, 4], F32) for _ in range(2)]; r_out = [Res(), Res()]
    T = {}; r_T = Res()
    for k, ap in tabs.items():
        T[k] = P.sbuf([128, D], F32)
        P.dma("sp", T[k][:], ap, d_c, writes=[r_T])
    barrier(P)
    P.op("dve", STT(T["sc2r"][:], T["sc2r"][:], 1.0, T["n2gr"][:], ALU.add, ALU.mult), reads=[r_T], writes=[r_T])
    wt = P.sbuf([128, 16, D], BF16); r_w = Res()
    wstg = WStage(P)
    for cgp in range(4):
        load_w_bf16(P, wt, w_out[:, cgp * 512:(cgp + 1) * 512], 16, 512, wstg, None, None, r_w, dst_c0=cgp * 512)
    at = [P.sbuf([128, 16, 128], BF16) for _ in range(2)]; r_a = [Res(), Res()]; d_a = [P.dsem(), P.dsem()]
    xt = [P.sbuf([128, D], F32) for _ in range(2)]; r_x = [Res(), Res()]; d_xl = [P.dsem(), P.dsem()]
    ps = [P.psum([128, 512], F32) for _ in range(4)]; r_ps = [Res() for _ in range(4)]
    yt = P.sbuf([128, D], F32); r_y = Res()
    junk = P.sbuf([128, D], BF16); r_junk = Res()
    ss = P.sbuf([128, 1], F32); rs = P.sbuf([128, 1], F32); r_ss = Res(); r_rs = Res()
    h2 = [P.sbuf([128, D], F32) for _ in range(2)]; r_h2 = [Res(), Res()]
    h2b = [P.sbuf([128, D], BF16) for _ in range(2)]; r_h2b = [Res(), Res()]
    for t in range(nt):
        sl = t % 2
        rows = slice(t * 128, (t + 1) * 128)
        P.dma("sp", at[sl][:], AT[:, rows].rearrange("(kc p) n -> p kc n", p=128), d_a[sl], writes=[r_a[sl]])
        P.dma("sp", xt[sl][:], xin[rows, :], d_xl[sl], writes=[r_x[sl]])
        for cgp in range(4):
            for kc in range(16):
                P.op("pe", MM(ps[cgp][:], at[sl][:, kc, :], wt[:, kc, cgp * 512:(cgp + 1) * 512], kc == 0, kc == 15), reads=[r_a[sl], r_w], writes=[r_ps[cgp]])
            cs_ = slice(cgp * 512, (cgp + 1) * 512)
            P.op("dve", TT(yt[:, cs_], ps[cgp][:], T["boutr"][:, cs_], ALU.add), reads=[r_ps[cgp], r_T], writes=[r_y])
        P.op("dve", TT(yt[:], yt[:], T["g1r"][:], ALU.mult), reads=[r_y, r_T], writes=[r_y])
        P.op("dve", TT(xt[sl][:], xt[sl][:], yt[:], ALU.add), reads=[r_y, r_x[sl]], writes=[r_x[sl]])
        P.dma("sp", x1o[rows, :], xt[sl][:], d_stx[sl], reads=[r_x[sl]])
        P.op("act", ACTV(junk[:], xt[sl][:], AF.Square, accum_out=ss[:]), reads=[r_x[sl]], writes=[r_junk, r_ss])
        rstd_ops(P, ss[:], rs[:], D, r_ss, r_rs)
        P.op("dve", STT(h2[sl][:], xt[sl][:], rs[:, 0:1], T["sc2r"][:], ALU.mult, ALU.mult), reads=[r_x[sl], r_rs, r_T], writes=[r_h2[sl]])
        P.op("dve", TT(h2[sl][:], h2[sl][:], T["sh2r"][:], ALU.add), reads=[r_h2[sl], r_T], writes=[r_h2[sl]])
        P.op("act", ACTV(h2b[sl][:], h2[sl][:], AF.Copy), reads=[r_h2[sl]], writes=[r_h2b[sl]])
        P.dma("sp", h2bo[rows, :], h2b[sl][:], d_sth[sl], reads=[r_h2b[sl]])
        for q4 in range(4):
            b_ = q4 % 2
            for i4 in range(4):
                kc = q4 * 4 + i4
                P.op("pe", TR(pst[b_][:, i4 * 128:(i4 + 1) * 128], h2[sl][:, kc * 128:(kc + 1) * 128], idf[:]), reads=[r_h2[sl], r_c], writes=[r_pst[b_]])
            P.op("act", ACTV(h2T[:, q4 * 4:(q4 + 1) * 4, :], pst[b_][:].rearrange("p (a b) -> p a b", b=128), AF.Copy), reads=[r_pst[b_]], writes=[r_h2T])
        for kc in range(16):
            P.op("pe", MM(psl[:, 0:36], h2T[:, kc, :], wgt[:, kc, :], kc == 0, kc == 15), reads=[r_h2T, r_c], writes=[r_psl])
        def dv(fn):
            P.op("dve", fn, reads=[r_s, r_c], writes=[r_s])
        P.op("dve", TT(S["l"][:], psl[:, 0:36], bg[:], ALU.add), reads=[r_psl, r_c, r_s], writes=[r_s])
        dv(TRED(S["gmax"][:], S["l"][:, 0:4], ALU.max))
        dv(TS(S["e4"][:], S["l"][:, 0:4], S["gmax"][:, 0:1], None, ALU.subtract))
        P.op("act", ACTV(S["e4"][:], S["e4"][:], AF.Exp), reads=[r_s], writes=[r_s])
        dv(TRED(S["s4"][:], S["e4"][:], ALU.add))
        dv(RECIP(S["s4"][:], S["s4"][:]))
        dv(TS(S["oh4"][:], S["l"][:, 0:4], S["gmax"][:, 0:1], None, ALU.is_equal))
        dv(TS(S["sel"][:], S["l"][:, 4:12], S["oh4"][:, 0:1], None, ALU.mult))
        for g in range(1, 4):
            dv(STT(S["sel"][:], S["l"][:, 4 + 8 * g:12 + 8 * g], S["oh4"][:, g:g + 1], S["sel"][:], ALU.mult, ALU.add))
        dv(TT(S["i4"][:], S["oh4"][:], io[:, 0:4], ALU.mult))
        dv(TRED(S["gi"][:], S["i4"][:], ALU.add))
        dv(TRED(S["m1"][:], S["sel"][:], ALU.max))
        dv(TS(S["mk1"][:], S["sel"][:], S["m1"][:, 0:1], None, ALU.is_equal))
        dv(STT(S["sel2"][:], S["mk1"][:], -1e30, S["sel"][:], ALU.mult, ALU.add))
        dv(TRED(S["m2"][:], S["sel2"][:], ALU.max))
        dv(TS(S["mk2"][:], S["sel2"][:], S["m2"][:, 0:1], None, ALU.is_equal))
        o = outt[sl]
        P.op("dve", TT(S["t8"][:], S["mk1"][:], io[:], ALU.mult), reads=[r_s, r_c], writes=[r_s])
        P.op("dve", TRED(o[:, 0:1], S["t8"][:], ALU.add), reads=[r_s], writes=[r_out[sl]])
        P.op("dve", TT(S["t8"][:], S["mk2"][:], io[:], ALU.mult), reads=[r_s, r_c, r_out[sl]], writes=[r_s])
        P.op("dve", TRED(o[:, 1:2], S["t8"][:], ALU.add), reads=[r_s], writes=[r_out[sl]])
        for k in range(2):
            P.op("dve", STT(o[:, k:k + 1], S["gi"][:], 8.0, o[:, k:k + 1], ALU.mult, ALU.add), reads=[r_s, r_out[sl]], writes=[r_out[sl]])
        dv(TT(S["dd"][:], S["m2"][:], S["m1"][:], ALU.subtract))
        P.op("act", ACTV(S["dd"][:], S["dd"][:], AF.Exp), reads=[r_s], writes=[r_s])
        dv(TS(S["dd"][:], S["dd"][:], 1.0, None, ALU.add))
        dv(RECIP(S["dd"][:], S["dd"][:]))
        P.op("dve", TT(o[:, 2:3], S["dd"][:], S["s4"][:], ALU.mult), reads=[r_s, r_out[sl]], writes=[r_out[sl]])
        P.op("dve", TT(o[:, 3:4], S["s4"][:], o[:, 2:3], ALU.subtract), reads=[r_s, r_out[sl]], writes=[r_out[sl]])
        P.dma("sp", ro[rows, :], o[:], d_str[sl], reads=[r_out[sl]])
    return P.finish()


def build_route(nt):
    P = Prog("route")
    n = nt * 128
    h2i = P.dram_in("h2", [n, D], F32)
    wgT = P.dram_in("wgT", [36, D], F32)
    bgr = P.dram_in("bgr", [128, 36], F32)
    iot = P.dram_in("iota8", [128, 8], F32)
    ro = P.dram_out("route", [n, 4], F32)
    d_c = P.dsem(); d_st = P.dsem(); d_h = P.dsem()
    hall = P.sbuf([128, nt, D], F32); r_h = Res()
    for t in range(nt):
        P.dma("sp", hall[:, t, :], h2i[t * 128:(t + 1) * 128, :], d_h, writes=[r_h])
    bg = P.sbuf([128, 36], F32); io = P.sbuf([128, 8], F32); r_c = Res()
    P.dma("sp", bg[:], bgr, d_c, writes=[r_c]); P.dma("sp", io[:], iot, d_c, writes=[r_c])
    wc = [P.sbuf([128, D], F32) for _ in range(2)]; r_wc = [Res(), Res()]; d_wc = [P.dsem(), P.dsem()]
    lg = P.sbuf([128, nt, 36], F32); r_lg = Res()
    junk = P.sbuf([128, D], F32); r_junk = Res()
    for j in range(36):
        sl = j % 2
        P.dma("sp", wc[sl][:], wgT[j:j + 1, :].partition_broadcast(128), d_wc[sl], writes=[r_wc[sl]])
        for t in range(nt):
            P.op("dve", TT(junk[:], hall[:, t, :], wc[sl][:], ALU.mult), reads=[r_h, r_wc[sl]], writes=[r_junk])
            P.op("dve", TRED(lg[:, t, j:j + 1], junk[:], ALU.add), reads=[r_junk], writes=[r_lg])
    S = {k: P.sbuf([128, w], F32) for k, w in (("l", 36), ("gmax", 1), ("e4", 4), ("s4", 1), ("oh4", 4), ("sel", 8), ("m1", 1), ("mk1", 8),
                                                ("sel2", 8), ("m2", 1), ("mk2", 8), ("t8", 8), ("dd", 1), ("out", 4), ("gi", 1), ("i4", 4))}
    r_s = Res()
    def dv(fn):
        P.op("dve", fn, reads=[r_s, r_lg, r_c], writes=[r_s])
    outt = [P.sbuf([128, 4], F32) for _ in range(2)]; r_out = [Res(), Res()]
    for t in range(nt):
        sl = t % 2
        dv(TT(S["l"][:], lg[:, t, :], bg[:], ALU.add))
        dv(TRED(S["gmax"][:], S["l"][:, 0:4], ALU.max))
        dv(TS(S["e4"][:], S["l"][:, 0:4], S["gmax"][:, 0:1], None, ALU.subtract))
        P.op("act", ACTV(S["e4"][:], S["e4"][:], AF.Exp), reads=[r_s], writes=[r_s])
        dv(TRED(S["s4"][:], S["e4"][:], ALU.add))
        dv(RECIP(S["s4"][:], S["s4"][:]))
        dv(TS(S["oh4"][:], S["l"][:, 0:4], S["gmax"][:, 0:1], None, ALU.is_equal))
        dv(TS(S["sel"][:], S["l"][:, 4:12], S["oh4"][:, 0:1], None, ALU.mult))
        for g in range(1, 4):
            dv(STT(S["sel"][:], S["l"][:, 4 + 8 * g:12 + 8 * g], S["oh4"][:, g:g + 1], S["sel"][:], ALU.mult, ALU.add))
        dv(TT(S["i4"][:], S["oh4"][:], io[:, 0:4], ALU.mult))
        dv(TRED(S["gi"][:], S["i4"][:], ALU.add))
        dv(TRED(S["m1"][:], S["sel"][:], ALU.max))
        dv(TS(S["mk1"][:], S["sel"][:], S["m1"][:, 0:1], None, ALU.is_equal))
        dv(STT(S["sel2"][:], S["mk1"][:], -1e30, S["sel"][:], ALU.mult, ALU.add))
        dv(TRED(S["m2"][:], S["sel2"][:], ALU.max))
        dv(TS(S["mk2"][:], S["sel2"][:], S["m2"][:, 0:1], None, ALU.is_equal))
        o = outt[sl]
        P.op("dve", TT(S["t8"][:], S["mk1"][:], io[:], ALU.mult), reads=[r_s, r_c], writes=[r_s])
        P.op("dve", TRED(o[:, 0:1], S["t8"][:], ALU.add), reads=[r_s], writes=[r_out[sl]])
        P.op("dve", TT(S["t8"][:], S["mk2"][:], io[:], ALU.mult), reads=[r_s, r_c, r_out[sl]], writes=[r_s])
        P.op("dve", TRED(o[:, 1:2], S["t8"][:], ALU.add), reads=[r_s], writes=[r_out[sl]])
        for k in range(2):
            P.op("dve", STT(o[:, k:k + 1], S["gi"][:], 8.0, o[:, k:k + 1], ALU.mult, ALU.add), reads=[r_s, r_out[sl]], writes=[r_out[sl]])
        dv(TT(S["dd"][:], S["m2"][:], S["m1"][:], ALU.subtract))
        P.op("act", ACTV(S["dd"][:], S["dd"][:], AF.Exp), reads=[r_s], writes=[r_s])
        dv(TS(S["dd"][:], S["dd"][:], 1.0, None, ALU.add))
        dv(RECIP(S["dd"][:], S["dd"][:]))
        P.op("dve", TT(o[:, 2:3], S["dd"][:], S["s4"][:], ALU.mult), reads=[r_s, r_out[sl]], writes=[r_out[sl]])
        P.op("dve", TT(o[:, 3:4], S["s4"][:], o[:, 2:3], ALU.subtract), reads=[r_s, r_out[sl]], writes=[r_out[sl]])
        P.dma("sp", ro[t * 128:(t + 1) * 128, :], o[:], d_st, reads=[r_out[sl]])
    return P.finish()


def build_moe(caps):
    P = Prog("moe")
    tot = sum(caps)
    offs = [sum(caps[:j]) for j in range(4)]
    xsT = P.dram_in("xsT", [D, tot], BF16)
    wg = P.dram_in("wg", [4, D, 1024], F32)
    wu = P.dram_in("wu", [4, D, 1024], F32)
    wd = P.dram_in("wd", [4, 1024, D], F32)
    yT = P.dram_out("yT", [D, tot], BF16)
    d_sty = [P.dsem(), P.dsem()]
    wgt = P.sbuf([128, 16, 1024], BF16); wut = P.sbuf([128, 16, 1024], BF16); wdt = P.sbuf([128, 8, D], BF16)
    r_wg = Res(); r_wu = Res(); r_wd = Res()
    wstg = WStage(P)
    xt = [P.sbuf([128, 16, 512], BF16) for _ in range(2)]; r_x = [Res(), Res()]; d_x = [P.dsem(), P.dsem()]
    psg = [P.psum([128, 512], F32) for _ in range(2)]; psu = [P.psum([128, 512], F32) for _ in range(2)]; psy = [P.psum([128, 512], F32) for _ in range(2)]
    r_pg = [Res(), Res()]; r_pu = [Res(), Res()]; r_py = [Res(), Res()]
    sg = P.sbuf([128, 512], F32); r_sg = Res()
    ht = P.sbuf([128, 8, 512], BF16); r_ht = Res()
    yo = [P.sbuf([128, 16, 512], BF16) for _ in range(2)]; r_yo = [Res(), Res()]
    ib = 0
    for e_ in range(4):
        for dst, src, nk, r_ in ((wgt, wg[e_], 16, r_wg), (wut, wu[e_], 16, r_wu)):
            for c0 in range(0, 1024, 512):
                load_w_bf16(P, dst, src[:, c0:c0 + 512], nk, 512, wstg, None, None, r_, dst_c0=c0)
        for c0 in range(0, D, 512):
            load_w_bf16(P, wdt, wd[e_][:, c0:c0 + 512], 8, 512, wstg, None, None, r_wd, dst_c0=c0)
        for c0 in range(0, caps[e_], 512):
            nc_ = min(512, caps[e_] - c0)
            sl = ib % 2; ib += 1
            cols = slice(offs[e_] + c0, offs[e_] + c0 + nc_)
            P.dma("sp", xt[sl][:, :, :nc_], xsT[:, cols].rearrange("(kc p) n -> p kc n", p=128), d_x[sl], writes=[r_x[sl]])
            for fc in range(8):
                b_ = fc % 2
                for kc in range(16):
                    P.op("pe", MM(psg[b_][:, :nc_], wgt[:, kc, fc * 128:(fc + 1) * 128], xt[sl][:, kc, :nc_], kc == 0, kc == 15), reads=[r_wg, r_x[sl]], writes=[r_pg[b_]])
                for kc in range(16):
                    P.op("pe", MM(psu[b_][:, :nc_], wut[:, kc, fc * 128:(fc + 1) * 128], xt[sl][:, kc, :nc_], kc == 0, kc == 15), reads=[r_wu, r_x[sl]], writes=[r_pu[b_]])
                P.op("act", ACTV(sg[:, :nc_], psg[b_][:, :nc_], AF.Silu), reads=[r_pg[b_]], writes=[r_sg])
                P.op("dve", TT(ht[:, fc, :nc_], sg[:, :nc_], psu[b_][:, :nc_], ALU.mult), reads=[r_sg, r_pu[b_]], writes=[r_ht])
            for dc in range(16):
                b_ = dc % 2
                for fc in range(8):
                    P.op("pe", MM(psy[b_][:, :nc_], wdt[:, fc, dc * 128:(dc + 1) * 128], ht[:, fc, :nc_], fc == 0, fc == 7), reads=[r_wd, r_ht], writes=[r_py[b_]])
                P.op("act", ACTV(yo[sl][:, dc, :nc_], psy[b_][:, :nc_], AF.Copy), reads=[r_py[b_]], writes=[r_yo[sl]])
            P.dma("sp", yT[:, cols].rearrange("(kc p) n -> p kc n", p=128), yo[sl][:, :, :nc_], d_sty[sl], reads=[r_yo[sl]])
    return P.finish()


_PROGS = {}


def _prog(key, builder):
    if key not in _PROGS:
        _PROGS[key] = builder()
    return _PROGS[key]


def _tok_shard(a):
    return [a[c // 2, (c % 2) * 2048:(c % 2 + 1) * 2048] for c in range(8)]


def _moe_layer(l, route, h2b, inp, x1_sh, M, next_norm):
    eid = np.rint(route[:, 0:2]).astype(np.int64)
    h2b_all = np.concatenate(h2b, 0)
    flat_e = eid.reshape(-1)
    order = np.argsort(flat_e, kind="stable")
    counts = np.bincount(flat_e, minlength=32)
    starts = np.cumsum(counts) - counts
    slot = np.empty(flat_e.shape[0], np.int64)
    slot[order] = np.arange(flat_e.shape[0]) - starts[flat_e[order]]
    rank = np.argsort(-counts, kind="stable")
    place = np.empty((32, 2), np.int64)
    for r, e_ in enumerate(rank):
        place[e_] = (r % 8, r // 8)
    caps = tuple(int(max(128, -(-counts[rank[8 * j:8 * j + 8]].max() // 128) * 128)) for j in range(4))
    offs = [sum(caps[:j]) for j in range(4)]
    tot = sum(caps)
    exp_of = np.empty((8, 4), np.int64)
    for e_ in range(32):
        exp_of[place[e_, 0], place[e_, 1]] = e_
    in_maps = []
    for c in range(8):
        xsT = np.zeros((D, tot), NPBF16)
        for j in range(4):
            e_ = exp_of[c, j]
            a = order[starts[e_]:starts[e_] + counts[e_]]
            xsT[:, offs[j]:offs[j] + counts[e_]] = h2b_all[a // 2].T
        es = exp_of[c]
        in_maps.append({"xsT": xsT, "wg": np.ascontiguousarray(inp["moe_w_gate"][l][es]), "wu": np.ascontiguousarray(inp["moe_w_up"][l][es]),
                        "wd": np.ascontiguousarray(inp["moe_w_down"][l][es])})
    res = run(_prog(("moe", caps), lambda: build_moe(caps)), in_maps)
    yT = np.stack([r["yT"] for r in res])
    yrows = np.ascontiguousarray(yT.transpose(0, 2, 1))
    a_core = place[flat_e, 0]; a_pos = np.asarray(offs)[place[flat_e, 1]] + slot
    Y = yrows[a_core, a_pos].reshape(16384, 2, D)
    in_maps = []
    for c in range(8):
        b = c // 2
        rows = slice(c * 2048, (c + 1) * 2048)
        m = {"xin": x1_sh[c], "Y": Y[rows], "gates": np.ascontiguousarray(route[rows, 2:4]), "g2r": rep128(M[b, l, 5 * D:6 * D])}
        if next_norm:
            m.update({"ngr": rep128(inp["norm1_g"][l + 1]), "scr": rep128(M[b, l + 1, D:2 * D]), "shr": rep128(M[b, l + 1, 0:D])})
        in_maps.append(m)
    return run(_prog(("comb", next_norm), lambda: build_comb(16, True, next_norm)), in_maps)


def _post(AT_sh, x_sh, w_out, b_out, inp, M, l):
    in_maps = []
    for c in range(8):
        b = c // 2
        in_maps.append({"AT": AT_sh[c], "xin": x_sh[c], "w_out": w_out, "g1r": rep128(M[b, l, 2 * D:3 * D]), "boutr": rep128(b_out),
                        "n2gr": rep128(inp["norm2_g"][l]), "sc2r": rep128(M[b, l, 4 * D:5 * D]), "sh2r": rep128(M[b, l, 3 * D:4 * D]),
                        "wg": np.ascontiguousarray(np.concatenate([inp["moe_wg1"][l], inp["moe_wg2"][l]], 1)),
                        "bgr": rep128(np.concatenate([inp["moe_bg1"][l], inp["moe_bg2"][l]])),
                        "iota8": rep128(np.arange(8, dtype=np.float32)), "identf": np.eye(128, dtype=np.float32)})
    res = run(_prog("post", lambda: build_post(16)), in_maps)
    return [r["x1"] for r in res], np.concatenate([r["route"] for r in res], 0), [r["h2b"] for r in res]


def kernel(**inp):
    inp = {k: np.asarray(v) for k, v in inp.items()}
    x = inp["x"]; ctx = inp["ctx"]
    cc = np.concatenate([inp["c"], inp["c_ctx"][None]], 0)
    sT = np.ascontiguousarray(cc.T.reshape(16, 128, 5).transpose(1, 0, 2))
    in_maps = [{"sT": sT, "W": np.ascontiguousarray(inp["ada_w"][:, :, i * 1536:(i + 1) * 1536]),
                "B": np.ascontiguousarray(np.broadcast_to(inp["ada_b"][:, i * 1536:(i + 1) * 1536][None], (5, 2, 1536)))} for i in range(8)]
    res = run(_prog("mod", build_mod), in_maps)
    M = np.concatenate([r["M"] for r in res], axis=2)
    x_sh = _tok_shard(x)
    in_maps = []
    for c in range(8):
        b, hf = c // 2, c % 2
        in_maps.append({"xin": np.concatenate([x_sh[c], ctx[b, hf * 128:(hf + 1) * 128]], 0), "ngr": rep128(inp["norm1_g"][0]),
                        "scr": rep128(M[b, 0, D:2 * D]), "shr": rep128(M[b, 0, 0:D]), "cscr": rep128(M[4, 0, D:2 * D]), "cshr": rep128(M[4, 0, 0:D])})
    res = run(_prog("norm0", lambda: build_comb(17, False, True, ctx_tiles=1)), in_maps)
    h0 = [r["hout"] for r in res]
    in_maps = []
    for c in range(8):
        hf = c % 2
        pos = np.arange(hf * 2048, (hf + 1) * 2048)
        cd, sd = rope_tables(pos, 64); cs, sn = rope_tables(pos, 128)
        one = lambda a: np.concatenate([a, np.ones((128, a.shape[1]), np.float32)], 0)
        zero = lambda a: np.concatenate([a, np.zeros((128, a.shape[1]), np.float32)], 0)
        in_maps.append({"hT": np.ascontiguousarray(h0[c].T), "w_in": inp["attn_w_in"][0],
                        "gq": rep128(np.tile(inp["diff_q_g"][0], 8)), "gk": rep128(np.tile(inp["diff_k_g"][0], 8)),
                        "gsq": rep128(np.tile(inp["swa_q_g"][0], 4)), "gsk": rep128(np.tile(inp["swa_k_g"][0], 4)),
                        "cosd": one(cd), "sind": zero(sd), "coss": one(cs), "sins": zero(sn)})
    res = run(_prog("qkv", lambda: build_qkv(17)), in_maps)
    qkv = np.stack([r["qkv"] for r in res])
    res = run(_prog("att", build_att), att_inputs(qkv, inp))
    AT_sh = [np.ascontiguousarray(r["A"].T) for r in res]
    x1_sh, h2, h2b = _post(AT_sh, x_sh, inp["attn_w_out"][0], np.zeros(D, np.float32), inp, M, 0)
    res = _moe_layer(0, h2, h2b, inp, x1_sh, M, True)
    x2_sh = [r["xout"] for r in res]; h1 = [r["hout"] for r in res]
    if _DEBUG.get("stop") == "l0":
        return np.stack(x2_sh).reshape(4, 4096, D)
    ycvT_sh = hyena_layer(h1, inp)[0]
    x3_sh, h2, h2b = _post(ycvT_sh, x2_sh, inp["hy_w_out"][0], inp["hy_b_out"][0], inp, M, 1)
    res = _moe_layer(1, h2, h2b, inp, x3_sh, M, False)
    return np.stack([r["xout"] for r in res]).reshape(4, 4096, D).astype(np.float32)


_DEBUG = {}


def sin_reduced(P, v, tmp, ki, negpi_unused, r_v, r_tmp):
    import math
    P.op("dve", TS(v, v, 1.0 / (2 * math.pi), 16.5, ALU.mult, ALU.add), reads=[r_v], writes=[r_v])
    P.op("dve", COPY(ki, v), reads=[r_v], writes=[r_tmp])
    P.op("dve", COPY(tmp, ki), reads=[r_tmp], writes=[r_tmp])
    P.op("dve", TT(v, v, tmp, ALU.subtract), reads=[r_v, r_tmp], writes=[r_v])
    P.op("dve", TS(v, v, 2 * math.pi, -math.pi, ALU.mult, ALU.add), reads=[r_v], writes=[r_v])
    P.op("dve", TS(tmp, v, -math.pi, None, ALU.is_lt), reads=[r_v, r_tmp], writes=[r_tmp])
    P.op("dve", STT(v, tmp, 2 * math.pi, v, ALU.mult, ALU.add), reads=[r_v, r_tmp], writes=[r_v])
    P.op("dve", TS(v, v, 3.1415925, -3.1415925, ALU.min, ALU.max), reads=[r_v], writes=[r_v])
    P.op("act", ACTV(v, v, AF.Sin), reads=[r_v], writes=[r_v])


def build_hyin(nt=16):
    P = Prog("hyin")
    n = nt * 128
    NCOL = 6144
    hTp = P.dram_in("hTp", [D, n + 2], BF16)
    onesr = P.dram_in("onesr", [1, n + 2], BF16)
    W = P.dram_in("W", [D, NCOL], F32)
    b_in = P.dram_in("b_in", [1, NCOL], F32)
    cw = P.dram_in("cw", [128, 3, NCOL], F32)
    cb = P.dram_in("cb", [128, NCOL], F32)
    cmat = P.dram_in("cmat", [128, 4, 128], BF16)
    hmat = P.dram_in("hmat", [2, 2, 128], BF16)
    zo = P.dram_out("z", [n, NCOL], BF16)
    d_c = P.dsem(); d_stz = [P.dsem(), P.dsem()]; d_g = P.dsem()
    ow = P.sbuf([1, n + 2], BF16); cm = P.sbuf([128, 4, 128], BF16); hm = P.sbuf([2, 2, 128], BF16); r_c = Res()
    hall = P.sbuf([128, 16, n + 2], BF16)
    P.dma("sp", ow[:], onesr, d_c, writes=[r_c]); P.dma("sp", cm[:], cmat, d_c, writes=[r_c]); P.dma("sp", hm[:], hmat, d_c, writes=[r_c])
    for kc in range(16):
        P.dma("sp", hall[:, kc, :], hTp[kc * 128:(kc + 1) * 128, :], d_c, writes=[r_c])
    hh = P.sbuf([128, 16, 2], BF16); oh = P.sbuf([1, 2], BF16)
    barrier(P)
    P.op("dve", COPY(hh[:, :, 0:1], hall[:, :, 0:1]), reads=[r_c], writes=[r_c])
    P.op("dve", COPY(hh[:, :, 1:2], hall[:, :, n + 1:n + 2]), reads=[r_c], writes=[r_c])
    P.op("dve", COPY(oh[:, 0:1], ow[:, 0:1]), reads=[r_c], writes=[r_c])
    P.op("dve", COPY(oh[:, 1:2], ow[:, n + 1:n + 2]), reads=[r_c], writes=[r_c])
    wt = [P.sbuf([128, 16, 512], BF16) for _ in range(2)]; r_wt = [Res(), Res()]
    wst = WStage(P); r_wst = None; d_wst = None
    cwt = [P.sbuf([128, 3, 512], F32) for _ in range(2)]; cbt = [P.sbuf([128, 512], F32) for _ in range(2)]
    bint = [P.sbuf([1, 512], F32) for _ in range(2)]; brow = [P.sbuf([1, 512], BF16) for _ in range(2)]
    r_g = [Res(), Res()]; d_gs = [P.dsem(), P.dsem()]
    pu = [P.psum([128, 512], F32) for _ in range(2)]; r_pu = [Res(), Res()]
    pz = [P.psum([128, 512], F32) for _ in range(2)]; r_pz = [Res(), Res()]
    ph = P.psum([128, 512], F32); r_ph = Res()
    u0 = [P.sbuf([128, 512], BF16) for _ in range(3)]; u2 = [P.sbuf([128, 512], BF16) for _ in range(3)]; r_u = [Res() for _ in range(3)]
    zc = [P.sbuf([128, 512], F32) for _ in range(3)]; r_zc = [Res() for _ in range(3)]
    uh0 = P.sbuf([2, 512], BF16); uh2 = P.sbuf([2, 512], BF16); r_uh = Res()
    ob = [P.sbuf([128, 512], BF16) for _ in range(2)]; r_ob = [Res(), Res()]
    io = 0
    for cg in range(NCOL // 512):
        g_ = cg % 2
        cs_ = slice(cg * 512, (cg + 1) * 512)
        P.dma("sp", cwt[g_][:], cw[:, :, cs_], d_gs[g_], writes=[r_g[g_]])
        P.dma("sp", cbt[g_][:], cb[:, cs_], d_gs[g_], writes=[r_g[g_]])
        P.dma("sp", bint[g_][:], b_in[:, cs_], d_gs[g_], writes=[r_g[g_]])
        P.op("act", ACTV(brow[g_][:], bint[g_][:], AF.Copy), reads=[r_g[g_]], writes=[r_g[g_]])
        load_w_bf16(P, wt[g_], W[:, cs_], 16, 512, wst, r_wst, d_wst, r_wt[g_])

        def u_mm(out_ps, lhs_fn, one_lhs, r_out):
            for kc in range(16):
                P.op("pe", MM(out_ps, lhs_fn(kc), wt[g_][:, kc, :], kc == 0, False), reads=[r_c, r_wt[g_]], writes=[r_out])
            P.op("pe", MM(out_ps, one_lhs, brow[g_][0:1, :], False, True), reads=[r_c, r_g[g_]], writes=[r_out])
        u_mm(ph[0:2, :], lambda kc: hh[:, kc, :], oh[0:1, :], r_ph)
        P.op("dve", TT(uh0[:], ph[0:2, :], cwt[g_][0:2, 0, :], ALU.mult), reads=[r_ph, r_g[g_]], writes=[r_uh])
        P.op("dve", TT(uh2[:], ph[0:2, :], cwt[g_][0:2, 2, :], ALU.mult), reads=[r_ph, r_g[g_], r_uh], writes=[r_uh])

        def finalize(t):
            nonlocal io
            b_ = t % 2; s3 = t % 3
            P.op("pe", MM(pz[b_][:], cm[:, 0, :], u0[s3][:], True, False), reads=[r_c, r_u[s3]], writes=[r_pz[b_]])
            P.op("pe", MM(pz[b_][:], cm[:, 1, :], u2[s3][:], False, False), reads=[r_c, r_u[s3]], writes=[r_pz[b_]])
            if t > 0:
                P.op("pe", MM(pz[b_][:], cm[:, 2, :], u0[(t - 1) % 3][:], False, False), reads=[r_c, r_u[(t - 1) % 3]], writes=[r_pz[b_]])
            else:
                P.op("pe", MM(pz[b_][:], hm[:, 0, :], uh0[:], False, False), reads=[r_c, r_uh], writes=[r_pz[b_]])
            if t < nt - 1:
                P.op("pe", MM(pz[b_][:], cm[:, 3, :], u2[(t + 1) % 3][:], False, True), reads=[r_c, r_u[(t + 1) % 3]], writes=[r_pz[b_]])
            else:
                P.op("pe", MM(pz[b_][:], hm[:, 1, :], uh2[:], False, True), reads=[r_c, r_uh], writes=[r_pz[b_]])
            P.op("dve", TT(zc[s3][:], zc[s3][:], pz[b_][:], ALU.add), reads=[r_pz[b_], r_zc[s3]], writes=[r_zc[s3]])
            o_ = io % 2; io += 1
            P.op("dve", TT(ob[o_][:], zc[s3][:], cbt[g_][:], ALU.add), reads=[r_zc[s3], r_g[g_]], writes=[r_ob[o_]])
            P.dma("sp", zo[t * 128:(t + 1) * 128, cs_], ob[o_][:], d_stz[o_], reads=[r_ob[o_]])

        for t in range(nt):
            b_ = t % 2; s3 = t % 3
            u_mm(pu[b_][:], lambda kc: hall[:, kc, 1 + t * 128:1 + (t + 1) * 128], ow[0:1, 1 + t * 128:1 + (t + 1) * 128], r_pu[b_])
            P.op("dve", TT(u0[s3][:], pu[b_][:], cwt[g_][:, 0, :], ALU.mult), reads=[r_pu[b_], r_g[g_]], writes=[r_u[s3]])
            P.op("dve", TT(u2[s3][:], pu[b_][:], cwt[g_][:, 2, :], ALU.mult), reads=[r_pu[b_], r_g[g_], r_u[s3]], writes=[r_u[s3]])
            P.op("dve", TT(zc[s3][:], pu[b_][:], cwt[g_][:, 1, :], ALU.mult), reads=[r_pu[b_], r_g[g_]], writes=[r_zc[s3]])
            if t >= 1:
                finalize(t - 1)
        finalize(nt - 1)
    return P.finish()


def build_filt():
    P = Prog("filt")
    zT = P.dram_in("zT", [33, 8192], F32)
    w1 = P.dram_in("w1", [33, 64], F32); w2 = P.dram_in("w2", [64, 64], F32)
    cols = P.dram_in("cols", [64, 4], F32)
    w3s = P.dram_in("w3s", [64, 2, 2, 256], F32)
    text = P.dram_in("text", [128, 8192], F32)
    nad = P.dram_in("nad", [128, 2], F32)
    biasc = P.dram_in("biasc", [128, 2, 2], F32)
    kl = P.dram_out("kl", [2, 256, 8192], BF16)
    d_c = P.dsem(); d_stk = [P.dsem(), P.dsem()]
    zt = P.sbuf([33, 8192], F32); w1t = P.sbuf([33, 64], F32); w2t = P.sbuf([64, 64], F32); ct = P.sbuf([64, 4], F32)
    w3t = P.sbuf([64, 2, 2, 256], F32); tx = P.sbuf([128, 8192], F32); nd = P.sbuf([128, 2], F32); bc = P.sbuf([128, 2, 2], F32)
    r_c = Res()
    for t_, ap in ((zt, zT), (w1t, w1), (w2t, w2), (ct, cols), (w3t, w3s), (tx, text), (nd, nad), (bc, biasc)):
        P.dma("sp", t_[:], ap, d_c, writes=[r_c])
    barrier(P)
    ps1 = P.psum([128, 512], F32); ps2 = P.psum([128, 512], F32); r_p1 = Res(); r_p2 = Res()
    ps3 = [P.psum([128, 512], F32) for _ in range(2)]; r_p3 = [Res(), Res()]
    a1 = P.sbuf([64, 512], F32); a2 = P.sbuf([64, 512], F32); r_a1 = Res(); r_a2 = Res()
    tmp = P.sbuf([64, 512], F32); ki = P.sbuf([64, 512], I32); r_tmp = Res()
    dec = P.sbuf([128, 512], F32); r_dec = Res()
    kf = P.sbuf([128, 512], F32); r_kf = Res()
    kb = [P.sbuf([128, 512], BF16) for _ in range(2)]; r_kb = [Res(), Res()]
    i3 = 0
    for blk in range(16):
        dr = 1 if blk < 8 else 0
        cs_ = slice(blk * 512, (blk + 1) * 512)
        P.op("pe", MM(ps1[0:64, :], w1t[:], zt[:, cs_], True, True), reads=[r_c], writes=[r_p1])
        P.op("dve", TS(a1[:], ps1[0:64, :], ct[:, 0:1], ct[:, 1:2], ALU.add, ALU.mult), reads=[r_p1, r_c], writes=[r_a1])
        sin_reduced(P, a1[:], tmp[:], ki[:], None, r_a1, r_tmp)
        P.op("pe", MM(ps2[0:64, :], w2t[:], a1[:], True, True), reads=[r_c, r_a1], writes=[r_p2])
        P.op("dve", TS(a2[:], ps2[0:64, :], ct[:, 2:3], ct[:, 3:4], ALU.add, ALU.mult), reads=[r_p2, r_c], writes=[r_a2])
        sin_reduced(P, a2[:], tmp[:], ki[:], None, r_a2, r_tmp)
        for c2 in range(2):
            P.op("act", ACTV(dec[:], tx[:, cs_], AF.Exp, scale=nd[:, c2:c2 + 1]), reads=[r_c], writes=[r_dec])
            for o in range(2):
                b_ = i3 % 2; i3 += 1
                P.op("pe", MM(ps3[b_][:], w3t[:, o, dr, c2 * 128:(c2 + 1) * 128], a2[:], True, True), reads=[r_c, r_a2], writes=[r_p3[b_]])
                P.op("dve", TT(kf[:], ps3[b_][:], dec[:], ALU.mult), reads=[r_p3[b_], r_dec], writes=[r_kf])
                if blk == 8:
                    P.op("dve", TT(kf[:, 0:1], kf[:, 0:1], bc[:, o, c2:c2 + 1], ALU.add), reads=[r_kf, r_c], writes=[r_kf])
                P.op("act", ACTV(kb[b_][:], kf[:], AF.Copy), reads=[r_kf], writes=[r_kb[b_]])
                P.dma("sp", kl[o, c2 * 128:(c2 + 1) * 128, cs_], kb[b_][:], d_stk[b_], reads=[r_kb[b_]])
    return P.finish()


def build_conv():
    P = Prog("conv")
    Zv = P.dram_in("Zv", [128, 256, 128], BF16)
    Zx1 = P.dram_in("Zx1", [128, 256, 128], BF16)
    Zx2 = P.dram_in("Zx2", [128, 256, 128], BF16)
    kl1 = P.dram_in("kl1", [256, 8192], BF16)
    rl2 = P.dram_in("rl2", [256, 8192], BF16)
    yo = P.dram_out("ycv", [128, 256, 128], BF16)
    d_st = P.dsem()
    G = 64
    vt = P.sbuf([128, G, 128], BF16); x1t = P.sbuf([128, G, 128], BF16); x2t = P.sbuf([128, G, 128], BF16)
    r_in = Res(); d_in = P.dsem()
    ot = P.sbuf([128, G, 128], BF16); r_ot = Res()
    s1 = [P.sbuf([128, 8064], BF16) for _ in range(2)]; s2 = [P.sbuf([128, 8064], BF16) for _ in range(2)]
    r_s1 = [Res(), Res()]; r_s2 = [Res(), Res()]; d_s1 = [P.dsem(), P.dsem()]; d_s2 = [P.dsem(), P.dsem()]
    pa = [P.psum([128, 512], F32) for _ in range(2)]; pb = [P.psum([128, 512], F32) for _ in range(2)]
    r_pa = [Res(), Res()]; r_pb = [Res(), Res()]
    y1 = [P.sbuf([128, 128], BF16) for _ in range(2)]; r_y1 = [Res(), Res()]
    dseq = [0]
    for a in range(1, 32):
        dseq += [a, -a]
    for g in range(256 // G):
        P.dma("sp", vt[:], Zv[:, g * G:(g + 1) * G, :], d_in, writes=[r_in])
        P.dma("sp", x1t[:], Zx1[:, g * G:(g + 1) * G, :], d_in, writes=[r_in])
        P.dma("sp", x2t[:], Zx2[:, g * G:(g + 1) * G, :], d_in, writes=[r_in])
        def conv1(c):
            ch = g * G + c
            sl = ch % 2
            P.dma("sp", s1[sl][:], bass.AP(tensor=kl1.tensor, offset=ch * 8192 + 1, ap=[[1, 128], [1, 8064]]), d_s1[sl], writes=[r_s1[sl]])
            P.dma("sp", s2[sl][:], bass.AP(tensor=rl2.tensor, offset=ch * 8192, ap=[[1, 128], [1, 8064]]), d_s2[sl], writes=[r_s2[sl]])
            for n_, d_ in enumerate(dseq):
                T0 = max(0, d_); T1 = min(32, 32 + d_)
                P.op("pe", MM(pa[sl][:, 4 * T0:4 * T1], s1[sl][:, (d_ + 31) * 128:(d_ + 32) * 128], vt[:, c, 4 * (T0 - d_):4 * (T1 - d_)], n_ == 0, n_ == 62),
                     reads=[r_s1[sl], r_in], writes=[r_pa[sl]])
            P.op("dve", TT(y1[sl][:], pa[sl][:, 0:128], x1t[:, c, :], ALU.mult), reads=[r_pa[sl], r_in], writes=[r_y1[sl]])

        def conv2(c):
            ch = g * G + c
            sl = ch % 2
            for n_, d_ in enumerate(dseq):
                T0 = max(0, d_); T1 = min(32, 32 + d_)
                P.op("pe", MM(pb[sl][:, 4 * T0:4 * T1], s2[sl][:, (31 - d_) * 128:(32 - d_) * 128], y1[sl][:, 4 * (T0 - d_):4 * (T1 - d_)], n_ == 0, n_ == 62),
                     reads=[r_s2[sl], r_y1[sl]], writes=[r_pb[sl]])
            P.op("dve", TT(ot[:, c, :], pb[sl][:, 0:128], x2t[:, c, :], ALU.mult), reads=[r_pb[sl], r_in], writes=[r_ot])

        conv1(0)
        for c in range(G):
            if c + 1 < G:
                conv1(c + 1)
            conv2(c)
        P.dma("sp", yo[:, g * G:(g + 1) * G, :], ot[:], d_st, reads=[r_ot])
    return P.finish()


def _filter_consts():
    n = 4096
    idx = np.arange(8192)
    a = np.minimum(np.abs(idx - 4096), n - 1)
    t = np.linspace(0.0, 1.0, n, dtype=np.float32)
    w = (np.float32(2.0 * np.pi) * np.arange(n, dtype=np.float32) / np.float32(n)).astype(np.float32)
    f = np.linspace(1e-4, 15.0, 16, dtype=np.float32)
    wf = (w[:, None] * f[None, :]).astype(np.float32)
    z = np.concatenate([t[:, None], np.cos(wf), -np.sin(wf)], axis=-1).astype(np.float32)
    zT = np.ascontiguousarray(z[a].T)
    text = rep128(t[a])
    max_decay = np.log(1e-2) / 0.3; min_decay = np.log(1e-2) / 1.5
    deltas = np.abs(np.linspace(min_decay, max_decay, 2048, dtype=np.float32)).astype(np.float32)
    return zT, text, deltas


def hyena_layer(h1, inp):
    cw = rep128(inp["hy_conv_w"][0]); cb = rep128(inp["hy_conv_b"][0])
    cmat = np.zeros((128, 4, 128), NPBF16); hmat = np.zeros((2, 2, 128), NPBF16)
    for i in range(1, 128):
        cmat[i - 1, 0, i] = 1
    for i in range(127):
        cmat[i + 1, 1, i] = 1
    cmat[127, 2, 0] = 1
    cmat[0, 3, 127] = 1
    hmat[0, 0, 0] = 1; hmat[1, 1, 127] = 1
    in_maps = []
    for c in range(8):
        hf = c % 2
        hTp = np.zeros((D, 2050), NPBF16); ones = np.zeros((1, 2050), NPBF16)
        hTp[:, 1:2049] = h1[c].T; ones[0, 1:2049] = 1
        if hf == 1:
            hTp[:, 0] = h1[c - 1][-1]; ones[0, 0] = 1
        else:
            hTp[:, 2049] = h1[c + 1][0]; ones[0, 2049] = 1
        in_maps.append({"hTp": hTp, "onesr": ones, "W": inp["hy_w_in"][0], "b_in": inp["hy_b_in"][0][None], "cw": cw, "cb": cb,
                        "cmat": cmat, "hmat": hmat})
    res = run(_prog("hyin", build_hyin), in_maps)
    z = np.stack([r["z"] for r in res]).reshape(4, 4096, 6144)
    zT, text, deltas = _filter_consts()
    w3 = inp["flt_w3"][0].reshape(64, 2, 2, 2048)
    colsv = np.stack([inp["flt_b1"][0], inp["flt_f1"][0], inp["flt_b2"][0], inp["flt_f2"][0]], 1).astype(np.float32)
    in_maps = []
    for c in range(8):
        chs = slice(256 * c, 256 * c + 256)
        nad = np.ascontiguousarray((-deltas[chs]).reshape(2, 128).T)
        biasc = np.ascontiguousarray(inp["hy_bias"][0][:, chs].reshape(2, 2, 128).transpose(2, 0, 1))
        in_maps.append({"zT": zT, "w1": inp["flt_w1"][0], "w2": inp["flt_w2"][0], "cols": colsv, "w3s": np.ascontiguousarray(w3[:, :, :, chs]),
                        "text": text, "nad": nad, "biasc": biasc})
    res = run(_prog("filt", build_filt), in_maps)
    kls = [r["kl"] for r in res]
    def lay(a, rev):
        v = a.reshape(4, 32, 128, 256).transpose(2, 3, 1, 0).reshape(128, 256, 128)
        return np.ascontiguousarray(v[::-1] if rev else v)
    in_maps = []
    for c in range(8):
        chs = slice(256 * c, 256 * c + 256)
        in_maps.append({"Zv": lay(z[:, :, chs], True), "Zx1": lay(z[:, :, 2048 + 256 * c:2048 + 256 * c + 256], False),
                        "Zx2": lay(z[:, :, 4096 + 256 * c:4096 + 256 * c + 256], True),
                        "kl1": kls[c][0], "rl2": np.ascontiguousarray(kls[c][1][:, ::-1])})
    res = run(_prog("conv", build_conv), in_maps)
    ys = []
    for c in range(8):
        y = res[c]["ycv"][::-1].reshape(128, 256, 32, 4).transpose(3, 2, 0, 1).reshape(4, 4096, 256)
        ys.append(y)
    ycat = np.concatenate(ys, axis=2)
    return [np.ascontiguousarray(ycat[c // 2, (c % 2) * 2048:(c % 2 + 1) * 2048].T) for c in range(8)], z, kls
```
